# Optimizing a Trainium2 kernel written in Bass

```python
import jax, jax.numpy as jnp
from jax import lax
import numpy as np

D_MODEL = 1024
BATCH = 32
SEQ = 256
DEPTH = 4
DEC_BATCH = 4
DEC_SEQ = 1024
PAST_LEN = 256

GRID_W = 64
D_FF = 4 * D_MODEL
H_R = 4
DK_R = 128
DV_R = 128
W_R = H_R * DV_R
H_A = 8
KV_A = 2
HEAD_DIM = 64
W_A = H_A * HEAD_DIM
MIX_W = W_R + W_A
ROPE_AXIS_DIM = HEAD_DIM // 2
ROPE_THETA = 10000.0
CHUNK = 16
Q_BLOCK = 128
EPS = 1e-6
IN_SIZES = (H_R * DK_R, H_R * DK_R, H_R * DK_R, W_R, W_R, W_A, KV_A * HEAD_DIM, KV_A * HEAD_DIM)
IN_W = 3 * H_R * DK_R + 2 * W_R + W_A + 2 * KV_A * HEAD_DIM

kernel_name = "hymba_hgrn2_gqa_axial_dit_step"


def rmsnorm(x, g):
    xf = x.astype(jnp.float32)
    y = xf * lax.rsqrt(jnp.mean(xf * xf, axis=-1, keepdims=True) + EPS)
    return (y * g.astype(jnp.float32)).astype(x.dtype)


def axial_rope(T):
    rows = T // GRID_W
    row = jnp.repeat(jnp.arange(rows, dtype=jnp.float32), GRID_W)
    col = jnp.tile(jnp.arange(GRID_W, dtype=jnp.float32), rows)
    inv = ROPE_THETA ** (-jnp.arange(0, ROPE_AXIS_DIM, 2, dtype=jnp.float32) / ROPE_AXIS_DIM)
    ar = row[:, None] * inv[None, :]
    ac = col[:, None] * inv[None, :]
    ang = jnp.concatenate([ar, ar, ac, ac], axis=-1)
    return jnp.cos(ang), jnp.sin(ang)


def apply_rope(x, cos, sin):
    B, T, H, Dh = x.shape
    xa = x.reshape(B, T, H, 2, 2, ROPE_AXIS_DIM // 2)
    rot = jnp.stack([-xa[..., 1, :], xa[..., 0, :]], axis=-2).reshape(x.shape)
    c = cos.astype(x.dtype)[None, :, None, :]
    s = sin.astype(x.dtype)[None, :, None, :]
    return x * c + rot * s


def forget_log(fx, lb):
    lb = lb.reshape(H_R, DK_R)
    return jnp.logaddexp(jnp.log(lb), jnp.log1p(-lb) + jax.nn.log_sigmoid(fx.astype(jnp.float32)))


def gla_chunked(q, k, v, log_f, s0):
    B, T, H, DK = q.shape
    DV = v.shape[-1]
    n = T // CHUNK

    def rs(a):
        return a.reshape(B, n, CHUNK, H, a.shape[-1]).transpose(1, 0, 3, 2, 4)

    qc, kc, vc, gc = rs(q), rs(k), rs(v), rs(log_f)
    b = jnp.cumsum(gc, axis=3)
    mask = jnp.tril(jnp.ones((CHUNK, CHUNK), dtype=bool))
    diff = b[..., :, None, :] - b[..., None, :, :]
    decay = jnp.exp(jnp.where(mask[:, :, None], diff, -jnp.inf))
    a_intra = jnp.einsum('nbhtd,nbhsd,nbhtsd->nbhts', qc, kc, decay)
    o_intra = jnp.einsum('nbhts,nbhsv->nbhtv', a_intra, vc)
    q_in = qc * jnp.exp(b)
    b_last = b[..., -1:, :]
    chunk_kv = jnp.einsum('nbhsd,nbhsv->nbhdv', kc * jnp.exp(b_last - b), vc)
    chunk_decay = jnp.exp(b_last[..., 0, :])

    def step(S, inp):
        qi, dec, kv = inp
        o = jnp.einsum('bhtd,bhdv->bhtv', qi, S)
        return dec[..., None] * S + kv, o

    s_fin, o_inter = lax.scan(step, s0.astype(jnp.float32), (q_in, chunk_decay, chunk_kv))
    o = (o_intra + o_inter).transpose(1, 0, 3, 2, 4).reshape(B, T, H, DV)
    return o, s_fin


def attend_blocks(q, k, v):
    B, T, H, Dh = q.shape
    G = H // KV_A
    nb = T // Q_BLOCK
    qb = q.reshape(B, nb, Q_BLOCK, KV_A, G, Dh).transpose(1, 0, 2, 3, 4, 5)
    kf = k.astype(jnp.float32)
    scale = HEAD_DIM ** -0.5

    def one(qi):
        s = jnp.einsum('bqkgd,bskd->bkgqs', qi.astype(jnp.float32), kf) * scale
        p = jax.nn.softmax(s, axis=-1).astype(v.dtype)
        return jnp.einsum('bkgqs,bskd->bqkgd', p, v)

    o = lax.map(one, qb)
    return o.transpose(1, 0, 2, 3, 4, 5).reshape(B, T, H * Dh)


def trunk_layer(x, mod, p, lb_f, lb_b, rope, ctx_k, ctx_v, s0_f, s0_b):
    B, T, _ = x.shape
    sh1, sc1, g1, sh2, sc2, g2 = jnp.split(mod, 6, axis=-1)
    h = rmsnorm(x, p['norm1_g']) * (1 + sc1) + sh1
    proj = h @ p['w_in']
    q_r, f_f, f_b, i_r, g_r, q_a, k_a, v_a = jnp.split(proj, list(np.cumsum(IN_SIZES)[:-1]), axis=-1)

    qr = q_r.reshape(B, T, H_R, DK_R).astype(jnp.float32)
    vr = i_r.reshape(B, T, H_R, DV_R).astype(jnp.float32)
    lf_f = forget_log(f_f.reshape(B, T, H_R, DK_R), lb_f)
    lf_b = forget_log(f_b.reshape(B, T, H_R, DK_R), lb_b)
    o_f, s_f = gla_chunked(qr, -jnp.expm1(lf_f), vr, lf_f, s0_f)
    o_b, s_b = gla_chunked(qr[:, ::-1], -jnp.expm1(lf_b)[:, ::-1], vr[:, ::-1], lf_b[:, ::-1], s0_b)
    o_r = o_f + o_b[:, ::-1]
    o_r = rmsnorm(o_r, p['hgrn_norm_g']).reshape(B, T, W_R).astype(x.dtype) * jax.nn.silu(g_r)

    qa = rmsnorm(q_a.reshape(B, T, H_A, HEAD_DIM), p['q_norm_g'])
    ka = rmsnorm(k_a.reshape(B, T, KV_A, HEAD_DIM), p['k_norm_g'])
    va = v_a.reshape(B, T, KV_A, HEAD_DIM)
    if rope is None:
        k_all, v_all = ka, va
    else:
        cos, sin = rope
        qa = apply_rope(qa, cos, sin)
        ka = apply_rope(ka, cos, sin)
        k_all = jnp.concatenate([ctx_k.astype(ka.dtype), ka], axis=1)
        v_all = jnp.concatenate([ctx_v.astype(va.dtype), va], axis=1)
    o_a = attend_blocks(qa, k_all, v_all)

    x = x + g1 * (jnp.concatenate([o_r, o_a], axis=-1) @ p['w_out'])
    h2 = rmsnorm(x, p['norm2_g']) * (1 + sc2) + sh2
    x = x + g2 * (jnp.square(jax.nn.relu(h2 @ p['w1'])) @ p['w2'])
    return x, ka, va, s_f, s_b


def setup_inputs(seed: int = 0) -> dict:
    key = jax.random.key(seed)
    ks = jax.random.split(key, 20)
    f32 = jnp.float32
    nrm = lambda k, shape, s: jax.random.normal(k, shape, f32) * s
    return {
        'x_prompt': nrm(ks[0], (BATCH, SEQ, D_MODEL), 1.0),
        'x_sample': nrm(ks[1], (DEC_BATCH, DEC_SEQ, D_MODEL), 1.0),
        'cache_k': nrm(ks[2], (DEC_BATCH, DEPTH, PAST_LEN, KV_A, HEAD_DIM), 1.0),
        'cache_v': nrm(ks[3], (DEC_BATCH, DEPTH, PAST_LEN, KV_A, HEAD_DIM), 1.0),
        'state_hgrn': nrm(ks[4], (DEC_BATCH, DEPTH, 2, H_R, DK_R, DV_R), 1.0),
        'c': nrm(ks[5], (DEC_BATCH, D_MODEL), 1.0),
        'c_ctx': nrm(ks[6], (D_MODEL,), 1.0),
        'w_mod': nrm(ks[7], (DEPTH, D_MODEL, 6 * D_MODEL), 0.5 * D_MODEL ** -0.5),
        'b_mod': nrm(ks[8], (DEPTH, 6 * D_MODEL), 0.02),
        'norm1_g': 1.0 + nrm(ks[9], (DEPTH, D_MODEL), 0.02),
        'w_in': nrm(ks[10], (DEPTH, D_MODEL, IN_W), D_MODEL ** -0.5),
        'lb_raw': nrm(ks[11], (DEPTH, 2, H_R * DK_R), 1.0),
        'hgrn_norm_g': 1.0 + nrm(ks[12], (DEPTH, DV_R), 0.02),
        'q_norm_g': 1.0 + nrm(ks[13], (DEPTH, HEAD_DIM), 0.02),
        'k_norm_g': 1.0 + nrm(ks[14], (DEPTH, HEAD_DIM), 0.02),
        'w_out': nrm(ks[15], (DEPTH, MIX_W, D_MODEL), MIX_W ** -0.5),
        'norm2_g': 1.0 + nrm(ks[16], (DEPTH, D_MODEL), 0.02),
        'w1': nrm(ks[17], (DEPTH, D_MODEL, D_FF), D_MODEL ** -0.5),
        'w2': nrm(ks[18], (DEPTH, D_FF, D_MODEL), D_FF ** -0.5),
        'final_norm_g': 1.0 + nrm(ks[19], (D_MODEL,), 0.02),
    }


def reference(x_prompt, x_sample, cache_k, cache_v, state_hgrn, c, c_ctx, w_mod, b_mod, norm1_g, w_in, lb_raw,
              hgrn_norm_g, q_norm_g, k_norm_g, w_out, norm2_g, w1, w2, final_norm_g):
    lb_all = jnp.cumsum(jax.nn.softmax(lb_raw.astype(jnp.float32), axis=0), axis=0)
    lb_all = lb_all - lb_all[0:1]

    def params(l):
        return {'norm1_g': norm1_g[l], 'w_in': w_in[l], 'hgrn_norm_g': hgrn_norm_g[l], 'q_norm_g': q_norm_g[l],
                'k_norm_g': k_norm_g[l], 'w_out': w_out[l], 'norm2_g': norm2_g[l], 'w1': w1[l], 'w2': w2[l]}

    xp = x_prompt
    Bp = x_prompt.shape[0]
    zero_s = jnp.zeros((Bp, H_R, DK_R, DV_R), jnp.float32)
    ks_list, vs_list, ss_list = [], [], []
    for l in range(DEPTH):
        mod_ctx = (jax.nn.silu(c_ctx) @ w_mod[l] + b_mod[l])[None, None, :]
        xp, k_l, v_l, sf, sb = trunk_layer(xp, mod_ctx, params(l), lb_all[l, 0], lb_all[l, 1], None, None, None,
                                           zero_s, zero_s)
        ks_list.append(k_l)
        vs_list.append(v_l)
        ss_list.append(jnp.stack([sf, sb], axis=1))
    y_prompt = rmsnorm(xp, final_norm_g)
    new_k = jnp.stack(ks_list, axis=1)
    new_v = jnp.stack(vs_list, axis=1)
    new_s = jnp.stack(ss_list, axis=1)

    rope = axial_rope(x_sample.shape[1])
    xs = x_sample
    for l in range(DEPTH):
        mod_lat = (jax.nn.silu(c) @ w_mod[l] + b_mod[l])[:, None, :]
        xs, _, _, _, _ = trunk_layer(xs, mod_lat, params(l), lb_all[l, 0], lb_all[l, 1], rope,
                                     cache_k[:, l], cache_v[:, l], state_hgrn[:, l, 0], state_hgrn[:, l, 1])
    y_sample = rmsnorm(xs, final_norm_g)
    return (y_prompt, y_sample, new_k, new_v, new_s)
```

```python
import contextlib
import numpy as np
import concourse.bass as bass
import concourse.mybir as mybir
from concourse.bass_utils import run_bass_kernel_spmd

F32 = mybir.dt.float32
BF16 = mybir.dt.bfloat16
AF = mybir.ActivationFunctionType
ALU = mybir.AluOpType
AX = mybir.AxisListType

NCORES = 8
DM = 1024
DEPTH = 4
NSEQP = 4
SEQ = 256
DEC_SEQ = 1024
PAST = 256
NT = 2048
INW = 3328
DFF = 4096
EPS = 1e-6
NL = DEPTH


import types


def _freeze(fn):
    if fn is None or fn.__closure__ is None:
        return fn
    cells = []
    for c in fn.__closure__:
        try:
            cells.append(types.CellType(c.cell_contents))
        except ValueError:
            cells.append(c)
    return types.FunctionType(fn.__code__, fn.__globals__, fn.__name__, fn.__defaults__, tuple(cells))


_DBG = {}


class Sem:
    def __init__(self, h):
        self.h = h


class TT:
    def __init__(self, name, ap, dsem=None):
        self.name = name
        self.ap = ap
        self.w = None
        self.r = {}
        self.dsem = dsem
        self.dcnt = 0

    def __getitem__(self, k):
        return self.ap[k]


class Prog:
    ENGS = ("pe", "act", "dve", "pool", "sp")

    def __init__(self, nc, es):
        self.nc = nc
        self.es = es
        self.streams = {k: [] for k in self.ENGS}
        self.sem = {k: Sem(es.enter_context(nc.semaphore("s_" + k))) for k in self.ENGS}
        self.cnt = {k: 0 for k in self.ENGS}
        self.seen = {k: {} for k in self.ENGS}
        self.out_tiles = []
        self.off = 16512
        self.n_alloc = 0

    def alloc(self, shape, dtype, at=None):
        nbytes = int(np.prod(shape[1:])) * (4 if dtype == F32 else 2)
        nbytes = (nbytes + 31) // 32 * 32
        if at is None:
            at = self.off
            self.off += nbytes
            assert self.off <= 229376, f"SBUF overflow {self.off}"
        self.n_alloc += 1
        h = self.nc.alloc_sbuf_tensor_at(f"t{self.n_alloc}", list(shape), dtype, offset=at)
        self.last_name = f"t{self.n_alloc}"
        return h, at, nbytes

    def tile(self, name, shape, dtype, dma=False, at=None):
        h, a, nb = self.alloc(shape, dtype, at)
        t = TT(name, h[tuple(slice(None) for _ in shape)])
        _DBG[name] = self.last_name
        t.addr = a
        t.nbytes = nb
        if dma:
            t.dsem = Sem(self.es.enter_context(self.nc.semaphore("d_" + name)))
        return t

    def dsem(self, name):
        return Sem(self.es.enter_context(self.nc.semaphore("d_" + name)))

    def _deps(self, eng, rd, wr):
        deps = {}

        def add(sv):
            s, v = sv
            if v > deps.get(s, 0):
                deps[s] = v

        for t in rd:
            if t.w:
                add(t.w)
        for t in wr:
            if t.w:
                add(t.w)
            for s, v in t.r.items():
                add((s, v))
        out = []
        seen = self.seen[eng]
        for s, v in deps.items():
            if eng == "pe" and s is self.sem["pe"]:
                continue
            if seen.get(s, 0) >= v:
                continue
            seen[s] = v
            out.append((s, v))
        return out

    def op(self, eng, fn, rd=(), wr=(), inc=True):
        if INCALL:
            inc = True
        fn = _freeze(fn)
        waits = self._deps(eng, rd, wr)
        s = self.sem[eng]
        v = self.cnt[eng] + 1
        if inc:
            self.cnt[eng] = v
        self.streams[eng].append((waits, fn, (s, 1) if inc else None))
        for t in wr:
            t.w = (s, v)
            t.r = {}
        for t in rd:
            if t in wr:
                continue
            if t.r.get(s, 0) < v:
                t.r[s] = v

    def dma(self, q, pairs, tile, load, rd=(), wr=()):
        if load:
            waits = self._deps(q, rd, [tile] + list(wr))
        else:
            waits = self._deps(q, [tile] + list(rd), wr)
        s = tile.dsem
        first = True
        for (o, i) in pairs:
            tile.dcnt += 1
            self.streams[q].append((waits if first else [], (lambda e, o=o, i=i: e.dma_start(out=o, in_=i)), (s, 16)))
            first = False
        v = 16 * tile.dcnt
        if load:
            tile.w = (s, v)
            tile.r = {}
        else:
            if tile.r.get(s, 0) < v:
                tile.r[s] = v
            if tile not in self.out_tiles:
                self.out_tiles.append(tile)

    def barrier(self, tiles=()):
        waits = [(self.sem[k], self.cnt[k]) for k in ("pe", "act", "dve", "pool") if self.cnt[k] > 0]
        for t in tiles:
            if t.dsem is not None and t.dcnt > 0:
                waits.append((t.dsem, 16 * t.dcnt))
        for k in ("pe", "act", "dve", "sp"):
            seen = self.seen[k]
            w = []
            for s_, v in waits:
                if s_ is self.sem.get(k):
                    continue
                if seen.get(s_, 0) >= v:
                    continue
                seen[s_] = v
                w.append((s_, v))
            if w:
                self.streams[k].append((w, None, None))

    def finish(self):
        waits = []
        for t in self.out_tiles:
            waits.append((t.dsem, 16 * t.dcnt))
        self.streams["sp"].append((waits, None, None))

    def simulate(self):
        vals = {}
        pc = {k: 0 for k in self.ENGS}
        names = {id(v): k for k, v in self.sem.items()}
        progress = True
        while progress:
            progress = False
            for k in self.ENGS:
                st = self.streams[k]
                while pc[k] < len(st):
                    waits, fn, inc = st[pc[k]]
                    if any(vals.get(id(s_), 0) < v for s_, v in waits):
                        break
                    if inc is not None:
                        vals[id(inc[0])] = vals.get(id(inc[0]), 0) + inc[1]
                    pc[k] += 1
                    progress = True
        stuck = {k: (pc[k], len(self.streams[k])) for k in self.ENGS if pc[k] < len(self.streams[k])}
        if stuck:
            msg = []
            for k, (p, n) in stuck.items():
                waits, fn, inc = self.streams[k][p]
                msg.append(f"{k} stuck at {p}/{n}: " + ", ".join(
                    f"{names.get(id(s_), 'dma')}>={v} (now {vals.get(id(s_), 0)})" for s_, v in waits if vals.get(id(s_), 0) < v))
            raise RuntimeError("DEADLOCK: " + " | ".join(msg))
        return {k: len(self.streams[k]) for k in self.ENGS}

    def replay(self, block):
        def run(name):
            def f(e):
                for waits, fn, inc in self.streams[name]:
                    for s, v in waits:
                        e.wait_ge(s.h, v)
                    if fn is None:
                        continue
                    ins = fn(e)
                    if inc is not None:
                        ins.then_inc(inc[0].h, inc[1])
            return f

        block.tensor(run("pe"))
        block.vector(run("dve"))
        block.scalar(run("act"))
        block.gpsimd(run("pool"))
        block.sync(run("sp"))


class _Stop(Exception):
    pass


import os
STAGE = float(os.environ.get("KSTAGE", "99"))
INCALL = os.environ.get("KINCALL", "0") == "1"


def ck(n):
    if STAGE <= n:
        raise _Stop()


def _body(nc, es, P):
    def din(name, shape):
        return nc.dram_tensor(name, list(shape), F32, kind="ExternalInput").ap()

    def dout(name, shape):
        return nc.dram_tensor(name, list(shape), F32, kind="ExternalOutput").ap()

    xin = din("xin", [NT, DM])
    cvec = din("cvec", [128, 16])
    w_mod = din("w_mod", [NL, DM, 6 * DM])
    bmod = din("bmod", [128, NL * 48])
    n1g = din("n1g", [128, NL * 8])
    n2g = din("n2g", [128, NL * 8])
    fng = din("fng", [128, 8])
    w_in = din("w_in", [NL, DM, INW])
    w_out = din("w_out", [NL, DM, DM])
    w1 = din("w1", [NL, DM, DFF])
    w2 = din("w2", [NL, DFF, DM])
    lbr = din("lbr", [128, 8 * NL])
    hgn = din("hgn", [128, NL])
    gqk = din("gqk", [NL, 128, 640])
    cosr = din("cosr", [8, 128, 640])
    sinr = din("sinr", [8, 128, 640])
    cachek = din("cachek", [NL, PAST, 128])
    cachev = din("cachev", [NL, PAST, 128])
    state0 = din("state0", [NL, 2, 4, 128, 128])
    identd = din("identd", [128, 128])
    maskf = din("maskf", [128, 128])
    maskb = din("maskb", [128, 128])
    rstd_ = din("rstd", [128, 1024])
    mh0d = din("mh0d", [128, 512])
    mh1d = din("mh1d", [128, 512])

    yout = dout("yout", [NT, DM])
    newk = dout("newk", [NSEQP, NL, SEQ, 128])
    newv = dout("newv", [NSEQP, NL, SEQ, 128])
    news = dout("news", [NSEQP, NL, 2, 4, 128, 128])

    x_all = P.tile("x_all", [128, 8, NT], F32)
    XG = [TT(f"x{g}", x_all[:, :, g * 512:(g + 1) * 512]) for g in range(4)]
    hbuf = P.tile("hbuf", [128, 2, 8, 1024], BF16)
    HB = [[TT(f"hb{h}{g}", hbuf[:, h, :, g * 512:(g + 1) * 512]) for g in range(2)] for h in range(2)]
    WS = [P.tile(f"ws{i}", [128, 8192], BF16, dma=True) for i in range(3)]

    CONST = TT("const", None, dsem=P.dsem("const"))

    def ctile(shape, dtype):
        h, _, _ = P.alloc(shape, dtype)
        return h[tuple(slice(None) for _ in shape)]

    idf = ctile([128, 128], F32)
    mf = ctile([128, 128], F32)
    mb = ctile([128, 128], F32)
    rst = ctile([128, 512], F32)
    c_sb = ctile([128, 16], F32)
    bmod_sb = ctile([128, NL, 48], F32)
    n1g_sb = ctile([128, NL, 8], F32)
    n2g_sb = ctile([128, NL, 8], F32)
    fng_sb = ctile([128, 8], F32)
    lbr_sb = ctile([128, 8, NL], F32)
    hgn_sb = ctile([128, NL], F32)
    CONSTB = TT("constb", None, dsem=P.dsem("constb"))
    MH0 = ctile([128, 512], BF16)
    MH1 = ctile([128, 512], BF16)
    SMALL = TT("small", None)
    idb = ctile([128, 128], BF16)
    ones_bf = ctile([128, 128], BF16)
    zeros_bf = ctile([128, 128], BF16)
    ones_f = ctile([128, 128], F32)
    scT = ctile([128, 8, 2], BF16)
    lb_sb = ctile([128, 8, NL], F32)
    oml_sb = ctile([128, 8, NL], F32)
    noml_sb = ctile([128, 8, NL], F32)
    lbe = ctile([128, 8, NL], F32)
    lbs = ctile([128, 8], F32)
    MODS = P.tile("mods", [128, NL, 48, 2], F32)
    AB = P.tile("ab", [128, NL, 2, 8, 2], F32)
    RSTD = P.tile("rstdt", [128, 512], F32)
    GQKL = P.tile("gqkl", [128, 640], F32, dma=True)
    TMPF = [P.tile(f"tmpf{i}", [128, 512], F32) for i in range(2)]
    SQ = [P.tile(f"sq{i}", [128, 512], BF16) for i in range(2)]

    WORK0 = P.off
    WORK_END = 229376

    PSF = []
    for i in range(8):
        h = es.enter_context(nc.psum_tensor(f"psf{i}", [128, 512], F32))
        PSF.append(TT(f"psf{i}", h[:, :]))
    PSB = []
    psf_i = [0]
    psb_i = [0]

    def psf():
        t = PSF[psf_i[0] % 6]
        psf_i[0] += 1
        return t

    def psacc():
        t = PSF[6 + psb_i[0] % 2]
        psb_i[0] += 1
        return t

    def psb():
        return psf()

    rot = {}

    def rr(lst, key):
        i = rot.get(key, 0)
        rot[key] = i + 1
        return lst[i % len(lst)]

    op = P.op

    ws_i = [0]

    def wslot():
        t = WS[ws_i[0] % 3]
        ws_i[0] += 1
        return t

    def w_pairs(key, slot):
        kind = key[0]
        if kind == "mod":
            _, l, piece = key
            v = slot.ap.rearrange("p (a b) -> p a b", b=1024)
            return [(v[:, kc, :], w_mod[l, kc * 128:(kc + 1) * 128, piece * 1024:(piece + 1) * 1024]) for kc in range(8)]
        if kind == "A":
            _, l, hf = key
            v = slot.ap[:, 0:8 * 768].rearrange("p (a b) -> p a b", b=768)
            return [(v[:, kc, :], w_in[l, kc * 128:(kc + 1) * 128, 2560:3328]) for kc in range(8)]
        if kind == "R":
            _, l, hf, hd = key
            v = slot.ap[:, 0:8 * 640].rearrange("p (a b) -> p a b", b=640)
            return [(v[:, kc, :], w_in[l, kc * 128:(kc + 1) * 128, hd * 640:(hd + 1) * 640]) for kc in range(8)]
        if kind == "O":
            _, l, hf = key
            v = slot.ap.rearrange("p (a b) -> p a b", b=1024)
            return [(v[:, kc, :], w_out[l, kc * 128:(kc + 1) * 128, :]) for kc in range(8)]
        if kind == "F":
            _, l, c = key
            v1 = slot.ap[:, 0:4096].rearrange("p (a b) -> p a b", b=512)
            v2 = slot.ap[:, 4096:8192].rearrange("p (a b) -> p a b", b=1024)
            pr = [(v1[:, kc, :], w1[l, kc * 128:(kc + 1) * 128, c * 512:(c + 1) * 512]) for kc in range(8)]
            pr += [(v2[:, f, :], w2[l, c * 512 + f * 128: c * 512 + (f + 1) * 128, :]) for f in range(4)]
            return pr
        raise ValueError(key)

    WPLAN = [("mod", 0, piece) for piece in range(6)]
    for l_ in range(NL):
        for hf_ in range(2):
            WPLAN.append(("A", l_, hf_))
            WPLAN += [("R", l_, hf_, hd_) for hd_ in range(4)]
            WPLAN.append(("O", l_, hf_))
        if l_ + 1 < NL:
            WPLAN += [("mod", l_ + 1, piece) for piece in range(6)]
        WPLAN += [("F", l_, c_) for c_ in range(8)]
    wstate = {"wp": 0, "issued": 0, "slots": {}}

    def acquire(key):
        wp = wstate["wp"]
        assert WPLAN[wp] == key, (WPLAN[wp], key)
        while wstate["issued"] <= min(wp + 1, len(WPLAN) - 1):
            k = wstate["issued"]
            slot = WS[k % 3]
            P.dma("pool", w_pairs(WPLAN[k], slot), slot, load=True)
            wstate["slots"][k] = slot
            wstate["issued"] = k + 1
        wstate["wp"] = wp + 1
        return wstate["slots"].pop(wp)

    P.dma("sp", [
        (idf, identd), (mf, maskf), (mb, maskb),
        (rst, rstd_[:, 0:512]),
        (c_sb, cvec), (bmod_sb, bmod.rearrange("p (a b) -> p a b", b=48)),
        (n1g_sb, n1g.rearrange("p (a b) -> p a b", b=8)), (n2g_sb, n2g.rearrange("p (a b) -> p a b", b=8)),
        (fng_sb, fng), (lbr_sb, lbr.rearrange("p (a b) -> p a b", b=NL)), (hgn_sb, hgn),
    ], CONST, load=True)

    P.dma("pool", [(MH0, mh0d), (MH1, mh1d)], CONSTB, load=True)
    op("dve", lambda e: e.memset(ones_bf, 1.0), wr=[SMALL])
    op("dve", lambda e: e.memset(zeros_bf, 0.0), wr=[SMALL])
    op("dve", lambda e: e.memset(ones_f, 1.0), wr=[SMALL])
    op("dve", lambda e: e.tensor_copy(out=idb, in_=idf), rd=[CONST], wr=[SMALL])
    op("act", lambda e: e.activation(out=scT.rearrange("p a b -> p (a b)"), in_=c_sb, func=AF.Silu), rd=[CONST], wr=[SMALL])
    op("act", lambda e: e.activation(out=lbe, in_=lbr_sb, func=AF.Exp), rd=[CONST], wr=[SMALL])
    op("dve", lambda e: e.tensor_reduce(out=lbs, in_=lbe, axis=AX.X, op=ALU.add), rd=[SMALL], wr=[SMALL])
    op("dve", lambda e: e.reciprocal(out=lbs, in_=lbs), rd=[SMALL], wr=[SMALL])
    op("dve", lambda e: e.tensor_tensor(out=lbe, in0=lbe, in1=lbs.unsqueeze(2).to_broadcast([128, 8, NL]), op=ALU.mult), rd=[SMALL], wr=[SMALL])
    op("dve", lambda e: e.memset(lb_sb[:, :, 0:1], 0.0), wr=[SMALL])
    op("dve", lambda e: e.tensor_copy(out=lb_sb[:, :, 1:2], in_=lbe[:, :, 1:2]), rd=[SMALL], wr=[SMALL])
    op("dve", lambda e: e.tensor_tensor(out=lb_sb[:, :, 2:3], in0=lb_sb[:, :, 1:2], in1=lbe[:, :, 2:3], op=ALU.add), rd=[SMALL], wr=[SMALL])
    op("dve", lambda e: e.tensor_tensor(out=lb_sb[:, :, 3:4], in0=lb_sb[:, :, 2:3], in1=lbe[:, :, 3:4], op=ALU.add), rd=[SMALL], wr=[SMALL])
    op("dve", lambda e: e.tensor_scalar(out=oml_sb, in0=lb_sb, scalar1=-1.0, scalar2=1.0, op0=ALU.mult, op1=ALU.add), rd=[SMALL], wr=[SMALL])
    op("dve", lambda e: e.tensor_scalar(out=noml_sb, in0=oml_sb, scalar1=-1.0, scalar2=None, op0=ALU.mult), rd=[SMALL], wr=[SMALL])

    ck(1)
    def emit_mods(l):
        ps = psf()
        for piece in range(6):
            slot = acquire(("mod", l, piece))
            wv = slot.ap.rearrange("p (a b) -> p a b", b=1024)
            for j in range(8):
                jj = piece * 8 + j
                for kc in range(8):
                    last = (kc == 7 and j == 7)
                    op("pe", lambda e, ps=ps, wv=wv, j=j, jj=jj, kc=kc: e.matmul(
                        ps[:, jj * 2:jj * 2 + 2], lhsT=wv[:, kc, j * 128:(j + 1) * 128], rhs=scT[:, kc, :],
                        start=(kc == 0), stop=(kc == 7)),
                       rd=[slot, SMALL], wr=[ps], inc=last)
        op("dve", lambda e, ps=ps, l=l: e.tensor_tensor(
            out=MODS[:, l, :, :], in0=ps[:, 0:96].rearrange("p (a b) -> p a b", b=2),
            in1=bmod_sb[:, l, :].unsqueeze(2).to_broadcast([128, 48, 2]), op=ALU.add), rd=[ps, CONST], wr=[MODS])
        for which, (jo, gsb) in enumerate(((8, n1g_sb), (32, n2g_sb))):
            op("dve", lambda e, l=l, which=which, jo=jo, gsb=gsb: e.scalar_tensor_tensor(
                out=AB[:, l, which, :, :], in0=MODS[:, l, jo:jo + 8, :], scalar=1.0,
                in1=gsb[:, l, :].unsqueeze(2).to_broadcast([128, 8, 2]), op0=ALU.add, op1=ALU.mult),
               rd=[MODS, CONST], wr=[AB])

    emit_mods(0)

    def modv(l, which, kc, i):
        return MODS[:, l, which * 8 + kc, i:i + 1]

    ck(2)
    XST = [P.tile(f"xst{i}", [128, 1024], F32, dma=True, at=WORK0 + i * 4096) for i in range(2)]
    for tt in range(16):
        st = XST[tt % 2]
        P.dma("sp", [(st.ap, xin[tt * 128:(tt + 1) * 128, :])], st, load=True)
        g = tt // 4
        for hh in range(2):
            ps = psf()
            for k4 in range(4):
                kc = hh * 4 + k4
                op("pe", lambda e, ps=ps, st=st, kc=kc, k4=k4: e.transpose(
                    out=ps[:, k4 * 128:(k4 + 1) * 128], in_=st[:, kc * 128:(kc + 1) * 128], identity=idf),
                   rd=[st, CONST], wr=[ps], inc=(k4 == 3))
            eng = "dve" if hh == 0 else "act"
            dst = x_all[:, hh * 4:(hh + 1) * 4, tt * 128:(tt + 1) * 128]
            if eng == "dve":
                op("dve", lambda e, ps=ps, dst=dst: e.tensor_copy(out=dst, in_=ps[:, :].rearrange("p (a b) -> p a b", b=128)),
                   rd=[ps], wr=[XG[g]])
            else:
                op("act", lambda e, ps=ps, dst=dst: e.activation(out=dst, in_=ps[:, :].rearrange("p (a b) -> p a b", b=128), func=AF.Copy),
                   rd=[ps], wr=[XG[g]])

    ck(3)
    def rmsnorm_to(g, l, which, dstT):
        i = g // 2
        xg = XG[g]
        ps = psf()
        for kc in range(8):
            sq = rr(SQ, "sq")
            op("act", lambda e, sq=sq, kc=kc: e.activation(out=sq.ap, in_=xg[:, kc, :], func=AF.Square), rd=[xg], wr=[sq])
            op("pe", lambda e, sq=sq, kc=kc, ps=ps: e.matmul(ps[:, :], lhsT=ones_bf, rhs=sq.ap, start=(kc == 0), stop=(kc == 7)),
               rd=[sq, SMALL], wr=[ps])
        op("act", lambda e, ps=ps: e.activation(out=RSTD.ap, in_=ps[:, :], func=AF.Ln, scale=1.0 / DM, bias=EPS), rd=[ps], wr=[RSTD])
        op("act", lambda e: e.activation(out=RSTD.ap, in_=RSTD.ap, func=AF.Exp, scale=-0.5), rd=[], wr=[RSTD])
        sh = 0 if which == 0 else 3
        for kc in range(8):
            tmp = rr(TMPF, "tmpf")
            op("dve" if kc % 2 == 0 else "pool", lambda e, tmp=tmp, kc=kc: e.tensor_tensor(out=tmp.ap, in0=xg[:, kc, :], in1=RSTD.ap, op=ALU.mult),
               rd=[xg, RSTD], wr=[tmp])
            op("act", lambda e, tmp=tmp, kc=kc: e.activation(
                out=dstT[:, kc, :], in_=tmp.ap, func=AF.Identity, scale=AB[:, l, which, kc, i:i + 1], bias=modv(l, sh, kc, i)),
               rd=[tmp, AB, MODS], wr=[dstT])

    W = WORK0
    a = [W]

    def wt(name, shape, dtype, dma=False):
        t = P.tile(name, shape, dtype, dma=dma, at=a[0])
        a[0] += t.nbytes
        assert a[0] <= WORK_END, f"work overflow {name} {a[0]}"
        return t

    QKF = [wt(f"qkf{i}", [128, 640], F32, dma=True) for i in range(2)]
    SQFS = [wt(f"sqf{i}", [128, 640], BF16) for i in range(2)]
    RT1 = wt("rt1", [128, 640], F32, dma=True)
    RT2 = wt("rt2", [128, 640], F32, dma=True)
    VF = [wt(f"vf{i}", [128, 128], F32, dma=True) for i in range(2)]
    QB = [wt(f"qb{i}", [128, 640], BF16) for i in range(2)]
    QT = wt("qT", [128, 8, 512], BF16)
    KT = wt("kT", [128, 1280], BF16)
    VA0 = wt("va0", [128, 10, 65], BF16)
    VA1 = wt("va1", [128, 10, 128], BF16)
    PT = [wt(f"pt{i}", [128, 512], BF16) for i in range(3)]
    RECS = [wt(f"rec{i}", [128, 512], F32) for i in range(2)]
    BC = wt("bc", [128, 512], F32)
    CK = wt("ck", [128, 2, 128], F32, dma=True)
    CV = wt("cv", [128, 2, 128], F32, dma=True)
    CKB = wt("ckb", [128, 2, 128], BF16)
    SSQS = [wt(f"ssq{i}", [128, 10], F32) for i in range(2)]
    att_end = a[0]
    a[0] = W
    Q32 = wt("q32", [128, 512], BF16)
    TS = [[wt(f"t{k}_{d}", [128, 512], F32) for k in range(4)] for d in range(2)]
    QM = wt("qm", [128, 512], BF16)
    QML = wt("qml", [128, 512], BF16)
    KME = wt("kme", [128, 512], BF16)
    KML = wt("kml", [128, 512], BF16)
    QTD = [wt(f"qtd{d}", [128, 1024], BF16) for d in range(2)]
    KDT = wt("kdt", [128, 512], BF16)
    OPB = [(QM, QML, KME, KML, KDT), (QM, QML, KME, KML, KDT)]
    KBS = [P.tile(f"kb{d}", [128, 512], BF16, at=TMPF[0].addr + d * 1024) for d in range(2)]
    KMS = [P.tile(f"km{d}", [128, 512], BF16, at=TMPF[1].addr + d * 1024) for d in range(2)]
    KDEC = [wt(f"kdec{d}", [128, 8, 128], BF16) for d in range(2)]
    GATE = wt("gate", [128, 1024], BF16)
    VT = wt("vt", [128, 8, 128], BF16)
    AMALL = [wt(f"amall{d}", [128, 8, 128], BF16) for d in range(2)]
    DEC = [wt(f"dec{d}", [128, 16], F32) for d in range(2)]
    S32 = [[wt(f"s32{d}{k}", [128, 128], F32, dma=True) for k in range(2)] for d in range(2)]
    SBC = [[wt(f"sbc{d}{k}", [128, 128], BF16) for k in range(2)] for d in range(2)]
    OSQ = wt("osq", [128, 512], BF16)
    ORS = TS[0][0]
    hg_end = a[0]
    a[0] = W
    UT = [wt(f"ut{i}", [128, 4, 512], BF16) for i in range(2)]
    RL = [wt(f"rl{i}", [128, 512], BF16) for i in range(2)]

    def init_va():
        op("dve", lambda e: e.memset(VA0.ap, 1.0), wr=[VA0])
        op("dve", lambda e: e.memset(VA1.ap, 0.0), wr=[VA1])
        op("dve", lambda e: e.memset(VA1[:, :, 0:1], 1.0), wr=[VA1])

    DMA_WORK = XST + QKF + VF + [CK, CV, RT1, RT2] + S32[0] + S32[1]
    for l in range(NL):
        P.dma("sp", [(GQKL.ap, gqk[l])], GQKL, load=True)
        for hf in range(2):
            P.barrier(DMA_WORK)
            i = hf
            sample = (hf == 1)
            hT = HB[0]
            om = HB[1]
            gs = [2 * hf, 2 * hf + 1]
            for gl in range(2):
                rmsnorm_to(gs[gl], l, 0, hT[gl])

            ck(3.1)
            slotA = acquire(("A", l, hf))
            wA = slotA.ap[:, 0:8 * 768].rearrange("p (a b) -> p a b", b=768)
            init_va()
            ctx_tiles = 2 if sample else 0
            if sample:
                P.dma("sp", [(CK.ap, cachek[l].rearrange("(a p) d -> p a d", p=128))], CK, load=True)
                P.dma("sp", [(CV.ap, cachev[l].rearrange("(a p) d -> p a d", p=128))], CV, load=True)
                op("act", lambda e: e.activation(out=CKB.ap, in_=CK.ap, func=AF.Copy), rd=[CK], wr=[CKB])
                pb = psb()
                for t2 in range(2):
                    op("pe", lambda e, pb=pb, t2=t2: e.matmul(pb[:, t2 * 128:(t2 + 1) * 128], lhsT=CKB[:, t2, :], rhs=idb, start=True, stop=True),
                       rd=[CKB, SMALL], wr=[pb], inc=(t2 == 1))
                op("dve", lambda e, pb=pb: e.tensor_copy(out=KT[:, 0:256], in_=pb[:, 0:256]), rd=[pb], wr=[KT])
                op("dve", lambda e: e.tensor_copy(out=VA0[:, 0:2, 0:64], in_=CV[:, :, 0:64]), rd=[CV], wr=[VA0])
                op("dve", lambda e: e.tensor_copy(out=VA1[:, 0:2, 64:128], in_=CV[:, :, 64:128]), rd=[CV], wr=[VA1])
            def stageA(tt):
                gl = tt // 4
                tsl = slice((tt % 4) * 128, (tt % 4 + 1) * 128)
                sqf = SQFS[tt % 2]
                ssq = SSQS[tt % 2]
                psq = psf()
                pskv = psf()
                for kc in range(8):
                    op("pe", lambda e, kc=kc, psq=psq, gl=gl, tsl=tsl: e.matmul(
                        psq[:, :], lhsT=hT[gl][:, kc, tsl], rhs=wA[:, kc, 0:512], start=(kc == 0), stop=(kc == 7)),
                       rd=[hT[gl], slotA], wr=[psq], inc=(kc == 7))
                for kc in range(8):
                    op("pe", lambda e, kc=kc, pskv=pskv, gl=gl, tsl=tsl: e.matmul(
                        pskv[:, 0:256], lhsT=hT[gl][:, kc, tsl], rhs=wA[:, kc, 512:768], start=(kc == 0), stop=(kc == 7)),
                       rd=[hT[gl], slotA], wr=[pskv], inc=(kc == 7))
                kti = ctx_tiles + tt
                op("act", lambda e, pskv=pskv, kti=kti: e.activation(out=VA0[:, kti, 0:64], in_=pskv[:, 128:192], func=AF.Copy), rd=[pskv], wr=[VA0])
                op("act", lambda e, pskv=pskv, kti=kti: e.activation(out=VA1[:, kti, 64:128], in_=pskv[:, 192:256], func=AF.Copy), rd=[pskv], wr=[VA1])
                if not sample:
                    vf = rr(VF, "vf")
                    op("act", lambda e, pskv=pskv, vf=vf: e.activation(out=vf.ap, in_=pskv[:, 128:256], func=AF.Copy), rd=[pskv], wr=[vf])
                    sq_i, t_i = tt // 2, (tt % 2) * 128
                    P.dma("sp", [(newv[sq_i, l, t_i:t_i + 128, :], vf.ap)], vf, load=False)
                op("act", lambda e, psq=psq: e.activation(out=sqf[:, 0:512], in_=psq[:, :], func=AF.Square), rd=[psq], wr=[sqf])
                op("act", lambda e, pskv=pskv: e.activation(out=sqf[:, 512:640], in_=pskv[:, 0:128], func=AF.Square), rd=[pskv], wr=[sqf])
                op("dve", lambda e: e.tensor_reduce(out=ssq.ap, in_=sqf.ap.rearrange("p (a b) -> p a b", b=64), axis=AX.X, op=ALU.add),
                   rd=[sqf], wr=[ssq])
                qkf = rr(QKF, "qkf")
                op("dve", lambda e, psq=psq, qkf=qkf: e.tensor_tensor(out=qkf[:, 0:512], in0=psq[:, :], in1=GQKL[:, 0:512], op=ALU.mult),
                   rd=[psq, GQKL], wr=[qkf])
                op("dve", lambda e, pskv=pskv, qkf=qkf: e.tensor_tensor(out=qkf[:, 512:640], in0=pskv[:, 0:128], in1=GQKL[:, 512:640], op=ALU.mult),
                   rd=[pskv, GQKL], wr=[qkf])
                return dict(qkf=qkf, kti=kti, ssq=ssq)

            def stageB(tt, st_):
                qkf = st_['qkf']
                kti = st_['kti']
                ssq = st_['ssq']
                op("act", lambda e: e.activation(out=ssq.ap, in_=ssq.ap, func=AF.Ln, scale=1.0 / 64, bias=EPS), rd=[ssq], wr=[ssq])
                op("act", lambda e: e.activation(out=ssq.ap, in_=ssq.ap, func=AF.Exp, scale=-0.5), rd=[ssq], wr=[ssq])
                op("dve", lambda e, qkf=qkf: e.tensor_tensor(
                    out=qkf.ap.rearrange("p (a b) -> p a b", b=64), in0=qkf.ap.rearrange("p (a b) -> p a b", b=64),
                    in1=ssq.ap.unsqueeze(2).to_broadcast([128, 10, 64]), op=ALU.mult), rd=[ssq], wr=[qkf])
                qb = rr(QB, "qb")
                if not sample:
                    sq_i, t_i = tt // 2, (tt % 2) * 128
                    P.dma("sp", [(newk[sq_i, l, t_i:t_i + 128, :], qkf[:, 512:640])], qkf, load=False)
                    op("act", lambda e, qkf=qkf, qb=qb: e.activation(
                        out=qb[:, 0:512].rearrange("p (g k d) -> p k g d", g=4, k=2),
                        in_=qkf[:, 0:512].rearrange("p (k g d) -> p k g d", k=2, g=4), func=AF.Copy), rd=[qkf], wr=[qb])
                    op("act", lambda e, qkf=qkf, qb=qb: e.activation(out=qb[:, 512:640], in_=qkf[:, 512:640], func=AF.Copy), rd=[qkf], wr=[qb])
                else:
                    q3 = qkf.ap.rearrange("p (a b) -> p a b", b=64)
                    r2 = RT2.ap.rearrange("p (a b) -> p a b", b=64)
                    P.dma("sp", [(RT1.ap, cosr[tt])], RT1, load=True)
                    P.dma("sp", [(RT2.ap, sinr[tt])], RT2, load=True)
                    op("dve", lambda e, qkf=qkf: e.tensor_tensor(out=RT1.ap, in0=RT1.ap, in1=qkf.ap, op=ALU.mult), rd=[qkf], wr=[RT1])
                    for ax in range(2):
                        lo = slice(ax * 32, ax * 32 + 16)
                        hi = slice(ax * 32 + 16, ax * 32 + 32)
                        op("dve", lambda e, q3=q3, r2=r2, lo=lo, hi=hi: e.tensor_tensor(out=r2[:, :, lo], in0=r2[:, :, lo], in1=q3[:, :, hi], op=ALU.mult),
                           rd=[qkf], wr=[RT2])
                        op("dve", lambda e, q3=q3, r2=r2, lo=lo, hi=hi: e.tensor_tensor(out=r2[:, :, hi], in0=r2[:, :, hi], in1=q3[:, :, lo], op=ALU.mult),
                           rd=[qkf], wr=[RT2])
                    op("dve", lambda e, qb=qb: e.tensor_tensor(
                        out=qb[:, 0:512].rearrange("p (g k d) -> p k g d", g=4, k=2),
                        in0=RT1[:, 0:512].rearrange("p (k g d) -> p k g d", k=2, g=4),
                        in1=RT2[:, 0:512].rearrange("p (k g d) -> p k g d", k=2, g=4), op=ALU.add), rd=[RT1, RT2], wr=[qb])
                    op("dve", lambda e, qb=qb: e.tensor_tensor(out=qb[:, 512:640], in0=RT1[:, 512:640], in1=RT2[:, 512:640], op=ALU.add),
                       rd=[RT1, RT2], wr=[qb])
                pb = psb()
                for g4 in range(4):
                    src = qb[:, g4 * 128:(g4 + 1) * 128]
                    op("pe", lambda e, pb=pb, src=src, g4=g4: e.matmul(pb[:, g4 * 128:(g4 + 1) * 128], lhsT=src, rhs=idb, start=True, stop=True),
                       rd=[qb, SMALL], wr=[pb], inc=(g4 == 3))
                pbk = psf()
                op("pe", lambda e, pbk=pbk, qb=qb: e.matmul(pbk[:, 0:128], lhsT=qb[:, 512:640], rhs=idb, start=True, stop=True),
                   rd=[qb, SMALL], wr=[pbk])
                op("dve", lambda e, pb=pb, tt=tt: e.tensor_copy(
                    out=QT[:, tt, :], in_=pb[:, 0:512]), rd=[pb], wr=[QT])
                op("act", lambda e, pbk=pbk, kti=kti: e.activation(out=KT[:, kti * 128:(kti + 1) * 128], in_=pbk[:, 0:128], func=AF.Copy),
                   rd=[pbk], wr=[KT])


            st_q = {}
            for tt in range(9):
                if tt < 8:
                    st_q[tt] = stageA(tt)
                if tt >= 1:
                    stageB(tt - 1, st_q.pop(tt - 1))

            ck(4)
            if sample:
                units = [(list(range(8)), list(range(10)))]
            else:
                units = [([2 * j, 2 * j + 1], [2 * j, 2 * j + 1]) for j in range(4)]
            blocks = []
            for (qtiles, ktiles) in units:
                for kv in range(2):
                    for qt in qtiles:
                        blocks.append((kv, qt, ktiles))
            tasks = [(bi, ki) for bi, b in enumerate(blocks) for ki in range(len(b[2]))]
            LA = 2
            ptq = {}
            psoq = {}
            fin2 = {}

            def emit_qk(j):
                bi, ki = tasks[j]
                kv, qt, ktiles = blocks[bi]
                kt = ktiles[ki]
                pr = slice(kv * 64, kv * 64 + 64)
                pss = psf()
                op("pe", lambda e, pss=pss, kt=kt, qt=qt, pr=pr: e.matmul(
                    pss[:, :], lhsT=KT[pr, kt * 128:(kt + 1) * 128], rhs=QT[pr, qt, :], start=True, stop=True),
                   rd=[KT, QT], wr=[pss])
                pt = rr(PT, "pt")
                op("act", lambda e, pss=pss, pt=pt: e.activation(out=pt.ap, in_=pss[:, :], func=AF.Exp, scale=0.125), rd=[pss], wr=[pt])
                ptq[j] = pt

            def emit_pv(j):
                bi, ki = tasks[j]
                kv, qt, ktiles = blocks[bi]
                kt = ktiles[ki]
                va = VA0 if kv == 0 else VA1
                M = 65 if kv == 0 else 128
                if ki == 0:
                    psoq[bi] = psacc()
                pso = psoq[bi]
                pt = ptq.pop(j)
                n = len(ktiles)
                op("pe", lambda e, pso=pso, va=va, kt=kt, pt=pt, M=M, ki=ki, n=n: e.matmul(
                    pso[0:M, :], lhsT=va[:, kt, 0:M], rhs=pt.ap, start=(ki == 0), stop=(ki == n - 1)),
                   rd=[va, pt], wr=[pso])
                if ki == n - 1:
                    srow = 64 if kv == 0 else 0
                    REC = RECS[bi % 2]
                    op("act", lambda e, pso=pso, srow=srow, REC=REC: e.activation(out=REC[srow:srow + 1, :], in_=pso[srow:srow + 1, :], func=AF.Ln), rd=[pso], wr=[REC])
                    op("act", lambda e, srow=srow, REC=REC: e.activation(out=REC[srow:srow + 1, :], in_=REC[srow:srow + 1, :], func=AF.Exp, scale=-1.0), rd=[], wr=[REC])
                    fin2.setdefault(j + 2, []).append(bi)

            def emit_fin2(bi):
                kv, qt, ktiles = blocks[bi]
                pr = slice(kv * 64, kv * 64 + 64)
                srow = 64 if kv == 0 else 0
                pso = psoq[bi]
                REC = RECS[bi % 2]
                psbc = psf()
                op("pe", lambda e, psbc=psbc, srow=srow, REC=REC: e.matmul(
                    psbc[:, :], lhsT=ones_f[srow:srow + 1, :], rhs=REC[srow:srow + 1, :], start=True, stop=True),
                   rd=[REC, SMALL], wr=[psbc])
                op("act", lambda e, psbc=psbc, pr=pr: e.activation(out=BC[pr, :], in_=psbc[pr, :], func=AF.Copy), rd=[psbc], wr=[BC])
                gl = qt // 4
                tsl = slice((qt % 4) * 128, (qt % 4 + 1) * 128)
                op("dve", lambda e, pso=pso, pr=pr, gl=gl, tsl=tsl: e.tensor_tensor(
                    out=om[gl][pr, 4:8, tsl], in0=pso[pr, :].rearrange("p (a b) -> p a b", b=128),
                    in1=BC[pr, :].rearrange("p (a b) -> p a b", b=128), op=ALU.mult), rd=[pso, BC], wr=[om[gl]])

            nt_ = len(tasks)
            for j in range(nt_ + LA + 3):
                if j < nt_:
                    emit_qk(j)
                jj = j - LA
                if 0 <= jj < nt_:
                    emit_pv(jj)
                for bi in fin2.pop(jj, []):
                    emit_fin2(bi)
            assert not fin2 and not ptq

            ck(5)
            P.barrier(DMA_WORK)
            seqs = [(0, 1024)] if sample else [(j * 256, 256) for j in range(4)]
            OACC = [PSF[6], PSF[7]]
            for hd in range(4):
                slotR = acquire(("R", l, hf, hd))
                wR = slotR.ap[:, 0:8 * 640].rearrange("p (a b) -> p a b", b=640)
                for gl in range(2):
                    gsl = slice(gl * 512, (gl + 1) * 512)
                    pps = {}
                    for ci in (0, 4, 1, 2):
                        ps = psf()
                        pps[ci] = ps
                        for kc in range(8):
                            op("pe", lambda e, ps=ps, kc=kc, ci=ci, gl=gl: e.matmul(
                                ps[:, :], lhsT=wR[:, kc, ci * 128:(ci + 1) * 128], rhs=hT[gl][:, kc, :], start=(kc == 0), stop=(kc == 7)),
                               rd=[slotR, hT[gl]], wr=[ps], inc=(kc == 7))
                        if ci == 0:
                            op("act", lambda e, ps=ps: e.activation(out=Q32.ap, in_=ps[:, :], func=AF.Copy), rd=[ps], wr=[Q32])
                        elif ci == 4:
                            GT = TS[1][3]
                            op("act", lambda e, ps=ps, GT=GT: e.activation(out=GT.ap, in_=ps[:, :], func=AF.Exp, scale=-1.0), rd=[ps], wr=[GT])
                            op("act", lambda e, GT=GT: e.activation(out=GT.ap, in_=GT.ap, func=AF.Ln, bias=1.0), rd=[], wr=[GT])
                            op("act", lambda e, GT=GT: e.activation(out=GT.ap, in_=GT.ap, func=AF.Exp, scale=-1.0), rd=[], wr=[GT])
                            op("dve", lambda e, ps=ps, gsl=gsl, GT=GT: e.tensor_tensor(out=GATE[:, gsl], in0=ps[:, :], in1=GT.ap, op=ALU.mult), rd=[ps, GT], wr=[GATE])

                    def dir_steps(d, ps, gl=gl, gsl=gsl, hd=hd, l=l):
                        T1, T2, T3, T4 = TS[d]
                        QM, QML, KME, KML, KDT = OPB[d]
                        KB = KBS[d]
                        KMb = KMS[d]
                        sc_oml = oml_sb[:, d * 4 + hd, l:l + 1]
                        sc_lb = lb_sb[:, d * 4 + hd, l:l + 1]
                        sc_noml = noml_sb[:, d * 4 + hd, l:l + 1]
                        last = 63 if d == 0 else 0
                        mid = 31 if d == 0 else 32
                        mE, mL = (MH0, MH1) if d == 0 else (MH1, MH0)
                        t2c = T2.ap.rearrange("p (c t) -> p c t", t=64)
                        t3c = T3.ap.rearrange("p (c t) -> p c t", t=64)
                        t4c = T4.ap.rearrange("p (c t) -> p c t", t=64)
                        msk = mf if d == 0 else mb
                        st = []
                        A = st.append
                        A(lambda: op("act", lambda e: e.activation(out=T1.ap, in_=ps[:, :], func=AF.Exp, scale=-1.0), rd=[ps], wr=[T1]))
                        A(lambda: op("act", lambda e: e.activation(out=T2.ap, in_=T1.ap, func=AF.Ln, bias=1.0), rd=[T1], wr=[T2]))
                        A(lambda: op("act", lambda e: e.activation(out=T4.ap, in_=T1.ap, func=AF.Ln, scale=sc_lb, bias=1.0), rd=[T1, SMALL], wr=[T4]))
                        A(lambda: op("pool", lambda e: e.tensor_tensor(out=T2.ap, in0=T4.ap, in1=T2.ap, op=ALU.subtract), rd=[T4], wr=[T2]))
                        A(lambda: op("act", lambda e: e.activation(out=T1.ap, in_=T2.ap, func=AF.Exp), rd=[T2], wr=[T1]))
                        A(lambda: op("dve", lambda e: e.tensor_tensor_scan(out=T3.ap, data0=rst, data1=T2.ap, initial=0.0, op0=ALU.mult, op1=ALU.add),
                                     rd=[T2, CONST], wr=[T3]))
                        if d == 1:
                            A(lambda: op("dve", lambda e: e.tensor_tensor(
                                out=t3c, in0=t3c[:, :, 63:64].to_broadcast([128, 8, 64]), in1=t3c, op=ALU.subtract), rd=[], wr=[T3]))
                            A(lambda: op("dve", lambda e: e.tensor_tensor(out=T3.ap, in0=T3.ap, in1=T2.ap, op=ALU.add), rd=[T2], wr=[T3]))
                        A(lambda: op("dve", lambda e: e.tensor_scalar(out=KB.ap, in0=T1.ap, scalar1=-1.0, scalar2=1.0, op0=ALU.mult, op1=ALU.add), rd=[T1], wr=[KB]))
                        A(lambda: op("act", lambda e: e.activation(out=T2.ap, in_=T3.ap, func=AF.Exp), rd=[T3], wr=[T2]))
                        A(lambda: op("pool", lambda e: e.tensor_tensor(out=QTD[d][:, gsl], in0=Q32.ap, in1=T2.ap, op=ALU.mult), rd=[Q32, T2], wr=[QTD[d]]))
                        A(lambda: op("pool", lambda e: e.tensor_copy(out=DEC[d][:, gl * 8:(gl + 1) * 8], in_=t2c[:, :, last]), rd=[T2], wr=[DEC[d]]))
                        A(lambda: op("dve", lambda e: e.tensor_tensor(
                            out=t4c, in0=t3c[:, :, last:last + 1].to_broadcast([128, 8, 64]), in1=t3c, op=ALU.subtract), rd=[T3], wr=[T4]))
                        A(lambda: op("act", lambda e: e.activation(out=T4.ap, in_=T4.ap, func=AF.Exp), rd=[], wr=[T4]))
                        A(lambda: op("dve", lambda e: e.tensor_tensor(out=KDT.ap, in0=T4.ap, in1=KB.ap, op=ALU.mult), rd=[KB, T4], wr=[KDT]))

                        def kdt_tr():
                            pb = psf()
                            for t4 in range(4):
                                op("pe", lambda e, pb=pb, t4=t4: e.matmul(pb[:, t4 * 128:(t4 + 1) * 128], lhsT=KDT[:, t4 * 128:(t4 + 1) * 128], rhs=idb, start=True, stop=True),
                                   rd=[KDT, SMALL], wr=[pb], inc=(t4 == 3))
                            op("act", lambda e, pb=pb: e.activation(
                                out=KDEC[d][:, gl * 4:(gl + 1) * 4, :], in_=pb[:, 0:512].rearrange("p (a b) -> p a b", b=128), func=AF.Copy), rd=[pb], wr=[KDEC[d]])
                        A(kdt_tr)
                        A(lambda: op("dve", lambda e: e.tensor_tensor(
                            out=t4c, in0=t3c, in1=t3c[:, :, mid:mid + 1].to_broadcast([128, 8, 64]), op=ALU.subtract), rd=[T3], wr=[T4]))
                        A(lambda: op("act", lambda e: e.activation(out=T2.ap, in_=T4.ap, func=AF.Exp), rd=[T4], wr=[T2]))
                        A(lambda: op("act", lambda e: e.activation(out=T4.ap, in_=T4.ap, func=AF.Exp, scale=-1.0), rd=[], wr=[T4]))
                        A(lambda: op("dve", lambda e: e.tensor_tensor(out=QM.ap, in0=Q32.ap, in1=T2.ap, op=ALU.mult), rd=[Q32, T2], wr=[QM]))
                        A(lambda: op("pool", lambda e: e.tensor_tensor(out=QML.ap, in0=QM.ap, in1=mL, op=ALU.mult), rd=[QM, CONSTB], wr=[QML]))
                        A(lambda: op("dve", lambda e: e.tensor_tensor(out=KMb.ap, in0=T4.ap, in1=KB.ap, op=ALU.mult), rd=[T4, KB], wr=[KMb]))
                        A(lambda: op("pool", lambda e: e.tensor_tensor(out=KME.ap, in0=KMb.ap, in1=mE, op=ALU.mult), rd=[KMb, CONSTB], wr=[KME]))
                        A(lambda: op("dve", lambda e: e.tensor_tensor(out=KML.ap, in0=KMb.ap, in1=mL, op=ALU.mult), rd=[KMb, CONSTB], wr=[KML]))

                        def amat():
                            psa = psf()
                            for t4 in range(4):
                                tsl = slice(t4 * 128, (t4 + 1) * 128)
                                op("pe", lambda e, psa=psa, tsl=tsl: e.matmul(psa[:, tsl], lhsT=KME[:, tsl], rhs=QM[:, tsl], start=True, stop=False, skip_group_check=True),
                                   rd=[KME, QM], wr=[psa], inc=False)
                                op("pe", lambda e, psa=psa, tsl=tsl: e.matmul(psa[:, tsl], lhsT=KML[:, tsl], rhs=QML[:, tsl], start=False, stop=True, skip_group_check=True),
                                   rd=[KML, QML], wr=[psa], inc=(t4 == 3))
                            for t4 in range(4):
                                op("dve", lambda e, psa=psa, t4=t4: e.tensor_tensor(
                                    out=AMALL[d][:, gl * 4 + t4, :], in0=psa[:, t4 * 128:(t4 + 1) * 128], in1=msk, op=ALU.mult),
                                   rd=[psa, CONST], wr=[AMALL[d]])
                        A(amat)
                        return st

                    sts = [dir_steps(0, pps[1]), dir_steps(1, pps[2])]
                    LAG = 4
                    for k_ in range(max(len(sts[0]), len(sts[1]) + LAG)):
                        if k_ < len(sts[0]):
                            sts[0][k_]()
                        if 0 <= k_ - LAG < len(sts[1]):
                            sts[1][k_ - LAG]()
                    psv = psf()
                    for t4 in range(4):
                        for kc in range(8):
                            op("pe", lambda e, psv=psv, t4=t4, kc=kc, gl=gl: e.matmul(
                                psv[:, t4 * 128:(t4 + 1) * 128], lhsT=hT[gl][:, kc, t4 * 128:(t4 + 1) * 128], rhs=wR[:, kc, 384:512],
                                start=(kc == 0), stop=(kc == 7)), rd=[slotR, hT[gl]], wr=[psv], inc=(kc == 7 and t4 == 3))
                    op("act", lambda e, psv=psv, gl=gl: e.activation(
                        out=VT[:, gl * 4:(gl + 1) * 4, :], in_=psv[:, :].rearrange("p (a b) -> p a b", b=128), func=AF.Copy), rd=[psv], wr=[VT])

                ck(5.1)
                for acc in OACC:
                    op("pe", lambda e, acc=acc: e.matmul(acc[:, :], lhsT=zeros_bf, rhs=MH0, start=True, stop=True, skip_group_check=True),
                       rd=[SMALL, CONSTB], wr=[acc])
                for ti in range(8):
                    acc = OACC[ti // 4]
                    osl = slice((ti % 4) * 128, (ti % 4 + 1) * 128)
                    for d in range(2):
                        op("pe", lambda e, acc=acc, osl=osl, ti=ti, d=d: e.matmul(
                            acc[:, osl], lhsT=VT[:, ti, :], rhs=AMALL[d][:, ti, :], start=False, stop=True, skip_group_check=True),
                           rd=[VT, AMALL[d]], wr=[acc], inc=(d == 1))
                ck(5.2)
                for p0 in range(0, len(seqs), 2):
                    grp = seqs[p0:p0 + 2]
                    cur = {}
                    for k, (s0, slen) in enumerate(grp):
                        for d in range(2):
                            if sample:
                                P.dma("sp", [(S32[d][k].ap, state0[l, d, hd])], S32[d][k], load=True)
                            else:
                                op("dve", lambda e, d=d, k=k: e.memset(S32[d][k].ap, 0.0), wr=[S32[d][k]])
                            op("act", lambda e, d=d, k=k: e.activation(out=SBC[d][k].ap, in_=S32[d][k].ap, func=AF.Copy), rd=[S32[d][k]], wr=[SBC[d][k]])
                    nch = grp[0][1] // 64
                    for step in range(nch):
                        for k, (s0, slen) in enumerate(grp):
                            c0 = s0 // 64
                            for d in range(2):
                                c = c0 + (step if d == 0 else nch - 1 - step)
                                tile_i, half_i = c // 2, c % 2
                                prt = slice(half_i * 64, half_i * 64 + 64)
                                acc = OACC[tile_i // 4]
                                ocol = (tile_i % 4) * 128 + half_i * 64
                                op("pe", lambda e, acc=acc, ocol=ocol, d=d, k=k, c=c: e.matmul(
                                    acc[:, ocol:ocol + 64], lhsT=SBC[d][k].ap, rhs=QTD[d][:, c * 64:(c + 1) * 64], start=False, stop=True, skip_group_check=True),
                                   rd=[SBC[d][k], QTD[d]], wr=[acc])
                                pkv = psf()
                                op("pe", lambda e, pkv=pkv, d=d, tile_i=tile_i, prt=prt: e.matmul(
                                    pkv[:, 0:128], lhsT=KDEC[d][prt, tile_i, :], rhs=VT[prt, tile_i, :], start=True, stop=True),
                                   rd=[KDEC[d], VT], wr=[pkv])
                                op("dve", lambda e, pkv=pkv, d=d, k=k, c=c: e.scalar_tensor_tensor(
                                    out=S32[d][k].ap, in0=S32[d][k].ap, scalar=DEC[d][:, c:c + 1], in1=pkv[:, 0:128], op0=ALU.mult, op1=ALU.add),
                                   rd=[pkv, DEC[d]], wr=[S32[d][k]])
                                if step < nch - 1:
                                    op("dve", lambda e, d=d, k=k: e.tensor_copy(out=SBC[d][k].ap, in_=S32[d][k].ap), rd=[S32[d][k]], wr=[SBC[d][k]])
                    if not sample:
                        for k in range(len(grp)):
                            for d in range(2):
                                P.dma("sp", [(news[p0 + k, l, d, hd], S32[d][k].ap)], S32[d][k], load=False)
                ck(5.3)
                for gl in range(2):
                    acc = OACC[gl]
                    hsl = slice(gl * 512, (gl + 1) * 512)
                    op("act", lambda e, acc=acc: e.activation(out=OSQ.ap, in_=acc[:, :], func=AF.Square), rd=[acc], wr=[OSQ])
                    pss = psf()
                    op("pe", lambda e, pss=pss: e.matmul(pss[:, :], lhsT=ones_bf, rhs=OSQ.ap, start=True, stop=True), rd=[OSQ, SMALL], wr=[pss])
                    op("act", lambda e, pss=pss: e.activation(out=ORS.ap, in_=pss[:, :], func=AF.Ln, scale=1.0 / 128, bias=EPS), rd=[pss], wr=[ORS])
                    op("act", lambda e: e.activation(out=ORS.ap, in_=ORS.ap, func=AF.Exp, scale=-0.5), rd=[], wr=[ORS])
                    op("dve", lambda e, hsl=hsl: e.tensor_tensor(out=ORS.ap, in0=ORS.ap, in1=GATE[:, hsl], op=ALU.mult), rd=[GATE], wr=[ORS])
                    op("dve", lambda e, acc=acc, gl=gl, hd=hd: e.scalar_tensor_tensor(
                        out=om[gl][:, hd, :], in0=acc[:, :], scalar=hgn_sb[:, l:l + 1], in1=ORS.ap, op0=ALU.mult, op1=ALU.mult),
                       rd=[acc, ORS, CONST], wr=[om[gl]])

            ck(6)
            slotO = acquire(("O", l, hf))
            wO = slotO.ap.rearrange("p (a b) -> p a b", b=1024)
            for gl in range(2):
                g = gs[gl]
                for dc in range(8):
                    ps = psf()
                    for kc in range(8):
                        op("pe", lambda e, ps=ps, kc=kc, dc=dc, gl=gl: e.matmul(
                            ps[:, :], lhsT=wO[:, kc, dc * 128:(dc + 1) * 128], rhs=om[gl][:, kc, :], start=(kc == 0), stop=(kc == 7)),
                           rd=[slotO, om[gl]], wr=[ps], inc=(kc == 7))
                    op("dve", lambda e, ps=ps, dc=dc, g=g: e.scalar_tensor_tensor(
                        out=XG[g][:, dc, :], in0=ps[:, :], scalar=modv(l, 2, dc, i), in1=XG[g][:, dc, :], op0=ALU.mult, op1=ALU.add),
                       rd=[ps, MODS], wr=[XG[g]])

        ck(8)
        P.barrier(DMA_WORK)
        if l + 1 < NL:
            emit_mods(l + 1)
        def ffn_u(c, g, slotF):
            w1v = slotF.ap[:, 0:4096].rearrange("p (a b) -> p a b", b=512)
            h2 = HB[g // 2][g % 2]
            ut = rr(UT, "ut")
            for f in range(4):
                ps = psf()
                for kc in range(8):
                    op("pe", lambda e, ps=ps, kc=kc, f=f, h2=h2: e.matmul(
                        ps[:, :], lhsT=w1v[:, kc, f * 128:(f + 1) * 128], rhs=h2[:, kc, :], start=(kc == 0), stop=(kc == 7)),
                       rd=[slotF, h2], wr=[ps], inc=(kc == 7))
                rl = rr(RL, "rl")
                op("act", lambda e, ps=ps, rl=rl: e.activation(out=rl.ap, in_=ps[:, :], func=AF.Relu), rd=[ps], wr=[rl])
                op("dve", lambda e, rl=rl, ut=ut, f=f: e.tensor_tensor(out=ut[:, f, :], in0=rl.ap, in1=rl.ap, op=ALU.mult), rd=[rl], wr=[ut])
            return ut

        def ffn_y(c, g, slotF, ut):
            w2v = slotF.ap[:, 4096:8192].rearrange("p (a b) -> p a b", b=1024)
            i = g // 2
            for dc in range(8):
                ps = psf()
                for f in range(4):
                    op("pe", lambda e, ps=ps, f=f, dc=dc, ut=ut: e.matmul(
                        ps[:, :], lhsT=w2v[:, f, dc * 128:(dc + 1) * 128], rhs=ut[:, f, :], start=(f == 0), stop=(f == 3)),
                       rd=[slotF, ut], wr=[ps], inc=(f == 3))
                op("dve", lambda e, ps=ps, dc=dc, g=g, i=i: e.scalar_tensor_tensor(
                    out=XG[g][:, dc, :], in0=ps[:, :], scalar=modv(l, 5, dc, i), in1=XG[g][:, dc, :], op0=ALU.mult, op1=ALU.add),
                   rd=[ps, MODS], wr=[XG[g]])

        pend = None
        slotF = acquire(("F", l, 0))
        for g in range(4):
            rmsnorm_to(g, l, 1, HB[g // 2][g % 2])
            ut = ffn_u(0, g, slotF)
            if pend is not None:
                ffn_y(*pend)
            pend = (0, g, slotF, ut)
        for c in range(1, 8):
            slotF = acquire(("F", l, c))
            for g in range(4):
                ut = ffn_u(c, g, slotF)
                ffn_y(*pend)
                pend = (c, g, slotF, ut)
        ffn_y(*pend)

    ck(10)
    P.barrier(DMA_WORK)
    for g in range(4):
        xg = XG[g]
        ps = psf()
        for kc in range(8):
            sq = rr(SQ, "sq")
            op("act", lambda e, sq=sq, kc=kc, xg=xg: e.activation(out=sq.ap, in_=xg[:, kc, :], func=AF.Square), rd=[xg], wr=[sq])
            op("pe", lambda e, sq=sq, kc=kc, ps=ps: e.matmul(ps[:, :], lhsT=ones_bf, rhs=sq.ap, start=(kc == 0), stop=(kc == 7)),
               rd=[sq, SMALL], wr=[ps])
        op("act", lambda e, ps=ps: e.activation(out=RSTD.ap, in_=ps[:, :], func=AF.Ln, scale=1.0 / DM, bias=EPS), rd=[ps], wr=[RSTD])
        op("act", lambda e: e.activation(out=RSTD.ap, in_=RSTD.ap, func=AF.Exp, scale=-0.5), rd=[], wr=[RSTD])
        for kc in range(8):
            op("dve", lambda e, kc=kc, xg=xg: e.scalar_tensor_tensor(
                out=xg[:, kc, :], in0=xg[:, kc, :], scalar=fng_sb[:, kc:kc + 1], in1=RSTD.ap, op0=ALU.mult, op1=ALU.mult),
               rd=[RSTD, CONST], wr=[xg])
        for t4 in range(4):
            tt = g * 4 + t4
            st = XST[tt % 2]
            for hh in range(2):
                ps2 = psf()
                for k4 in range(4):
                    kc = hh * 4 + k4
                    op("pe", lambda e, ps2=ps2, kc=kc, k4=k4, t4=t4, xg=xg: e.transpose(
                        out=ps2[:, k4 * 128:(k4 + 1) * 128], in_=xg[:, kc, t4 * 128:(t4 + 1) * 128], identity=idf),
                       rd=[xg, CONST], wr=[ps2], inc=(k4 == 3))
                if hh == 0:
                    op("dve", lambda e, ps2=ps2, st=st: e.tensor_copy(out=st[:, 0:512], in_=ps2[:, :]), rd=[ps2], wr=[st])
                else:
                    op("act", lambda e, ps2=ps2, st=st: e.activation(out=st[:, 512:1024], in_=ps2[:, :], func=AF.Copy), rd=[ps2], wr=[st])
            P.dma("sp", [(yout[tt * 128:(tt + 1) * 128, :], st.ap)], st, load=False)

def _tail(nc, es, P):
    P.finish()
    print("stream sizes", P.simulate(), "cnt", P.cnt)
    with nc.Block() as block:
        P.replay(block)
    es.close()
    return nc


def build_program():
    nc = bass.Bass("TRN2", target_bir_lowering=False)
    es = contextlib.ExitStack()
    P = Prog(nc, es)
    try:
        _body(nc, es, P)
    except _Stop:
        pass
    return _tail(nc, es, P)


_CACHE = {}


def _get_prog():
    if "nc" not in _CACHE:
        _CACHE["nc"] = build_program()
    return _CACHE["nc"]


def _rope_tables():
    T = DEC_SEQ
    GRID_W = 64
    rows = T // GRID_W
    row = np.repeat(np.arange(rows, dtype=np.float32), GRID_W)
    col = np.tile(np.arange(GRID_W, dtype=np.float32), rows)
    inv = (10000.0 ** (-np.arange(0, 32, 2, dtype=np.float32) / 32)).astype(np.float32)
    ar = row[:, None] * inv[None, :]
    ac = col[:, None] * inv[None, :]
    ang = np.concatenate([ar, ar, ac, ac], axis=-1).astype(np.float32)
    cos = np.cos(ang).astype(np.float32)
    sin = np.sin(ang).astype(np.float32)
    sgn = np.ones(64, np.float32)
    sgn[0:16] = -1.0
    sgn[32:48] = -1.0
    sins = sin * sgn[None, :]
    cosT = np.tile(cos.reshape(8, 128, 64), (1, 1, 10))
    sinT = np.tile(sins.reshape(8, 128, 64), (1, 1, 10))
    return np.ascontiguousarray(cosT), np.ascontiguousarray(sinT)


def kernel(x_prompt, x_sample, cache_k, cache_v, state_hgrn, c, c_ctx, w_mod, b_mod, norm1_g, w_in, lb_raw,
           hgrn_norm_g, q_norm_g, k_norm_g, w_out, norm2_g, w1, w2, final_norm_g):
    f = lambda a: np.ascontiguousarray(np.asarray(a, dtype=np.float32))
    x_prompt, x_sample, cache_k, cache_v, state_hgrn = map(f, (x_prompt, x_sample, cache_k, cache_v, state_hgrn))
    c, c_ctx, w_mod, b_mod, norm1_g, w_in, lb_raw = map(f, (c, c_ctx, w_mod, b_mod, norm1_g, w_in, lb_raw))
    hgrn_norm_g, q_norm_g, k_norm_g, w_out, norm2_g, w1, w2, final_norm_g = map(
        f, (hgrn_norm_g, q_norm_g, k_norm_g, w_out, norm2_g, w1, w2, final_norm_g))

    colperm = []
    for hd in range(4):
        for blk in (0, 512, 1024, 1536, 2048):
            colperm += list(range(blk + hd * 128, blk + (hd + 1) * 128))
    colperm += list(range(2560, 3328))
    w_in_p = np.ascontiguousarray(w_in[:, :, colperm])
    rowperm = list(range(512))
    for g in range(4):
        rowperm += list(range(512 + g * 64, 512 + (g + 1) * 64))
        rowperm += list(range(512 + (4 + g) * 64, 512 + (5 + g) * 64))
    w_out_p = np.ascontiguousarray(w_out[:, rowperm, :])

    def fm(v):
        return v.reshape(v.shape[:-1] + (8, 128))

    bmod_l = np.ascontiguousarray(b_mod.reshape(NL, 48, 128).transpose(2, 0, 1).reshape(128, NL * 48))
    n1g_l = np.ascontiguousarray(norm1_g.reshape(NL, 8, 128).transpose(2, 0, 1).reshape(128, NL * 8))
    n2g_l = np.ascontiguousarray(norm2_g.reshape(NL, 8, 128).transpose(2, 0, 1).reshape(128, NL * 8))
    fng_l = np.ascontiguousarray(final_norm_g.reshape(8, 128).T)
    lbr_l = np.ascontiguousarray(lb_raw.reshape(NL, 2, 4, 128).transpose(3, 1, 2, 0).reshape(128, 8 * NL))
    hgn_l = np.ascontiguousarray(hgrn_norm_g.T)
    gqk_l = np.concatenate([np.tile(q_norm_g, (1, 8)), np.tile(k_norm_g, (1, 2))], axis=1)
    gqk_l = np.ascontiguousarray(np.broadcast_to(gqk_l[:, None, :], (NL, 128, 640)))
    cosT, sinT = _rope_tables()
    ident = np.eye(128, dtype=np.float32)
    s_idx = np.arange(128)[:, None]
    t_idx = np.arange(128)[None, :]
    same = (s_idx // 64) == (t_idx // 64)
    maskf = (same & (s_idx <= t_idx)).astype(np.float32)
    maskb = (same & (s_idx >= t_idx)).astype(np.float32)
    mh0 = np.ascontiguousarray(np.broadcast_to(((np.arange(512) % 64) < 32).astype(np.float32)[None, :], (128, 512)))
    mh1 = np.ascontiguousarray(1.0 - mh0)
    rstm = np.ones((128, 1024), np.float32)
    rstm[:, ::64] = 0.0

    in_maps = []
    for core in range(NCORES):
        b = core % 4
        xin = np.concatenate([x_prompt[core * 4:(core + 1) * 4].reshape(1024, DM), x_sample[b]], axis=0)
        cv = np.stack([c_ctx.reshape(8, 128), c[b].reshape(8, 128)], axis=-1)
        cv = np.ascontiguousarray(cv.transpose(1, 0, 2).reshape(128, 16))
        in_maps.append({
            "xin": np.ascontiguousarray(xin), "cvec": cv, "w_mod": w_mod, "bmod": bmod_l, "n1g": n1g_l, "n2g": n2g_l,
            "fng": fng_l, "w_in": w_in_p, "w_out": w_out_p, "w1": w1, "w2": w2, "lbr": lbr_l, "hgn": hgn_l,
            "gqk": gqk_l,
            "cachek": np.ascontiguousarray(cache_k[b].reshape(NL, PAST, 128)),
            "cachev": np.ascontiguousarray(cache_v[b].reshape(NL, PAST, 128)),
            "state0": np.ascontiguousarray(state_hgrn[b]),
            "cosr": cosT, "sinr": sinT, "identd": ident, "maskf": maskf, "maskb": maskb, "rstd": rstm, "mh0d": mh0, "mh1d": mh1,
        })
    nc = _get_prog()
    res = run_bass_kernel_spmd(nc, in_maps, core_ids=list(range(NCORES)))
    R = res.results
    y_prompt = np.concatenate([R[cix]["yout"][:1024].reshape(4, SEQ, DM) for cix in range(NCORES)], axis=0)
    y_sample = np.stack([R[cix]["yout"][1024:] for cix in range(4)], axis=0)
    new_k = np.concatenate([R[cix]["newk"].reshape(4, NL, SEQ, 2, 64) for cix in range(NCORES)], axis=0)
    new_v = np.concatenate([R[cix]["newv"].reshape(4, NL, SEQ, 2, 64) for cix in range(NCORES)], axis=0)
    new_s = np.concatenate([R[cix]["news"] for cix in range(NCORES)], axis=0)
    return (y_prompt.astype(np.float32), y_sample.astype(np.float32), new_k.astype(np.float32),
            new_v.astype(np.float32), new_s.astype(np.float32))
```

```python
import contextlib
import numpy as np
import concourse.bass as bass
import concourse.mybir as mybir
from concourse.bass_utils import run_bass_kernel_spmd

F32 = mybir.dt.float32
BF16 = mybir.dt.bfloat16
AF = mybir.ActivationFunctionType
ALU = mybir.AluOpType
AX = mybir.AxisListType

NCORES = 8
DM = 1024
DEPTH = 4
NSEQP = 4
SEQ = 256
DEC_SEQ = 1024
PAST = 256
NT = 2048
INW = 3328
DFF = 4096
EPS = 1e-6
NL = DEPTH


import types


def _freeze(fn):
    if fn is None or fn.__closure__ is None:
        return fn
    cells = []
    for c in fn.__closure__:
        try:
            cells.append(types.CellType(c.cell_contents))
        except ValueError:
            cells.append(c)
    return types.FunctionType(fn.__code__, fn.__globals__, fn.__name__, fn.__defaults__, tuple(cells))


_DBG = {}


class Sem:
    def __init__(self, h):
        self.h = h


class TT:
    def __init__(self, name, ap, dsem=None):
        self.name = name
        self.ap = ap
        self.w = None
        self.r = {}
        self.dsem = dsem
        self.dcnt = 0

    def __getitem__(self, k):
        return self.ap[k]


class Prog:
    ENGS = ("pe", "act", "dve", "pool", "sp")

    def __init__(self, nc, es):
        self.nc = nc
        self.es = es
        self.streams = {k: [] for k in self.ENGS}
        self.sem = {k: Sem(es.enter_context(nc.semaphore("s_" + k))) for k in self.ENGS}
        self.cnt = {k: 0 for k in self.ENGS}
        self.seen = {k: {} for k in self.ENGS}
        self.out_tiles = []
        self.off = 16512
        self.n_alloc = 0

    def alloc(self, shape, dtype, at=None):
        nbytes = int(np.prod(shape[1:])) * (4 if dtype == F32 else 2)
        nbytes = (nbytes + 31) // 32 * 32
        if at is None:
            at = self.off
            self.off += nbytes
            assert self.off <= 229376, f"SBUF overflow {self.off}"
        self.n_alloc += 1
        h = self.nc.alloc_sbuf_tensor_at(f"t{self.n_alloc}", list(shape), dtype, offset=at)
        self.last_name = f"t{self.n_alloc}"
        return h, at, nbytes

    def tile(self, name, shape, dtype, dma=False, at=None):
        h, a, nb = self.alloc(shape, dtype, at)
        t = TT(name, h[tuple(slice(None) for _ in shape)])
        _DBG[name] = self.last_name
        t.addr = a
        t.nbytes = nb
        if dma:
            t.dsem = Sem(self.es.enter_context(self.nc.semaphore("d_" + name)))
        return t

    def dsem(self, name):
        return Sem(self.es.enter_context(self.nc.semaphore("d_" + name)))

    def _deps(self, eng, rd, wr):
        deps = {}

        def add(sv):
            s, v = sv
            if v > deps.get(s, 0):
                deps[s] = v

        for t in rd:
            if t.w:
                add(t.w)
        for t in wr:
            if t.w:
                add(t.w)
            for s, v in t.r.items():
                add((s, v))
        out = []
        seen = self.seen[eng]
        for s, v in deps.items():
            if eng == "pe" and s is self.sem["pe"]:
                continue
            if seen.get(s, 0) >= v:
                continue
            seen[s] = v
            out.append((s, v))
        return out

    def op(self, eng, fn, rd=(), wr=(), inc=True):
        if INCALL:
            inc = True
        fn = _freeze(fn)
        waits = self._deps(eng, rd, wr)
        s = self.sem[eng]
        v = self.cnt[eng] + 1
        if inc:
            self.cnt[eng] = v
        self.streams[eng].append((waits, fn, (s, 1) if inc else None))
        for t in wr:
            t.w = (s, v)
            t.r = {}
        for t in rd:
            if t in wr:
                continue
            if t.r.get(s, 0) < v:
                t.r[s] = v

    def dma(self, q, pairs, tile, load, rd=(), wr=()):
        if load:
            waits = self._deps(q, rd, [tile] + list(wr))
        else:
            waits = self._deps(q, [tile] + list(rd), wr)
        s = tile.dsem
        first = True
        for (o, i) in pairs:
            tile.dcnt += 1
            self.streams[q].append((waits if first else [], (lambda e, o=o, i=i: e.dma_start(out=o, in_=i)), (s, 16)))
            first = False
        v = 16 * tile.dcnt
        if load:
            tile.w = (s, v)
            tile.r = {}
        else:
            if tile.r.get(s, 0) < v:
                tile.r[s] = v
            if tile not in self.out_tiles:
                self.out_tiles.append(tile)

    def barrier(self, tiles=()):
        waits = [(self.sem[k], self.cnt[k]) for k in ("pe", "act", "dve", "pool") if self.cnt[k] > 0]
        for t in tiles:
            if t.dsem is not None and t.dcnt > 0:
                waits.append((t.dsem, 16 * t.dcnt))
        for k in ("pe", "act", "dve", "sp"):
            seen = self.seen[k]
            w = []
            for s_, v in waits:
                if s_ is self.sem.get(k):
                    continue
                if seen.get(s_, 0) >= v:
                    continue
                seen[s_] = v
                w.append((s_, v))
            if w:
                self.streams[k].append((w, None, None))

    def finish(self):
        waits = []
        for t in self.out_tiles:
            waits.append((t.dsem, 16 * t.dcnt))
        self.streams["sp"].append((waits, None, None))

    def simulate(self):
        vals = {}
        pc = {k: 0 for k in self.ENGS}
        names = {id(v): k for k, v in self.sem.items()}
        progress = True
        while progress:
            progress = False
            for k in self.ENGS:
                st = self.streams[k]
                while pc[k] < len(st):
                    waits, fn, inc = st[pc[k]]
                    if any(vals.get(id(s_), 0) < v for s_, v in waits):
                        break
                    if inc is not None:
                        vals[id(inc[0])] = vals.get(id(inc[0]), 0) + inc[1]
                    pc[k] += 1
                    progress = True
        stuck = {k: (pc[k], len(self.streams[k])) for k in self.ENGS if pc[k] < len(self.streams[k])}
        if stuck:
            msg = []
            for k, (p, n) in stuck.items():
                waits, fn, inc = self.streams[k][p]
                msg.append(f"{k} stuck at {p}/{n}: " + ", ".join(
                    f"{names.get(id(s_), 'dma')}>={v} (now {vals.get(id(s_), 0)})" for s_, v in waits if vals.get(id(s_), 0) < v))
            raise RuntimeError("DEADLOCK: " + " | ".join(msg))
        return {k: len(self.streams[k]) for k in self.ENGS}

    def replay(self, block):
        def run(name):
            def f(e):
                for waits, fn, inc in self.streams[name]:
                    for s, v in waits:
                        e.wait_ge(s.h, v)
                    if fn is None:
                        continue
                    ins = fn(e)
                    if inc is not None:
                        ins.then_inc(inc[0].h, inc[1])
            return f

        block.tensor(run("pe"))
        block.vector(run("dve"))
        block.scalar(run("act"))
        block.gpsimd(run("pool"))
        block.sync(run("sp"))


class _Stop(Exception):
    pass


import os
STAGE = float(os.environ.get("KSTAGE", "99"))
INCALL = os.environ.get("KINCALL", "0") == "1"


def ck(n):
    if STAGE <= n:
        raise _Stop()


def _body(nc, es, P):
    def din(name, shape):
        return nc.dram_tensor(name, list(shape), F32, kind="ExternalInput").ap()

    def dout(name, shape):
        return nc.dram_tensor(name, list(shape), F32, kind="ExternalOutput").ap()

    xin = din("xin", [NT, DM])
    cvec = din("cvec", [128, 16])
    w_mod = din("w_mod", [NL, DM, 6 * DM])
    bmod = din("bmod", [128, NL * 48])
    n1g = din("n1g", [128, NL * 8])
    n2g = din("n2g", [128, NL * 8])
    fng = din("fng", [128, 8])
    w_in = din("w_in", [NL, DM, INW])
    w_out = din("w_out", [NL, DM, DM])
    w1 = din("w1", [NL, DM, DFF])
    w2 = din("w2", [NL, DFF, DM])
    lbr = din("lbr", [128, 8 * NL])
    hgn = din("hgn", [128, NL])
    gqk = din("gqk", [NL, 128, 640])
    cosr = din("cosr", [8, 128, 640])
    sinr = din("sinr", [8, 128, 640])
    cachek = din("cachek", [NL, PAST, 128])
    cachev = din("cachev", [NL, PAST, 128])
    state0 = din("state0", [NL, 2, 4, 128, 128])
    identd = din("identd", [128, 128])
    maskf = din("maskf", [128, 128])
    maskb = din("maskb", [128, 128])
    rstd_ = din("rstd", [128, 1024])
    mh0d = din("mh0d", [128, 512])
    mh1d = din("mh1d", [128, 512])

    yout = dout("yout", [NT, DM])
    newk = dout("newk", [NSEQP, NL, SEQ, 128])
    newv = dout("newv", [NSEQP, NL, SEQ, 128])
    news = dout("news", [NSEQP, NL, 2, 4, 128, 128])

    x_all = P.tile("x_all", [128, 8, NT], F32)
    XG = [TT(f"x{g}", x_all[:, :, g * 512:(g + 1) * 512]) for g in range(4)]
    hbuf = P.tile("hbuf", [128, 2, 8, 1024], BF16)
    HB = [[TT(f"hb{h}{g}", hbuf[:, h, :, g * 512:(g + 1) * 512]) for g in range(2)] for h in range(2)]
    WS = [P.tile(f"ws{i}", [128, 8192], BF16, dma=True) for i in range(3)]

    CONST = TT("const", None, dsem=P.dsem("const"))

    def ctile(shape, dtype):
        h, _, _ = P.alloc(shape, dtype)
        return h[tuple(slice(None) for _ in shape)]

    idf = ctile([128, 128], F32)
    mf = ctile([128, 128], F32)
    mb = ctile([128, 128], F32)
    rst = ctile([128, 512], F32)
    c_sb = ctile([128, 16], F32)
    bmod_sb = ctile([128, NL, 48], F32)
    n1g_sb = ctile([128, NL, 8], F32)
    n2g_sb = ctile([128, NL, 8], F32)
    fng_sb = ctile([128, 8], F32)
    lbr_sb = ctile([128, 8, NL], F32)
    hgn_sb = ctile([128, NL], F32)
    CONSTB = TT("constb", None, dsem=P.dsem("constb"))
    MH0 = ctile([128, 512], BF16)
    MH1 = ctile([128, 512], BF16)
    SMALL = TT("small", None)
    idb = ctile([128, 128], BF16)
    ones_bf = ctile([128, 128], BF16)
    zeros_bf = ctile([128, 128], BF16)
    ones_f = ctile([128, 128], F32)
    scT = ctile([128, 8, 2], BF16)
    lb_sb = ctile([128, 8, NL], F32)
    oml_sb = ctile([128, 8, NL], F32)
    noml_sb = ctile([128, 8, NL], F32)
    lbe = ctile([128, 8, NL], F32)
    lbs = ctile([128, 8], F32)
    MODS = P.tile("mods", [128, NL, 48, 2], F32)
    AB = P.tile("ab", [128, NL, 2, 8, 2], F32)
    RSTD = P.tile("rstdt", [128, 512], F32)
    GQKL = P.tile("gqkl", [128, 640], F32, dma=True)
    TMPF = [P.tile(f"tmpf{i}", [128, 512], F32) for i in range(2)]
    SQ = [P.tile(f"sq{i}", [128, 512], BF16) for i in range(2)]

    WORK0 = P.off
    WORK_END = 229376

    PSF = []
    for i in range(8):
        h = es.enter_context(nc.psum_tensor(f"psf{i}", [128, 512], F32))
        PSF.append(TT(f"psf{i}", h[:, :]))
    PSB = []
    psf_i = [0]
    psb_i = [0]

    def psf():
        t = PSF[psf_i[0] % 6]
        psf_i[0] += 1
        return t

    def psacc():
        t = PSF[6 + psb_i[0] % 2]
        psb_i[0] += 1
        return t

    def psb():
        return psf()

    rot = {}

    def rr(lst, key):
        i = rot.get(key, 0)
        rot[key] = i + 1
        return lst[i % len(lst)]

    op = P.op

    ws_i = [0]

    def wslot():
        t = WS[ws_i[0] % 3]
        ws_i[0] += 1
        return t

    def w_pairs(key, slot):
        kind = key[0]
        if kind == "mod":
            _, l, piece = key
            v = slot.ap.rearrange("p (a b) -> p a b", b=1024)
            return [(v[:, kc, :], w_mod[l, kc * 128:(kc + 1) * 128, piece * 1024:(piece + 1) * 1024]) for kc in range(8)]
        if kind == "A":
            _, l, hf = key
            v = slot.ap[:, 0:8 * 768].rearrange("p (a b) -> p a b", b=768)
            return [(v[:, kc, :], w_in[l, kc * 128:(kc + 1) * 128, 2560:3328]) for kc in range(8)]
        if kind == "R":
            _, l, hf, hd = key
            v = slot.ap[:, 0:8 * 640].rearrange("p (a b) -> p a b", b=640)
            return [(v[:, kc, :], w_in[l, kc * 128:(kc + 1) * 128, hd * 640:(hd + 1) * 640]) for kc in range(8)]
        if kind == "O":
            _, l, hf = key
            v = slot.ap.rearrange("p (a b) -> p a b", b=1024)
            return [(v[:, kc, :], w_out[l, kc * 128:(kc + 1) * 128, :]) for kc in range(8)]
        if kind == "F":
            _, l, c = key
            v1 = slot.ap[:, 0:4096].rearrange("p (a b) -> p a b", b=512)
            v2 = slot.ap[:, 4096:8192].rearrange("p (a b) -> p a b", b=1024)
            pr = [(v1[:, kc, :], w1[l, kc * 128:(kc + 1) * 128, c * 512:(c + 1) * 512]) for kc in range(8)]
            pr += [(v2[:, f, :], w2[l, c * 512 + f * 128: c * 512 + (f + 1) * 128, :]) for f in range(4)]
            return pr
        raise ValueError(key)

    WPLAN = [("mod", 0, piece) for piece in range(6)]
    for l_ in range(NL):
        for hf_ in range(2):
            WPLAN.append(("A", l_, hf_))
            WPLAN += [("R", l_, hf_, hd_) for hd_ in range(4)]
            WPLAN.append(("O", l_, hf_))
        if l_ + 1 < NL:
            WPLAN += [("mod", l_ + 1, piece) for piece in range(6)]
        WPLAN += [("F", l_, c_) for c_ in range(8)]
    wstate = {"wp": 0, "issued": 0, "slots": {}}

    def acquire(key):
        wp = wstate["wp"]
        assert WPLAN[wp] == key, (WPLAN[wp], key)
        while wstate["issued"] <= min(wp + 1, len(WPLAN) - 1):
            k = wstate["issued"]
            slot = WS[k % 3]
            P.dma("pool", w_pairs(WPLAN[k], slot), slot, load=True)
            wstate["slots"][k] = slot
            wstate["issued"] = k + 1
        wstate["wp"] = wp + 1
        return wstate["slots"].pop(wp)

    P.dma("sp", [
        (idf, identd), (mf, maskf), (mb, maskb),
        (rst, rstd_[:, 0:512]),
        (c_sb, cvec), (bmod_sb, bmod.rearrange("p (a b) -> p a b", b=48)),
        (n1g_sb, n1g.rearrange("p (a b) -> p a b", b=8)), (n2g_sb, n2g.rearrange("p (a b) -> p a b", b=8)),
        (fng_sb, fng), (lbr_sb, lbr.rearrange("p (a b) -> p a b", b=NL)), (hgn_sb, hgn),
    ], CONST, load=True)

    P.dma("pool", [(MH0, mh0d), (MH1, mh1d)], CONSTB, load=True)
    op("dve", lambda e: e.memset(ones_bf, 1.0), wr=[SMALL])
    op("dve", lambda e: e.memset(zeros_bf, 0.0), wr=[SMALL])
    op("dve", lambda e: e.memset(ones_f, 1.0), wr=[SMALL])
    op("dve", lambda e: e.tensor_copy(out=idb, in_=idf), rd=[CONST], wr=[SMALL])
    op("act", lambda e: e.activation(out=scT.rearrange("p a b -> p (a b)"), in_=c_sb, func=AF.Silu), rd=[CONST], wr=[SMALL])
    op("act", lambda e: e.activation(out=lbe, in_=lbr_sb, func=AF.Exp), rd=[CONST], wr=[SMALL])
    op("dve", lambda e: e.tensor_reduce(out=lbs, in_=lbe, axis=AX.X, op=ALU.add), rd=[SMALL], wr=[SMALL])
    op("dve", lambda e: e.reciprocal(out=lbs, in_=lbs), rd=[SMALL], wr=[SMALL])
    op("dve", lambda e: e.tensor_tensor(out=lbe, in0=lbe, in1=lbs.unsqueeze(2).to_broadcast([128, 8, NL]), op=ALU.mult), rd=[SMALL], wr=[SMALL])
    op("dve", lambda e: e.memset(lb_sb[:, :, 0:1], 0.0), wr=[SMALL])
    op("dve", lambda e: e.tensor_copy(out=lb_sb[:, :, 1:2], in_=lbe[:, :, 1:2]), rd=[SMALL], wr=[SMALL])
    op("dve", lambda e: e.tensor_tensor(out=lb_sb[:, :, 2:3], in0=lb_sb[:, :, 1:2], in1=lbe[:, :, 2:3], op=ALU.add), rd=[SMALL], wr=[SMALL])
    op("dve", lambda e: e.tensor_tensor(out=lb_sb[:, :, 3:4], in0=lb_sb[:, :, 2:3], in1=lbe[:, :, 3:4], op=ALU.add), rd=[SMALL], wr=[SMALL])
    op("dve", lambda e: e.tensor_scalar(out=oml_sb, in0=lb_sb, scalar1=-1.0, scalar2=1.0, op0=ALU.mult, op1=ALU.add), rd=[SMALL], wr=[SMALL])
    op("dve", lambda e: e.tensor_scalar(out=noml_sb, in0=oml_sb, scalar1=-1.0, scalar2=None, op0=ALU.mult), rd=[SMALL], wr=[SMALL])

    ck(1)
    def emit_mods(l):
        ps = psf()
        for piece in range(6):
            slot = acquire(("mod", l, piece))
            wv = slot.ap.rearrange("p (a b) -> p a b", b=1024)
            for j in range(8):
                jj = piece * 8 + j
                for kc in range(8):
                    last = (kc == 7 and j == 7)
                    op("pe", lambda e, ps=ps, wv=wv, j=j, jj=jj, kc=kc: e.matmul(
                        ps[:, jj * 2:jj * 2 + 2], lhsT=wv[:, kc, j * 128:(j + 1) * 128], rhs=scT[:, kc, :],
                        start=(kc == 0), stop=(kc == 7)),
                       rd=[slot, SMALL], wr=[ps], inc=last)
        op("dve", lambda e, ps=ps, l=l: e.tensor_tensor(
            out=MODS[:, l, :, :], in0=ps[:, 0:96].rearrange("p (a b) -> p a b", b=2),
            in1=bmod_sb[:, l, :].unsqueeze(2).to_broadcast([128, 48, 2]), op=ALU.add), rd=[ps, CONST], wr=[MODS])
        for which, (jo, gsb) in enumerate(((8, n1g_sb), (32, n2g_sb))):
            op("dve", lambda e, l=l, which=which, jo=jo, gsb=gsb: e.scalar_tensor_tensor(
                out=AB[:, l, which, :, :], in0=MODS[:, l, jo:jo + 8, :], scalar=1.0,
                in1=gsb[:, l, :].unsqueeze(2).to_broadcast([128, 8, 2]), op0=ALU.add, op1=ALU.mult),
               rd=[MODS, CONST], wr=[AB])

    emit_mods(0)

    def modv(l, which, kc, i):
        return MODS[:, l, which * 8 + kc, i:i + 1]

    ck(2)
    XST = [P.tile(f"xst{i}", [128, 1024], F32, dma=True, at=WORK0 + i * 4096) for i in range(2)]
    for tt in range(16):
        st = XST[tt % 2]
        P.dma("sp", [(st.ap, xin[tt * 128:(tt + 1) * 128, :])], st, load=True)
        g = tt // 4
        for hh in range(2):
            ps = psf()
            for k4 in range(4):
                kc = hh * 4 + k4
                op("pe", lambda e, ps=ps, st=st, kc=kc, k4=k4: e.transpose(
                    out=ps[:, k4 * 128:(k4 + 1) * 128], in_=st[:, kc * 128:(kc + 1) * 128], identity=idf),
                   rd=[st, CONST], wr=[ps], inc=(k4 == 3))
            eng = "dve" if hh == 0 else "act"
            dst = x_all[:, hh * 4:(hh + 1) * 4, tt * 128:(tt + 1) * 128]
            if eng == "dve":
                op("dve", lambda e, ps=ps, dst=dst: e.tensor_copy(out=dst, in_=ps[:, :].rearrange("p (a b) -> p a b", b=128)),
                   rd=[ps], wr=[XG[g]])
            else:
                op("act", lambda e, ps=ps, dst=dst: e.activation(out=dst, in_=ps[:, :].rearrange("p (a b) -> p a b", b=128), func=AF.Copy),
                   rd=[ps], wr=[XG[g]])

    ck(3)
    def rmsnorm_to(g, l, which, dstT):
        i = g // 2
        xg = XG[g]
        ps = psf()
        for kc in range(8):
            sq = rr(SQ, "sq")
            op("act", lambda e, sq=sq, kc=kc: e.activation(out=sq.ap, in_=xg[:, kc, :], func=AF.Square), rd=[xg], wr=[sq])
            op("pe", lambda e, sq=sq, kc=kc, ps=ps: e.matmul(ps[:, :], lhsT=ones_bf, rhs=sq.ap, start=(kc == 0), stop=(kc == 7)),
               rd=[sq, SMALL], wr=[ps])
        op("act", lambda e, ps=ps: e.activation(out=RSTD.ap, in_=ps[:, :], func=AF.Ln, scale=1.0 / DM, bias=EPS), rd=[ps], wr=[RSTD])
        op("act", lambda e: e.activation(out=RSTD.ap, in_=RSTD.ap, func=AF.Exp, scale=-0.5), rd=[], wr=[RSTD])
        sh = 0 if which == 0 else 3
        for kc in range(8):
            tmp = rr(TMPF, "tmpf")
            op("dve" if kc % 2 == 0 else "pool", lambda e, tmp=tmp, kc=kc: e.tensor_tensor(out=tmp.ap, in0=xg[:, kc, :], in1=RSTD.ap, op=ALU.mult),
               rd=[xg, RSTD], wr=[tmp])
            op("act", lambda e, tmp=tmp, kc=kc: e.activation(
                out=dstT[:, kc, :], in_=tmp.ap, func=AF.Identity, scale=AB[:, l, which, kc, i:i + 1], bias=modv(l, sh, kc, i)),
               rd=[tmp, AB, MODS], wr=[dstT])

    W = WORK0
    a = [W]

    def wt(name, shape, dtype, dma=False):
        t = P.tile(name, shape, dtype, dma=dma, at=a[0])
        a[0] += t.nbytes
        assert a[0] <= WORK_END, f"work overflow {name} {a[0]}"
        return t

    QKF = [wt(f"qkf{i}", [128, 640], F32, dma=True) for i in range(2)]
    SQFS = [wt(f"sqf{i}", [128, 640], BF16) for i in range(2)]
    RT1 = wt("rt1", [128, 640], F32, dma=True)
    RT2 = wt("rt2", [128, 640], F32, dma=True)
    VF = [wt(f"vf{i}", [128, 128], F32, dma=True) for i in range(2)]
    QB = [wt(f"qb{i}", [128, 640], BF16) for i in range(2)]
    QT = wt("qT", [128, 8, 512], BF16)
    KT = wt("kT", [128, 1280], BF16)
    VA0 = wt("va0", [128, 10, 65], BF16)
    VA1 = wt("va1", [128, 10, 128], BF16)
    PT = [wt(f"pt{i}", [128, 512], BF16) for i in range(3)]
    RECS = [wt(f"rec{i}", [128, 512], F32) for i in range(2)]
    BC = wt("bc", [128, 512], F32)
    CK = wt("ck", [128, 2, 128], F32, dma=True)
    CV = wt("cv", [128, 2, 128], F32, dma=True)
    CKB = wt("ckb", [128, 2, 128], BF16)
    SSQS = [wt(f"ssq{i}", [128, 10], F32) for i in range(2)]
    att_end = a[0]
    a[0] = W
    Q32 = wt("q32", [128, 512], BF16)
    TS = [[wt(f"t{k}_{d}", [128, 512], F32) for k in range(4)] for d in range(2)]
    QM = wt("qm", [128, 512], BF16)
    QML = wt("qml", [128, 512], BF16)
    KME = wt("kme", [128, 512], BF16)
    KML = wt("kml", [128, 512], BF16)
    QTD = [wt(f"qtd{d}", [128, 1024], BF16) for d in range(2)]
    KDT = wt("kdt", [128, 512], BF16)
    OPB = [(QM, QML, KME, KML, KDT), (QM, QML, KME, KML, KDT)]
    KBS = [P.tile(f"kb{d}", [128, 512], BF16, at=TMPF[0].addr + d * 1024) for d in range(2)]
    KMS = [P.tile(f"km{d}", [128, 512], BF16, at=TMPF[1].addr + d * 1024) for d in range(2)]
    KDEC = [wt(f"kdec{d}", [128, 8, 128], BF16) for d in range(2)]
    GATE = wt("gate", [128, 1024], BF16)
    VT = wt("vt", [128, 8, 128], BF16)
    AMALL = [wt(f"amall{d}", [128, 8, 128], BF16) for d in range(2)]
    DEC = [wt(f"dec{d}", [128, 16], F32) for d in range(2)]
    S32 = [[wt(f"s32{d}{k}", [128, 128], F32, dma=True) for k in range(2)] for d in range(2)]
    SBC = [[wt(f"sbc{d}{k}", [128, 128], BF16) for k in range(2)] for d in range(2)]
    OSQ = wt("osq", [128, 512], BF16)
    ORS = TS[0][0]
    hg_end = a[0]
    a[0] = W
    UT = [wt(f"ut{i}", [128, 4, 512], BF16) for i in range(2)]
    RL = [wt(f"rl{i}", [128, 512], BF16) for i in range(2)]

    def init_va():
        op("dve", lambda e: e.memset(VA0.ap, 1.0), wr=[VA0])
        op("dve", lambda e: e.memset(VA1.ap, 0.0), wr=[VA1])
        op("dve", lambda e: e.memset(VA1[:, :, 0:1], 1.0), wr=[VA1])

    DMA_WORK = XST + QKF + VF + [CK, CV, RT1, RT2] + S32[0] + S32[1]
    for l in range(NL):
        P.dma("sp", [(GQKL.ap, gqk[l])], GQKL, load=True)
        for hf in range(2):
            P.barrier(DMA_WORK)
            i = hf
            sample = (hf == 1)
            hT = HB[0]
            om = HB[1]
            gs = [2 * hf, 2 * hf + 1]
            for gl in range(2):
                rmsnorm_to(gs[gl], l, 0, hT[gl])

            ck(3.1)
            slotA = acquire(("A", l, hf))
            wA = slotA.ap[:, 0:8 * 768].rearrange("p (a b) -> p a b", b=768)
            init_va()
            ctx_tiles = 2 if sample else 0
            if sample:
                P.dma("sp", [(CK.ap, cachek[l].rearrange("(a p) d -> p a d", p=128))], CK, load=True)
                P.dma("sp", [(CV.ap, cachev[l].rearrange("(a p) d -> p a d", p=128))], CV, load=True)
                op("act", lambda e: e.activation(out=CKB.ap, in_=CK.ap, func=AF.Copy), rd=[CK], wr=[CKB])
                pb = psb()
                for t2 in range(2):
                    op("pe", lambda e, pb=pb, t2=t2: e.matmul(pb[:, t2 * 128:(t2 + 1) * 128], lhsT=CKB[:, t2, :], rhs=idb, start=True, stop=True),
                       rd=[CKB, SMALL], wr=[pb], inc=(t2 == 1))
                op("dve", lambda e, pb=pb: e.tensor_copy(out=KT[:, 0:256], in_=pb[:, 0:256]), rd=[pb], wr=[KT])
                op("dve", lambda e: e.tensor_copy(out=VA0[:, 0:2, 0:64], in_=CV[:, :, 0:64]), rd=[CV], wr=[VA0])
                op("dve", lambda e: e.tensor_copy(out=VA1[:, 0:2, 64:128], in_=CV[:, :, 64:128]), rd=[CV], wr=[VA1])
            def stageA(tt):
                gl = tt // 4
                tsl = slice((tt % 4) * 128, (tt % 4 + 1) * 128)
                sqf = SQFS[tt % 2]
                ssq = SSQS[tt % 2]
                psq = psf()
                pskv = psf()
                for kc in range(8):
                    op("pe", lambda e, kc=kc, psq=psq, gl=gl, tsl=tsl: e.matmul(
                        psq[:, :], lhsT=hT[gl][:, kc, tsl], rhs=wA[:, kc, 0:512], start=(kc == 0), stop=(kc == 7)),
                       rd=[hT[gl], slotA], wr=[psq], inc=(kc == 7))
                for kc in range(8):
                    op("pe", lambda e, kc=kc, pskv=pskv, gl=gl, tsl=tsl: e.matmul(
                        pskv[:, 0:256], lhsT=hT[gl][:, kc, tsl], rhs=wA[:, kc, 512:768], start=(kc == 0), stop=(kc == 7)),
                       rd=[hT[gl], slotA], wr=[pskv], inc=(kc == 7))
                kti = ctx_tiles + tt
                op("act", lambda e, pskv=pskv, kti=kti: e.activation(out=VA0[:, kti, 0:64], in_=pskv[:, 128:192], func=AF.Copy), rd=[pskv], wr=[VA0])
                op("act", lambda e, pskv=pskv, kti=kti: e.activation(out=VA1[:, kti, 64:128], in_=pskv[:, 192:256], func=AF.Copy), rd=[pskv], wr=[VA1])
                if not sample:
                    vf = rr(VF, "vf")
                    op("act", lambda e, pskv=pskv, vf=vf: e.activation(out=vf.ap, in_=pskv[:, 128:256], func=AF.Copy), rd=[pskv], wr=[vf])
                    sq_i, t_i = tt // 2, (tt % 2) * 128
                    P.dma("sp", [(newv[sq_i, l, t_i:t_i + 128, :], vf.ap)], vf, load=False)
                op("act", lambda e, psq=psq: e.activation(out=sqf[:, 0:512], in_=psq[:, :], func=AF.Square), rd=[psq], wr=[sqf])
                op("act", lambda e, pskv=pskv: e.activation(out=sqf[:, 512:640], in_=pskv[:, 0:128], func=AF.Square), rd=[pskv], wr=[sqf])
                op("dve", lambda e: e.tensor_reduce(out=ssq.ap, in_=sqf.ap.rearrange("p (a b) -> p a b", b=64), axis=AX.X, op=ALU.add),
                   rd=[sqf], wr=[ssq])
                qkf = rr(QKF, "qkf")
                op("dve", lambda e, psq=psq, qkf=qkf: e.tensor_tensor(out=qkf[:, 0:512], in0=psq[:, :], in1=GQKL[:, 0:512], op=ALU.mult),
                   rd=[psq, GQKL], wr=[qkf])
                op("dve", lambda e, pskv=pskv, qkf=qkf: e.tensor_tensor(out=qkf[:, 512:640], in0=pskv[:, 0:128], in1=GQKL[:, 512:640], op=ALU.mult),
                   rd=[pskv, GQKL], wr=[qkf])
                return dict(qkf=qkf, kti=kti, ssq=ssq)

            def stageB(tt, st_):
                qkf = st_['qkf']
                kti = st_['kti']
                ssq = st_['ssq']
                op("act", lambda e: e.activation(out=ssq.ap, in_=ssq.ap, func=AF.Ln, scale=1.0 / 64, bias=EPS), rd=[ssq], wr=[ssq])
                op("act", lambda e: e.activation(out=ssq.ap, in_=ssq.ap, func=AF.Exp, scale=-0.5), rd=[ssq], wr=[ssq])
                op("dve", lambda e, qkf=qkf: e.tensor_tensor(
                    out=qkf.ap.rearrange("p (a b) -> p a b", b=64), in0=qkf.ap.rearrange("p (a b) -> p a b", b=64),
                    in1=ssq.ap.unsqueeze(2).to_broadcast([128, 10, 64]), op=ALU.mult), rd=[ssq], wr=[qkf])
                qb = rr(QB, "qb")
                if not sample:
                    sq_i, t_i = tt // 2, (tt % 2) * 128
                    P.dma("sp", [(newk[sq_i, l, t_i:t_i + 128, :], qkf[:, 512:640])], qkf, load=False)
                    op("act", lambda e, qkf=qkf, qb=qb: e.activation(
                        out=qb[:, 0:512].rearrange("p (g k d) -> p k g d", g=4, k=2),
                        in_=qkf[:, 0:512].rearrange("p (k g d) -> p k g d", k=2, g=4), func=AF.Copy), rd=[qkf], wr=[qb])
                    op("act", lambda e, qkf=qkf, qb=qb: e.activation(out=qb[:, 512:640], in_=qkf[:, 512:640], func=AF.Copy), rd=[qkf], wr=[qb])
                else:
                    q3 = qkf.ap.rearrange("p (a b) -> p a b", b=64)
                    r2 = RT2.ap.rearrange("p (a b) -> p a b", b=64)
                    P.dma("sp", [(RT1.ap, cosr[tt])], RT1, load=True)
                    P.dma("sp", [(RT2.ap, sinr[tt])], RT2, load=True)
                    op("dve", lambda e, qkf=qkf: e.tensor_tensor(out=RT1.ap, in0=RT1.ap, in1=qkf.ap, op=ALU.mult), rd=[qkf], wr=[RT1])
                    for ax in range(2):
                        lo = slice(ax * 32, ax * 32 + 16)
                        hi = slice(ax * 32 + 16, ax * 32 + 32)
                        op("dve", lambda e, q3=q3, r2=r2, lo=lo, hi=hi: e.tensor_tensor(out=r2[:, :, lo], in0=r2[:, :, lo], in1=q3[:, :, hi], op=ALU.mult),
                           rd=[qkf], wr=[RT2])
                        op("dve", lambda e, q3=q3, r2=r2, lo=lo, hi=hi: e.tensor_tensor(out=r2[:, :, hi], in0=r2[:, :, hi], in1=q3[:, :, lo], op=ALU.mult),
                           rd=[qkf], wr=[RT2])
                    op("dve", lambda e, qb=qb: e.tensor_tensor(
                        out=qb[:, 0:512].rearrange("p (g k d) -> p k g d", g=4, k=2),
                        in0=RT1[:, 0:512].rearrange("p (k g d) -> p k g d", k=2, g=4),
                        in1=RT2[:, 0:512].rearrange("p (k g d) -> p k g d", k=2, g=4), op=ALU.add), rd=[RT1, RT2], wr=[qb])
                    op("dve", lambda e, qb=qb: e.tensor_tensor(out=qb[:, 512:640], in0=RT1[:, 512:640], in1=RT2[:, 512:640], op=ALU.add),
                       rd=[RT1, RT2], wr=[qb])
                pb = psb()
                for g4 in range(4):
                    src = qb[:, g4 * 128:(g4 + 1) * 128]
                    op("pe", lambda e, pb=pb, src=src, g4=g4: e.matmul(pb[:, g4 * 128:(g4 + 1) * 128], lhsT=src, rhs=idb, start=True, stop=True),
                       rd=[qb, SMALL], wr=[pb], inc=(g4 == 3))
                pbk = psf()
                op("pe", lambda e, pbk=pbk, qb=qb: e.matmul(pbk[:, 0:128], lhsT=qb[:, 512:640], rhs=idb, start=True, stop=True),
                   rd=[qb, SMALL], wr=[pbk])
                op("dve", lambda e, pb=pb, tt=tt: e.tensor_copy(
                    out=QT[:, tt, :], in_=pb[:, 0:512]), rd=[pb], wr=[QT])
                op("act", lambda e, pbk=pbk, kti=kti: e.activation(out=KT[:, kti * 128:(kti + 1) * 128], in_=pbk[:, 0:128], func=AF.Copy),
                   rd=[pbk], wr=[KT])


            st_q = {}
            for tt in range(9):
                if tt < 8:
                    st_q[tt] = stageA(tt)
                if tt >= 1:
                    stageB(tt - 1, st_q.pop(tt - 1))

            ck(4)
            if sample:
                units = [(list(range(8)), list(range(10)))]
            else:
                units = [([2 * j, 2 * j + 1], [2 * j, 2 * j + 1]) for j in range(4)]
            blocks = []
            for (qtiles, ktiles) in units:
                for kv in range(2):
                    for qt in qtiles:
                        blocks.append((kv, qt, ktiles))
            tasks = [(bi, ki) for bi, b in enumerate(blocks) for ki in range(len(b[2]))]
            LA = 2
            ptq = {}
            psoq = {}
            fin2 = {}

            def emit_qk(j):
                bi, ki = tasks[j]
                kv, qt, ktiles = blocks[bi]
                kt = ktiles[ki]
                pr = slice(kv * 64, kv * 64 + 64)
                pss = psf()
                op("pe", lambda e, pss=pss, kt=kt, qt=qt, pr=pr: e.matmul(
                    pss[:, :], lhsT=KT[pr, kt * 128:(kt + 1) * 128], rhs=QT[pr, qt, :], start=True, stop=True),
                   rd=[KT, QT], wr=[pss])
                pt = rr(PT, "pt")
                op("act", lambda e, pss=pss, pt=pt: e.activation(out=pt.ap, in_=pss[:, :], func=AF.Exp, scale=0.125), rd=[pss], wr=[pt])
                ptq[j] = pt

            def emit_pv(j):
                bi, ki = tasks[j]
                kv, qt, ktiles = blocks[bi]
                kt = ktiles[ki]
                va = VA0 if kv == 0 else VA1
                M = 65 if kv == 0 else 128
                if ki == 0:
                    psoq[bi] = psacc()
                pso = psoq[bi]
                pt = ptq.pop(j)
                n = len(ktiles)
                op("pe", lambda e, pso=pso, va=va, kt=kt, pt=pt, M=M, ki=ki, n=n: e.matmul(
                    pso[0:M, :], lhsT=va[:, kt, 0:M], rhs=pt.ap, start=(ki == 0), stop=(ki == n - 1)),
                   rd=[va, pt], wr=[pso])
                if ki == n - 1:
                    srow = 64 if kv == 0 else 0
                    REC = RECS[bi % 2]
                    op("act", lambda e, pso=pso, srow=srow, REC=REC: e.activation(out=REC[srow:srow + 1, :], in_=pso[srow:srow + 1, :], func=AF.Ln), rd=[pso], wr=[REC])
                    op("act", lambda e, srow=srow, REC=REC: e.activation(out=REC[srow:srow + 1, :], in_=REC[srow:srow + 1, :], func=AF.Exp, scale=-1.0), rd=[], wr=[REC])
                    fin2.setdefault(j + 2, []).append(bi)

            def emit_fin2(bi):
                kv, qt, ktiles = blocks[bi]
                pr = slice(kv * 64, kv * 64 + 64)
                srow = 64 if kv == 0 else 0
                pso = psoq[bi]
                REC = RECS[bi % 2]
                psbc = psf()
                op("pe", lambda e, psbc=psbc, srow=srow, REC=REC: e.matmul(
                    psbc[:, :], lhsT=ones_f[srow:srow + 1, :], rhs=REC[srow:srow + 1, :], start=True, stop=True),
                   rd=[REC, SMALL], wr=[psbc])
                op("act", lambda e, psbc=psbc, pr=pr: e.activation(out=BC[pr, :], in_=psbc[pr, :], func=AF.Copy), rd=[psbc], wr=[BC])
                gl = qt // 4
                tsl = slice((qt % 4) * 128, (qt % 4 + 1) * 128)
                op("dve", lambda e, pso=pso, pr=pr, gl=gl, tsl=tsl: e.tensor_tensor(
                    out=om[gl][pr, 4:8, tsl], in0=pso[pr, :].rearrange("p (a b) -> p a b", b=128),
                    in1=BC[pr, :].rearrange("p (a b) -> p a b", b=128), op=ALU.mult), rd=[pso, BC], wr=[om[gl]])

            nt_ = len(tasks)
            for j in range(nt_ + LA + 3):
                if j < nt_:
                    emit_qk(j)
                jj = j - LA
                if 0 <= jj < nt_:
                    emit_pv(jj)
                for bi in fin2.pop(jj, []):
                    emit_fin2(bi)
            assert not fin2 and not ptq

            ck(5)
            P.barrier(DMA_WORK)
            seqs = [(0, 1024)] if sample else [(j * 256, 256) for j in range(4)]
            OACC = [PSF[6], PSF[7]]
            for hd in range(4):
                slotR = acquire(("R", l, hf, hd))
                wR = slotR.ap[:, 0:8 * 640].rearrange("p (a b) -> p a b", b=640)
                for gl in range(2):
                    gsl = slice(gl * 512, (gl + 1) * 512)
                    pps = {}
                    for ci in (0, 4, 1, 2):
                        ps = psf()
                        pps[ci] = ps
                        for kc in range(8):
                            op("pe", lambda e, ps=ps, kc=kc, ci=ci, gl=gl: e.matmul(
                                ps[:, :], lhsT=wR[:, kc, ci * 128:(ci + 1) * 128], rhs=hT[gl][:, kc, :], start=(kc == 0), stop=(kc == 7)),
                               rd=[slotR, hT[gl]], wr=[ps], inc=(kc == 7))
                        if ci == 0:
                            op("act", lambda e, ps=ps: e.activation(out=Q32.ap, in_=ps[:, :], func=AF.Copy), rd=[ps], wr=[Q32])
                        elif ci == 4:
                            GT = TS[1][3]
                            op("act", lambda e, ps=ps, GT=GT: e.activation(out=GT.ap, in_=ps[:, :], func=AF.Exp, scale=-1.0), rd=[ps], wr=[GT])
                            op("act", lambda e, GT=GT: e.activation(out=GT.ap, in_=GT.ap, func=AF.Ln, bias=1.0), rd=[], wr=[GT])
                            op("act", lambda e, GT=GT: e.activation(out=GT.ap, in_=GT.ap, func=AF.Exp, scale=-1.0), rd=[], wr=[GT])
                            op("dve", lambda e, ps=ps, gsl=gsl, GT=GT: e.tensor_tensor(out=GATE[:, gsl], in0=ps[:, :], in1=GT.ap, op=ALU.mult), rd=[ps, GT], wr=[GATE])

                    def dir_steps(d, ps, gl=gl, gsl=gsl, hd=hd, l=l):
                        T1, T2, T3, T4 = TS[d]
                        QM, QML, KME, KML, KDT = OPB[d]
                        KB = KBS[d]
                        KMb = KMS[d]
                        sc_oml = oml_sb[:, d * 4 + hd, l:l + 1]
                        sc_lb = lb_sb[:, d * 4 + hd, l:l + 1]
                        sc_noml = noml_sb[:, d * 4 + hd, l:l + 1]
                        last = 63 if d == 0 else 0
                        mid = 31 if d == 0 else 32
                        mE, mL = (MH0, MH1) if d == 0 else (MH1, MH0)
                        t2c = T2.ap.rearrange("p (c t) -> p c t", t=64)
                        t3c = T3.ap.rearrange("p (c t) -> p c t", t=64)
                        t4c = T4.ap.rearrange("p (c t) -> p c t", t=64)
                        msk = mf if d == 0 else mb
                        st = []
                        A = st.append
                        A(lambda: op("act", lambda e: e.activation(out=T1.ap, in_=ps[:, :], func=AF.Exp, scale=-1.0), rd=[ps], wr=[T1]))
                        A(lambda: op("act", lambda e: e.activation(out=T2.ap, in_=T1.ap, func=AF.Ln, bias=1.0), rd=[T1], wr=[T2]))
                        A(lambda: op("act", lambda e: e.activation(out=T4.ap, in_=T1.ap, func=AF.Ln, scale=sc_lb, bias=1.0), rd=[T1, SMALL], wr=[T4]))
                        A(lambda: op("pool", lambda e: e.tensor_tensor(out=T2.ap, in0=T4.ap, in1=T2.ap, op=ALU.subtract), rd=[T4], wr=[T2]))
                        A(lambda: op("act", lambda e: e.activation(out=T1.ap, in_=T2.ap, func=AF.Exp), rd=[T2], wr=[T1]))
                        A(lambda: op("dve", lambda e: e.tensor_tensor_scan(out=T3.ap, data0=rst, data1=T2.ap, initial=0.0, op0=ALU.mult, op1=ALU.add),
                                     rd=[T2, CONST], wr=[T3]))
                        if d == 1:
                            A(lambda: op("dve", lambda e: e.tensor_tensor(
                                out=t3c, in0=t3c[:, :, 63:64].to_broadcast([128, 8, 64]), in1=t3c, op=ALU.subtract), rd=[], wr=[T3]))
                            A(lambda: op("dve", lambda e: e.tensor_tensor(out=T3.ap, in0=T3.ap, in1=T2.ap, op=ALU.add), rd=[T2], wr=[T3]))
                        A(lambda: op("dve", lambda e: e.tensor_scalar(out=KB.ap, in0=T1.ap, scalar1=-1.0, scalar2=1.0, op0=ALU.mult, op1=ALU.add), rd=[T1], wr=[KB]))
                        A(lambda: op("act", lambda e: e.activation(out=T2.ap, in_=T3.ap, func=AF.Exp), rd=[T3], wr=[T2]))
                        A(lambda: op("pool", lambda e: e.tensor_tensor(out=QTD[d][:, gsl], in0=Q32.ap, in1=T2.ap, op=ALU.mult), rd=[Q32, T2], wr=[QTD[d]]))
                        A(lambda: op("pool", lambda e: e.tensor_copy(out=DEC[d][:, gl * 8:(gl + 1) * 8], in_=t2c[:, :, last]), rd=[T2], wr=[DEC[d]]))
                        A(lambda: op("dve", lambda e: e.tensor_tensor(
                            out=t4c, in0=t3c[:, :, last:last + 1].to_broadcast([128, 8, 64]), in1=t3c, op=ALU.subtract), rd=[T3], wr=[T4]))
                        A(lambda: op("act", lambda e: e.activation(out=T4.ap, in_=T4.ap, func=AF.Exp), rd=[], wr=[T4]))
                        A(lambda: op("dve", lambda e: e.tensor_tensor(out=KDT.ap, in0=T4.ap, in1=KB.ap, op=ALU.mult), rd=[KB, T4], wr=[KDT]))

                        def kdt_tr():
                            pb = psf()
                            for t4 in range(4):
                                op("pe", lambda e, pb=pb, t4=t4: e.matmul(pb[:, t4 * 128:(t4 + 1) * 128], lhsT=KDT[:, t4 * 128:(t4 + 1) * 128], rhs=idb, start=True, stop=True),
                                   rd=[KDT, SMALL], wr=[pb], inc=(t4 == 3))
                            op("act", lambda e, pb=pb: e.activation(
                                out=KDEC[d][:, gl * 4:(gl + 1) * 4, :], in_=pb[:, 0:512].rearrange("p (a b) -> p a b", b=128), func=AF.Copy), rd=[pb], wr=[KDEC[d]])
                        A(kdt_tr)
                        A(lambda: op("dve", lambda e: e.tensor_tensor(
                            out=t4c, in0=t3c, in1=t3c[:, :, mid:mid + 1].to_broadcast([128, 8, 64]), op=ALU.subtract), rd=[T3], wr=[T4]))
                        A(lambda: op("act", lambda e: e.activation(out=T2.ap, in_=T4.ap, func=AF.Exp), rd=[T4], wr=[T2]))
                        A(lambda: op("act", lambda e: e.activation(out=T4.ap, in_=T4.ap, func=AF.Exp, scale=-1.0), rd=[], wr=[T4]))
                        A(lambda: op("dve", lambda e: e.tensor_tensor(out=QM.ap, in0=Q32.ap, in1=T2.ap, op=ALU.mult), rd=[Q32, T2], wr=[QM]))
                        A(lambda: op("pool", lambda e: e.tensor_tensor(out=QML.ap, in0=QM.ap, in1=mL, op=ALU.mult), rd=[QM, CONSTB], wr=[QML]))
                        A(lambda: op("dve", lambda e: e.tensor_tensor(out=KMb.ap, in0=T4.ap, in1=KB.ap, op=ALU.mult), rd=[T4, KB], wr=[KMb]))
                        A(lambda: op("pool", lambda e: e.tensor_tensor(out=KME.ap, in0=KMb.ap, in1=mE, op=ALU.mult), rd=[KMb, CONSTB], wr=[KME]))
                        A(lambda: op("dve", lambda e: e.tensor_tensor(out=KML.ap, in0=KMb.ap, in1=mL, op=ALU.mult), rd=[KMb, CONSTB], wr=[KML]))

                        def amat():
                            psa = psf()
                            for t4 in range(4):
                                tsl = slice(t4 * 128, (t4 + 1) * 128)
                                op("pe", lambda e, psa=psa, tsl=tsl: e.matmul(psa[:, tsl], lhsT=KME[:, tsl], rhs=QM[:, tsl], start=True, stop=False, skip_group_check=True),
                                   rd=[KME, QM], wr=[psa], inc=False)
                                op("pe", lambda e, psa=psa, tsl=tsl: e.matmul(psa[:, tsl], lhsT=KML[:, tsl], rhs=QML[:, tsl], start=False, stop=True, skip_group_check=True),
                                   rd=[KML, QML], wr=[psa], inc=(t4 == 3))
                            for t4 in range(4):
                                op("dve", lambda e, psa=psa, t4=t4: e.tensor_tensor(
                                    out=AMALL[d][:, gl * 4 + t4, :], in0=psa[:, t4 * 128:(t4 + 1) * 128], in1=msk, op=ALU.mult),
                                   rd=[psa, CONST], wr=[AMALL[d]])
                        A(amat)
                        return st

                    sts = [dir_steps(0, pps[1]), dir_steps(1, pps[2])]
                    LAG = 3
                    for k_ in range(max(len(sts[0]), len(sts[1]) + LAG)):
                        if k_ < len(sts[0]):
                            sts[0][k_]()
                        if 0 <= k_ - LAG < len(sts[1]):
                            sts[1][k_ - LAG]()
                    psv = psf()
                    for t4 in range(4):
                        for kc in range(8):
                            op("pe", lambda e, psv=psv, t4=t4, kc=kc, gl=gl: e.matmul(
                                psv[:, t4 * 128:(t4 + 1) * 128], lhsT=hT[gl][:, kc, t4 * 128:(t4 + 1) * 128], rhs=wR[:, kc, 384:512],
                                start=(kc == 0), stop=(kc == 7)), rd=[slotR, hT[gl]], wr=[psv], inc=(kc == 7 and t4 == 3))
                    op("act", lambda e, psv=psv, gl=gl: e.activation(
                        out=VT[:, gl * 4:(gl + 1) * 4, :], in_=psv[:, :].rearrange("p (a b) -> p a b", b=128), func=AF.Copy), rd=[psv], wr=[VT])

                ck(5.1)
                for acc in OACC:
                    op("pe", lambda e, acc=acc: e.matmul(acc[:, :], lhsT=zeros_bf, rhs=MH0, start=True, stop=True, skip_group_check=True),
                       rd=[SMALL, CONSTB], wr=[acc])
                for ti in range(8):
                    acc = OACC[ti // 4]
                    osl = slice((ti % 4) * 128, (ti % 4 + 1) * 128)
                    for d in range(2):
                        op("pe", lambda e, acc=acc, osl=osl, ti=ti, d=d: e.matmul(
                            acc[:, osl], lhsT=VT[:, ti, :], rhs=AMALL[d][:, ti, :], start=False, stop=True, skip_group_check=True),
                           rd=[VT, AMALL[d]], wr=[acc], inc=(d == 1))
                ck(5.2)
                for p0 in range(0, len(seqs), 2):
                    grp = seqs[p0:p0 + 2]
                    cur = {}
                    for k, (s0, slen) in enumerate(grp):
                        for d in range(2):
                            if sample:
                                P.dma("sp", [(S32[d][k].ap, state0[l, d, hd])], S32[d][k], load=True)
                            else:
                                op("dve", lambda e, d=d, k=k: e.memset(S32[d][k].ap, 0.0), wr=[S32[d][k]])
                            op("act", lambda e, d=d, k=k: e.activation(out=SBC[d][k].ap, in_=S32[d][k].ap, func=AF.Copy), rd=[S32[d][k]], wr=[SBC[d][k]])
                    nch = grp[0][1] // 64
                    for step in range(nch):
                        for k, (s0, slen) in enumerate(grp):
                            c0 = s0 // 64
                            for d in range(2):
                                c = c0 + (step if d == 0 else nch - 1 - step)
                                tile_i, half_i = c // 2, c % 2
                                prt = slice(half_i * 64, half_i * 64 + 64)
                                acc = OACC[tile_i // 4]
                                ocol = (tile_i % 4) * 128 + half_i * 64
                                op("pe", lambda e, acc=acc, ocol=ocol, d=d, k=k, c=c: e.matmul(
                                    acc[:, ocol:ocol + 64], lhsT=SBC[d][k].ap, rhs=QTD[d][:, c * 64:(c + 1) * 64], start=False, stop=True, skip_group_check=True),
                                   rd=[SBC[d][k], QTD[d]], wr=[acc])
                                pkv = psf()
                                op("pe", lambda e, pkv=pkv, d=d, tile_i=tile_i, prt=prt: e.matmul(
                                    pkv[:, 0:128], lhsT=KDEC[d][prt, tile_i, :], rhs=VT[prt, tile_i, :], start=True, stop=True),
                                   rd=[KDEC[d], VT], wr=[pkv])
                                op("dve", lambda e, pkv=pkv, d=d, k=k, c=c: e.scalar_tensor_tensor(
                                    out=S32[d][k].ap, in0=S32[d][k].ap, scalar=DEC[d][:, c:c + 1], in1=pkv[:, 0:128], op0=ALU.mult, op1=ALU.add),
                                   rd=[pkv, DEC[d]], wr=[S32[d][k]])
                                if step < nch - 1:
                                    op("dve", lambda e, d=d, k=k: e.tensor_copy(out=SBC[d][k].ap, in_=S32[d][k].ap), rd=[S32[d][k]], wr=[SBC[d][k]])
                    if not sample:
                        for k in range(len(grp)):
                            for d in range(2):
                                P.dma("sp", [(news[p0 + k, l, d, hd], S32[d][k].ap)], S32[d][k], load=False)
                ck(5.3)
                for gl in range(2):
                    acc = OACC[gl]
                    hsl = slice(gl * 512, (gl + 1) * 512)
                    op("act", lambda e, acc=acc: e.activation(out=OSQ.ap, in_=acc[:, :], func=AF.Square), rd=[acc], wr=[OSQ])
                    pss = psf()
                    op("pe", lambda e, pss=pss: e.matmul(pss[:, :], lhsT=ones_bf, rhs=OSQ.ap, start=True, stop=True), rd=[OSQ, SMALL], wr=[pss])
                    op("act", lambda e, pss=pss: e.activation(out=ORS.ap, in_=pss[:, :], func=AF.Ln, scale=1.0 / 128, bias=EPS), rd=[pss], wr=[ORS])
                    op("act", lambda e: e.activation(out=ORS.ap, in_=ORS.ap, func=AF.Exp, scale=-0.5), rd=[], wr=[ORS])
                    op("dve", lambda e, hsl=hsl: e.tensor_tensor(out=ORS.ap, in0=ORS.ap, in1=GATE[:, hsl], op=ALU.mult), rd=[GATE], wr=[ORS])
                    op("dve", lambda e, acc=acc, gl=gl, hd=hd: e.scalar_tensor_tensor(
                        out=om[gl][:, hd, :], in0=acc[:, :], scalar=hgn_sb[:, l:l + 1], in1=ORS.ap, op0=ALU.mult, op1=ALU.mult),
                       rd=[acc, ORS, CONST], wr=[om[gl]])

            ck(6)
            slotO = acquire(("O", l, hf))
            wO = slotO.ap.rearrange("p (a b) -> p a b", b=1024)
            for gl in range(2):
                g = gs[gl]
                for dc in range(8):
                    ps = psf()
                    for kc in range(8):
                        op("pe", lambda e, ps=ps, kc=kc, dc=dc, gl=gl: e.matmul(
                            ps[:, :], lhsT=wO[:, kc, dc * 128:(dc + 1) * 128], rhs=om[gl][:, kc, :], start=(kc == 0), stop=(kc == 7)),
                           rd=[slotO, om[gl]], wr=[ps], inc=(kc == 7))
                    op("dve", lambda e, ps=ps, dc=dc, g=g: e.scalar_tensor_tensor(
                        out=XG[g][:, dc, :], in0=ps[:, :], scalar=modv(l, 2, dc, i), in1=XG[g][:, dc, :], op0=ALU.mult, op1=ALU.add),
                       rd=[ps, MODS], wr=[XG[g]])

        ck(8)
        P.barrier(DMA_WORK)
        if l + 1 < NL:
            emit_mods(l + 1)
        def ffn_u(c, g, slotF):
            w1v = slotF.ap[:, 0:4096].rearrange("p (a b) -> p a b", b=512)
            h2 = HB[g // 2][g % 2]
            ut = rr(UT, "ut")
            for f in range(4):
                ps = psf()
                for kc in range(8):
                    op("pe", lambda e, ps=ps, kc=kc, f=f, h2=h2: e.matmul(
                        ps[:, :], lhsT=w1v[:, kc, f * 128:(f + 1) * 128], rhs=h2[:, kc, :], start=(kc == 0), stop=(kc == 7)),
                       rd=[slotF, h2], wr=[ps], inc=(kc == 7))
                rl = rr(RL, "rl")
                op("act", lambda e, ps=ps, rl=rl: e.activation(out=rl.ap, in_=ps[:, :], func=AF.Relu), rd=[ps], wr=[rl])
                op("dve", lambda e, rl=rl, ut=ut, f=f: e.tensor_tensor(out=ut[:, f, :], in0=rl.ap, in1=rl.ap, op=ALU.mult), rd=[rl], wr=[ut])
            return ut

        def ffn_y(c, g, slotF, ut):
            w2v = slotF.ap[:, 4096:8192].rearrange("p (a b) -> p a b", b=1024)
            i = g // 2
            for dc in range(8):
                ps = psf()
                for f in range(4):
                    op("pe", lambda e, ps=ps, f=f, dc=dc, ut=ut: e.matmul(
                        ps[:, :], lhsT=w2v[:, f, dc * 128:(dc + 1) * 128], rhs=ut[:, f, :], start=(f == 0), stop=(f == 3)),
                       rd=[slotF, ut], wr=[ps], inc=(f == 3))
                op("dve", lambda e, ps=ps, dc=dc, g=g, i=i: e.scalar_tensor_tensor(
                    out=XG[g][:, dc, :], in0=ps[:, :], scalar=modv(l, 5, dc, i), in1=XG[g][:, dc, :], op0=ALU.mult, op1=ALU.add),
                   rd=[ps, MODS], wr=[XG[g]])

        pend = None
        slotF = acquire(("F", l, 0))
        for g in range(4):
            rmsnorm_to(g, l, 1, HB[g // 2][g % 2])
            ut = ffn_u(0, g, slotF)
            if pend is not None:
                ffn_y(*pend)
            pend = (0, g, slotF, ut)
        for c in range(1, 8):
            slotF = acquire(("F", l, c))
            for g in range(4):
                ut = ffn_u(c, g, slotF)
                ffn_y(*pend)
                pend = (c, g, slotF, ut)
        ffn_y(*pend)

    ck(10)
    P.barrier(DMA_WORK)
    for g in range(4):
        xg = XG[g]
        ps = psf()
        for kc in range(8):
            sq = rr(SQ, "sq")
            op("act", lambda e, sq=sq, kc=kc, xg=xg: e.activation(out=sq.ap, in_=xg[:, kc, :], func=AF.Square), rd=[xg], wr=[sq])
            op("pe", lambda e, sq=sq, kc=kc, ps=ps: e.matmul(ps[:, :], lhsT=ones_bf, rhs=sq.ap, start=(kc == 0), stop=(kc == 7)),
               rd=[sq, SMALL], wr=[ps])
        op("act", lambda e, ps=ps: e.activation(out=RSTD.ap, in_=ps[:, :], func=AF.Ln, scale=1.0 / DM, bias=EPS), rd=[ps], wr=[RSTD])
        op("act", lambda e: e.activation(out=RSTD.ap, in_=RSTD.ap, func=AF.Exp, scale=-0.5), rd=[], wr=[RSTD])
        for kc in range(8):
            op("dve", lambda e, kc=kc, xg=xg: e.scalar_tensor_tensor(
                out=xg[:, kc, :], in0=xg[:, kc, :], scalar=fng_sb[:, kc:kc + 1], in1=RSTD.ap, op0=ALU.mult, op1=ALU.mult),
               rd=[RSTD, CONST], wr=[xg])
        for t4 in range(4):
            tt = g * 4 + t4
            st = XST[tt % 2]
            for hh in range(2):
                ps2 = psf()
                for k4 in range(4):
                    kc = hh * 4 + k4
                    op("pe", lambda e, ps2=ps2, kc=kc, k4=k4, t4=t4, xg=xg: e.transpose(
                        out=ps2[:, k4 * 128:(k4 + 1) * 128], in_=xg[:, kc, t4 * 128:(t4 + 1) * 128], identity=idf),
                       rd=[xg, CONST], wr=[ps2], inc=(k4 == 3))
                if hh == 0:
                    op("dve", lambda e, ps2=ps2, st=st: e.tensor_copy(out=st[:, 0:512], in_=ps2[:, :]), rd=[ps2], wr=[st])
                else:
                    op("act", lambda e, ps2=ps2, st=st: e.activation(out=st[:, 512:1024], in_=ps2[:, :], func=AF.Copy), rd=[ps2], wr=[st])
            P.dma("sp", [(yout[tt * 128:(tt + 1) * 128, :], st.ap)], st, load=False)

def _tail(nc, es, P):
    P.finish()
    print("stream sizes", P.simulate(), "cnt", P.cnt)
    with nc.Block() as block:
        P.replay(block)
    es.close()
    return nc


def build_program():
    nc = bass.Bass("TRN2", target_bir_lowering=False)
    es = contextlib.ExitStack()
    P = Prog(nc, es)
    try:
        _body(nc, es, P)
    except _Stop:
        pass
    return _tail(nc, es, P)


_CACHE = {}


def _get_prog():
    if "nc" not in _CACHE:
        _CACHE["nc"] = build_program()
    return _CACHE["nc"]


def _rope_tables():
    T = DEC_SEQ
    GRID_W = 64
    rows = T // GRID_W
    row = np.repeat(np.arange(rows, dtype=np.float32), GRID_W)
    col = np.tile(np.arange(GRID_W, dtype=np.float32), rows)
    inv = (10000.0 ** (-np.arange(0, 32, 2, dtype=np.float32) / 32)).astype(np.float32)
    ar = row[:, None] * inv[None, :]
    ac = col[:, None] * inv[None, :]
    ang = np.concatenate([ar, ar, ac, ac], axis=-1).astype(np.float32)
    cos = np.cos(ang).astype(np.float32)
    sin = np.sin(ang).astype(np.float32)
    sgn = np.ones(64, np.float32)
    sgn[0:16] = -1.0
    sgn[32:48] = -1.0
    sins = sin * sgn[None, :]
    cosT = np.tile(cos.reshape(8, 128, 64), (1, 1, 10))
    sinT = np.tile(sins.reshape(8, 128, 64), (1, 1, 10))
    return np.ascontiguousarray(cosT), np.ascontiguousarray(sinT)


def kernel(x_prompt, x_sample, cache_k, cache_v, state_hgrn, c, c_ctx, w_mod, b_mod, norm1_g, w_in, lb_raw,
           hgrn_norm_g, q_norm_g, k_norm_g, w_out, norm2_g, w1, w2, final_norm_g):
    f = lambda a: np.ascontiguousarray(np.asarray(a, dtype=np.float32))
    x_prompt, x_sample, cache_k, cache_v, state_hgrn = map(f, (x_prompt, x_sample, cache_k, cache_v, state_hgrn))
    c, c_ctx, w_mod, b_mod, norm1_g, w_in, lb_raw = map(f, (c, c_ctx, w_mod, b_mod, norm1_g, w_in, lb_raw))
    hgrn_norm_g, q_norm_g, k_norm_g, w_out, norm2_g, w1, w2, final_norm_g = map(
        f, (hgrn_norm_g, q_norm_g, k_norm_g, w_out, norm2_g, w1, w2, final_norm_g))

    colperm = []
    for hd in range(4):
        for blk in (0, 512, 1024, 1536, 2048):
            colperm += list(range(blk + hd * 128, blk + (hd + 1) * 128))
    colperm += list(range(2560, 3328))
    w_in_p = np.ascontiguousarray(w_in[:, :, colperm])
    rowperm = list(range(512))
    for g in range(4):
        rowperm += list(range(512 + g * 64, 512 + (g + 1) * 64))
        rowperm += list(range(512 + (4 + g) * 64, 512 + (5 + g) * 64))
    w_out_p = np.ascontiguousarray(w_out[:, rowperm, :])

    def fm(v):
        return v.reshape(v.shape[:-1] + (8, 128))

    bmod_l = np.ascontiguousarray(b_mod.reshape(NL, 48, 128).transpose(2, 0, 1).reshape(128, NL * 48))
    n1g_l = np.ascontiguousarray(norm1_g.reshape(NL, 8, 128).transpose(2, 0, 1).reshape(128, NL * 8))
    n2g_l = np.ascontiguousarray(norm2_g.reshape(NL, 8, 128).transpose(2, 0, 1).reshape(128, NL * 8))
    fng_l = np.ascontiguousarray(final_norm_g.reshape(8, 128).T)
    lbr_l = np.ascontiguousarray(lb_raw.reshape(NL, 2, 4, 128).transpose(3, 1, 2, 0).reshape(128, 8 * NL))
    hgn_l = np.ascontiguousarray(hgrn_norm_g.T)
    gqk_l = np.concatenate([np.tile(q_norm_g, (1, 8)), np.tile(k_norm_g, (1, 2))], axis=1)
    gqk_l = np.ascontiguousarray(np.broadcast_to(gqk_l[:, None, :], (NL, 128, 640)))
    cosT, sinT = _rope_tables()
    ident = np.eye(128, dtype=np.float32)
    s_idx = np.arange(128)[:, None]
    t_idx = np.arange(128)[None, :]
    same = (s_idx // 64) == (t_idx // 64)
    maskf = (same & (s_idx <= t_idx)).astype(np.float32)
    maskb = (same & (s_idx >= t_idx)).astype(np.float32)
    mh0 = np.ascontiguousarray(np.broadcast_to(((np.arange(512) % 64) < 32).astype(np.float32)[None, :], (128, 512)))
    mh1 = np.ascontiguousarray(1.0 - mh0)
    rstm = np.ones((128, 1024), np.float32)
    rstm[:, ::64] = 0.0

    in_maps = []
    for core in range(NCORES):
        b = core % 4
        xin = np.concatenate([x_prompt[core * 4:(core + 1) * 4].reshape(1024, DM), x_sample[b]], axis=0)
        cv = np.stack([c_ctx.reshape(8, 128), c[b].reshape(8, 128)], axis=-1)
        cv = np.ascontiguousarray(cv.transpose(1, 0, 2).reshape(128, 16))
        in_maps.append({
            "xin": np.ascontiguousarray(xin), "cvec": cv, "w_mod": w_mod, "bmod": bmod_l, "n1g": n1g_l, "n2g": n2g_l,
            "fng": fng_l, "w_in": w_in_p, "w_out": w_out_p, "w1": w1, "w2": w2, "lbr": lbr_l, "hgn": hgn_l,
            "gqk": gqk_l,
            "cachek": np.ascontiguousarray(cache_k[b].reshape(NL, PAST, 128)),
            "cachev": np.ascontiguousarray(cache_v[b].reshape(NL, PAST, 128)),
            "state0": np.ascontiguousarray(state_hgrn[b]),
            "cosr": cosT, "sinr": sinT, "identd": ident, "maskf": maskf, "maskb": maskb, "rstd": rstm, "mh0d": mh0, "mh1d": mh1,
        })
    nc = _get_prog()
    res = run_bass_kernel_spmd(nc, in_maps, core_ids=list(range(NCORES)))
    R = res.results
    y_prompt = np.concatenate([R[cix]["yout"][:1024].reshape(4, SEQ, DM) for cix in range(NCORES)], axis=0)
    y_sample = np.stack([R[cix]["yout"][1024:] for cix in range(4)], axis=0)
    new_k = np.concatenate([R[cix]["newk"].reshape(4, NL, SEQ, 2, 64) for cix in range(NCORES)], axis=0)
    new_v = np.concatenate([R[cix]["newv"].reshape(4, NL, SEQ, 2, 64) for cix in range(NCORES)], axis=0)
    new_s = np.concatenate([R[cix]["news"] for cix in range(NCORES)], axis=0)
    return (y_prompt.astype(np.float32), y_sample.astype(np.float32), new_k.astype(np.float32),
            new_v.astype(np.float32), new_s.astype(np.float32))
```

```python
import contextlib
import numpy as np
import concourse.bass as bass
import concourse.mybir as mybir
from concourse.bass_utils import run_bass_kernel_spmd

F32 = mybir.dt.float32
BF16 = mybir.dt.bfloat16
AF = mybir.ActivationFunctionType
ALU = mybir.AluOpType
AX = mybir.AxisListType

NCORES = 8
DM = 1024
DEPTH = 4
NSEQP = 4
SEQ = 256
DEC_SEQ = 1024
PAST = 256
NT = 2048
INW = 3328
DFF = 4096
EPS = 1e-6
NL = DEPTH


import types


def _freeze(fn):
    if fn is None or fn.__closure__ is None:
        return fn
    cells = []
    for c in fn.__closure__:
        try:
            cells.append(types.CellType(c.cell_contents))
        except ValueError:
            cells.append(c)
    return types.FunctionType(fn.__code__, fn.__globals__, fn.__name__, fn.__defaults__, tuple(cells))


_DBG = {}


class Sem:
    def __init__(self, h):
        self.h = h


class TT:
    def __init__(self, name, ap, dsem=None):
        self.name = name
        self.ap = ap
        self.w = None
        self.r = {}
        self.dsem = dsem
        self.dcnt = 0

    def __getitem__(self, k):
        return self.ap[k]


class Prog:
    ENGS = ("pe", "act", "dve", "pool", "sp")

    def __init__(self, nc, es):
        self.nc = nc
        self.es = es
        self.streams = {k: [] for k in self.ENGS}
        self.sem = {k: Sem(es.enter_context(nc.semaphore("s_" + k))) for k in self.ENGS}
        self.cnt = {k: 0 for k in self.ENGS}
        self.seen = {k: {} for k in self.ENGS}
        self.out_tiles = []
        self.off = 16512
        self.n_alloc = 0

    def alloc(self, shape, dtype, at=None):
        nbytes = int(np.prod(shape[1:])) * (4 if dtype == F32 else 2)
        nbytes = (nbytes + 31) // 32 * 32
        if at is None:
            at = self.off
            self.off += nbytes
            assert self.off <= 229376, f"SBUF overflow {self.off}"
        self.n_alloc += 1
        h = self.nc.alloc_sbuf_tensor_at(f"t{self.n_alloc}", list(shape), dtype, offset=at)
        self.last_name = f"t{self.n_alloc}"
        return h, at, nbytes

    def tile(self, name, shape, dtype, dma=False, at=None):
        h, a, nb = self.alloc(shape, dtype, at)
        t = TT(name, h[tuple(slice(None) for _ in shape)])
        _DBG[name] = self.last_name
        t.addr = a
        t.nbytes = nb
        if dma:
            t.dsem = Sem(self.es.enter_context(self.nc.semaphore("d_" + name)))
        return t

    def dsem(self, name):
        return Sem(self.es.enter_context(self.nc.semaphore("d_" + name)))

    def _deps(self, eng, rd, wr):
        deps = {}

        def add(sv):
            s, v = sv
            if v > deps.get(s, 0):
                deps[s] = v

        for t in rd:
            if t.w:
                add(t.w)
        for t in wr:
            if t.w:
                add(t.w)
            for s, v in t.r.items():
                add((s, v))
        out = []
        seen = self.seen[eng]
        for s, v in deps.items():
            if eng == "pe" and s is self.sem["pe"]:
                continue
            if seen.get(s, 0) >= v:
                continue
            seen[s] = v
            out.append((s, v))
        return out

    def op(self, eng, fn, rd=(), wr=(), inc=True):
        if INCALL:
            inc = True
        fn = _freeze(fn)
        waits = self._deps(eng, rd, wr)
        s = self.sem[eng]
        v = self.cnt[eng] + 1
        if inc:
            self.cnt[eng] = v
        self.streams[eng].append((waits, fn, (s, 1) if inc else None))
        for t in wr:
            t.w = (s, v)
            t.r = {}
        for t in rd:
            if t in wr:
                continue
            if t.r.get(s, 0) < v:
                t.r[s] = v

    def dma(self, q, pairs, tile, load, rd=(), wr=()):
        if load:
            waits = self._deps(q, rd, [tile] + list(wr))
        else:
            waits = self._deps(q, [tile] + list(rd), wr)
        s = tile.dsem
        first = True
        for (o, i) in pairs:
            tile.dcnt += 1
            self.streams[q].append((waits if first else [], (lambda e, o=o, i=i: e.dma_start(out=o, in_=i)), (s, 16)))
            first = False
        v = 16 * tile.dcnt
        if load:
            tile.w = (s, v)
            tile.r = {}
        else:
            if tile.r.get(s, 0) < v:
                tile.r[s] = v
            if tile not in self.out_tiles:
                self.out_tiles.append(tile)

    def barrier(self, tiles=()):
        waits = [(self.sem[k], self.cnt[k]) for k in ("pe", "act", "dve", "pool") if self.cnt[k] > 0]
        for t in tiles:
            if t.dsem is not None and t.dcnt > 0:
                waits.append((t.dsem, 16 * t.dcnt))
        for k in ("pe", "act", "dve", "sp"):
            seen = self.seen[k]
            w = []
            for s_, v in waits:
                if s_ is self.sem.get(k):
                    continue
                if seen.get(s_, 0) >= v:
                    continue
                seen[s_] = v
                w.append((s_, v))
            if w:
                self.streams[k].append((w, None, None))

    def finish(self):
        waits = []
        for t in self.out_tiles:
            waits.append((t.dsem, 16 * t.dcnt))
        self.streams["sp"].append((waits, None, None))

    def simulate(self):
        vals = {}
        pc = {k: 0 for k in self.ENGS}
        names = {id(v): k for k, v in self.sem.items()}
        progress = True
        while progress:
            progress = False
            for k in self.ENGS:
                st = self.streams[k]
                while pc[k] < len(st):
                    waits, fn, inc = st[pc[k]]
                    if any(vals.get(id(s_), 0) < v for s_, v in waits):
                        break
                    if inc is not None:
                        vals[id(inc[0])] = vals.get(id(inc[0]), 0) + inc[1]
                    pc[k] += 1
                    progress = True
        stuck = {k: (pc[k], len(self.streams[k])) for k in self.ENGS if pc[k] < len(self.streams[k])}
        if stuck:
            msg = []
            for k, (p, n) in stuck.items():
                waits, fn, inc = self.streams[k][p]
                msg.append(f"{k} stuck at {p}/{n}: " + ", ".join(
                    f"{names.get(id(s_), 'dma')}>={v} (now {vals.get(id(s_), 0)})" for s_, v in waits if vals.get(id(s_), 0) < v))
            raise RuntimeError("DEADLOCK: " + " | ".join(msg))
        return {k: len(self.streams[k]) for k in self.ENGS}

    def replay(self, block):
        def run(name):
            def f(e):
                for waits, fn, inc in self.streams[name]:
                    for s, v in waits:
                        e.wait_ge(s.h, v)
                    if fn is None:
                        continue
                    ins = fn(e)
                    if inc is not None:
                        ins.then_inc(inc[0].h, inc[1])
            return f

        block.tensor(run("pe"))
        block.vector(run("dve"))
        block.scalar(run("act"))
        block.gpsimd(run("pool"))
        block.sync(run("sp"))


class _Stop(Exception):
    pass


import os
STAGE = float(os.environ.get("KSTAGE", "99"))
INCALL = os.environ.get("KINCALL", "0") == "1"


def ck(n):
    if STAGE <= n:
        raise _Stop()


def _body(nc, es, P):
    def din(name, shape):
        return nc.dram_tensor(name, list(shape), F32, kind="ExternalInput").ap()

    def dout(name, shape):
        return nc.dram_tensor(name, list(shape), F32, kind="ExternalOutput").ap()

    xin = din("xin", [NT, DM])
    cvec = din("cvec", [128, 16])
    w_mod = din("w_mod", [NL, DM, 6 * DM])
    bmod = din("bmod", [128, NL * 48])
    n1g = din("n1g", [128, NL * 8])
    n2g = din("n2g", [128, NL * 8])
    fng = din("fng", [128, 8])
    w_in = din("w_in", [NL, DM, INW])
    w_out = din("w_out", [NL, DM, DM])
    w1 = din("w1", [NL, DM, DFF])
    w2 = din("w2", [NL, DFF, DM])
    lbr = din("lbr", [128, 8 * NL])
    hgn = din("hgn", [128, NL])
    gqk = din("gqk", [NL, 128, 640])
    cosr = din("cosr", [8, 128, 640])
    sinr = din("sinr", [8, 128, 640])
    cachek = din("cachek", [NL, PAST, 128])
    cachev = din("cachev", [NL, PAST, 128])
    state0 = din("state0", [NL, 2, 4, 128, 128])
    identd = din("identd", [128, 128])
    maskf = din("maskf", [128, 128])
    maskb = din("maskb", [128, 128])
    rstd_ = din("rstd", [128, 1024])
    mh0d = din("mh0d", [128, 512])
    mh1d = din("mh1d", [128, 512])

    yout = dout("yout", [NT, DM])
    newk = dout("newk", [NSEQP, NL, SEQ, 128])
    newv = dout("newv", [NSEQP, NL, SEQ, 128])
    news = dout("news", [NSEQP, NL, 2, 4, 128, 128])

    x_all = P.tile("x_all", [128, 8, NT], F32)
    XG = [TT(f"x{g}", x_all[:, :, g * 512:(g + 1) * 512]) for g in range(4)]
    hbuf = P.tile("hbuf", [128, 2, 8, 1024], BF16)
    HB = [[TT(f"hb{h}{g}", hbuf[:, h, :, g * 512:(g + 1) * 512]) for g in range(2)] for h in range(2)]
    WS = [P.tile(f"ws{i}", [128, 8192], BF16, dma=True) for i in range(3)]

    CONST = TT("const", None, dsem=P.dsem("const"))

    def ctile(shape, dtype):
        h, _, _ = P.alloc(shape, dtype)
        return h[tuple(slice(None) for _ in shape)]

    idf = ctile([128, 128], F32)
    mf = ctile([128, 128], F32)
    mb = ctile([128, 128], F32)
    rst = ctile([128, 512], F32)
    c_sb = ctile([128, 16], F32)
    bmod_sb = ctile([128, NL, 48], F32)
    n1g_sb = ctile([128, NL, 8], F32)
    n2g_sb = ctile([128, NL, 8], F32)
    fng_sb = ctile([128, 8], F32)
    lbr_sb = ctile([128, 8, NL], F32)
    hgn_sb = ctile([128, NL], F32)
    CONSTB = TT("constb", None, dsem=P.dsem("constb"))
    MH0 = ctile([128, 512], BF16)
    MH1 = ctile([128, 512], BF16)
    SMALL = TT("small", None)
    idb = ctile([128, 128], BF16)
    ones_bf = ctile([128, 128], BF16)
    zeros_bf = ctile([128, 128], BF16)
    ones_f = ctile([128, 128], F32)
    scT = ctile([128, 8, 2], BF16)
    lb_sb = ctile([128, 8, NL], F32)
    oml_sb = ctile([128, 8, NL], F32)
    noml_sb = ctile([128, 8, NL], F32)
    lbe = ctile([128, 8, NL], F32)
    lbs = ctile([128, 8], F32)
    MODS = P.tile("mods", [128, NL, 48, 2], F32)
    AB = P.tile("ab", [128, NL, 2, 8, 2], F32)
    RSTD = P.tile("rstdt", [128, 512], F32)
    GQKL = P.tile("gqkl", [128, 640], F32, dma=True)
    TMPF = [P.tile(f"tmpf{i}", [128, 512], F32) for i in range(2)]
    SQ = [P.tile(f"sq{i}", [128, 512], BF16) for i in range(2)]

    WORK0 = P.off
    WORK_END = 229376

    PSF = []
    for i in range(8):
        h = es.enter_context(nc.psum_tensor(f"psf{i}", [128, 512], F32))
        PSF.append(TT(f"psf{i}", h[:, :]))
    PSB = []
    psf_i = [0]
    psb_i = [0]

    def psf():
        t = PSF[psf_i[0] % 6]
        psf_i[0] += 1
        return t

    def psacc():
        t = PSF[6 + psb_i[0] % 2]
        psb_i[0] += 1
        return t

    def psb():
        return psf()

    rot = {}

    def rr(lst, key):
        i = rot.get(key, 0)
        rot[key] = i + 1
        return lst[i % len(lst)]

    op = P.op

    ws_i = [0]

    def wslot():
        t = WS[ws_i[0] % 3]
        ws_i[0] += 1
        return t

    def w_pairs(key, slot):
        kind = key[0]
        if kind == "mod":
            _, l, piece = key
            v = slot.ap.rearrange("p (a b) -> p a b", b=1024)
            return [(v[:, kc, :], w_mod[l, kc * 128:(kc + 1) * 128, piece * 1024:(piece + 1) * 1024]) for kc in range(8)]
        if kind == "A":
            _, l, hf = key
            v = slot.ap[:, 0:8 * 768].rearrange("p (a b) -> p a b", b=768)
            return [(v[:, kc, :], w_in[l, kc * 128:(kc + 1) * 128, 2560:3328]) for kc in range(8)]
        if kind == "R":
            _, l, hf, hd = key
            v = slot.ap[:, 0:8 * 640].rearrange("p (a b) -> p a b", b=640)
            return [(v[:, kc, :], w_in[l, kc * 128:(kc + 1) * 128, hd * 640:(hd + 1) * 640]) for kc in range(8)]
        if kind == "O":
            _, l, hf = key
            v = slot.ap.rearrange("p (a b) -> p a b", b=1024)
            return [(v[:, kc, :], w_out[l, kc * 128:(kc + 1) * 128, :]) for kc in range(8)]
        if kind == "F":
            _, l, c = key
            v1 = slot.ap[:, 0:4096].rearrange("p (a b) -> p a b", b=512)
            v2 = slot.ap[:, 4096:8192].rearrange("p (a b) -> p a b", b=1024)
            pr = [(v1[:, kc, :], w1[l, kc * 128:(kc + 1) * 128, c * 512:(c + 1) * 512]) for kc in range(8)]
            pr += [(v2[:, f, :], w2[l, c * 512 + f * 128: c * 512 + (f + 1) * 128, :]) for f in range(4)]
            return pr
        raise ValueError(key)

    WPLAN = [("mod", 0, piece) for piece in range(6)]
    for l_ in range(NL):
        for hf_ in range(2):
            WPLAN.append(("A", l_, hf_))
            WPLAN += [("R", l_, hf_, hd_) for hd_ in range(4)]
            WPLAN.append(("O", l_, hf_))
        if l_ + 1 < NL:
            WPLAN += [("mod", l_ + 1, piece) for piece in range(6)]
        WPLAN += [("F", l_, c_) for c_ in range(8)]
    wstate = {"wp": 0, "issued": 0, "slots": {}}

    def acquire(key):
        wp = wstate["wp"]
        assert WPLAN[wp] == key, (WPLAN[wp], key)
        while wstate["issued"] <= min(wp + 1, len(WPLAN) - 1):
            k = wstate["issued"]
            slot = WS[k % 3]
            P.dma("pool", w_pairs(WPLAN[k], slot), slot, load=True)
            wstate["slots"][k] = slot
            wstate["issued"] = k + 1
        wstate["wp"] = wp + 1
        return wstate["slots"].pop(wp)

    P.dma("sp", [
        (idf, identd), (mf, maskf), (mb, maskb),
        (rst, rstd_[:, 0:512]),
        (c_sb, cvec), (bmod_sb, bmod.rearrange("p (a b) -> p a b", b=48)),
        (n1g_sb, n1g.rearrange("p (a b) -> p a b", b=8)), (n2g_sb, n2g.rearrange("p (a b) -> p a b", b=8)),
        (fng_sb, fng), (lbr_sb, lbr.rearrange("p (a b) -> p a b", b=NL)), (hgn_sb, hgn),
    ], CONST, load=True)

    P.dma("pool", [(MH0, mh0d), (MH1, mh1d)], CONSTB, load=True)
    op("dve", lambda e: e.memset(ones_bf, 1.0), wr=[SMALL])
    op("dve", lambda e: e.memset(zeros_bf, 0.0), wr=[SMALL])
    op("dve", lambda e: e.memset(ones_f, 1.0), wr=[SMALL])
    op("dve", lambda e: e.tensor_copy(out=idb, in_=idf), rd=[CONST], wr=[SMALL])
    op("act", lambda e: e.activation(out=scT.rearrange("p a b -> p (a b)"), in_=c_sb, func=AF.Silu), rd=[CONST], wr=[SMALL])
    op("act", lambda e: e.activation(out=lbe, in_=lbr_sb, func=AF.Exp), rd=[CONST], wr=[SMALL])
    op("dve", lambda e: e.tensor_reduce(out=lbs, in_=lbe, axis=AX.X, op=ALU.add), rd=[SMALL], wr=[SMALL])
    op("dve", lambda e: e.reciprocal(out=lbs, in_=lbs), rd=[SMALL], wr=[SMALL])
    op("dve", lambda e: e.tensor_tensor(out=lbe, in0=lbe, in1=lbs.unsqueeze(2).to_broadcast([128, 8, NL]), op=ALU.mult), rd=[SMALL], wr=[SMALL])
    op("dve", lambda e: e.memset(lb_sb[:, :, 0:1], 0.0), wr=[SMALL])
    op("dve", lambda e: e.tensor_copy(out=lb_sb[:, :, 1:2], in_=lbe[:, :, 1:2]), rd=[SMALL], wr=[SMALL])
    op("dve", lambda e: e.tensor_tensor(out=lb_sb[:, :, 2:3], in0=lb_sb[:, :, 1:2], in1=lbe[:, :, 2:3], op=ALU.add), rd=[SMALL], wr=[SMALL])
    op("dve", lambda e: e.tensor_tensor(out=lb_sb[:, :, 3:4], in0=lb_sb[:, :, 2:3], in1=lbe[:, :, 3:4], op=ALU.add), rd=[SMALL], wr=[SMALL])
    op("dve", lambda e: e.tensor_scalar(out=oml_sb, in0=lb_sb, scalar1=-1.0, scalar2=1.0, op0=ALU.mult, op1=ALU.add), rd=[SMALL], wr=[SMALL])
    op("dve", lambda e: e.tensor_scalar(out=noml_sb, in0=oml_sb, scalar1=-1.0, scalar2=None, op0=ALU.mult), rd=[SMALL], wr=[SMALL])

    ck(1)
    def emit_mods(l):
        ps = psf()
        for piece in range(6):
            slot = acquire(("mod", l, piece))
            wv = slot.ap.rearrange("p (a b) -> p a b", b=1024)
            for j in range(8):
                jj = piece * 8 + j
                for kc in range(8):
                    last = (kc == 7 and j == 7)
                    op("pe", lambda e, ps=ps, wv=wv, j=j, jj=jj, kc=kc: e.matmul(
                        ps[:, jj * 2:jj * 2 + 2], lhsT=wv[:, kc, j * 128:(j + 1) * 128], rhs=scT[:, kc, :],
                        start=(kc == 0), stop=(kc == 7)),
                       rd=[slot, SMALL], wr=[ps], inc=last)
        op("dve", lambda e, ps=ps, l=l: e.tensor_tensor(
            out=MODS[:, l, :, :], in0=ps[:, 0:96].rearrange("p (a b) -> p a b", b=2),
            in1=bmod_sb[:, l, :].unsqueeze(2).to_broadcast([128, 48, 2]), op=ALU.add), rd=[ps, CONST], wr=[MODS])
        for which, (jo, gsb) in enumerate(((8, n1g_sb), (32, n2g_sb))):
            op("dve", lambda e, l=l, which=which, jo=jo, gsb=gsb: e.scalar_tensor_tensor(
                out=AB[:, l, which, :, :], in0=MODS[:, l, jo:jo + 8, :], scalar=1.0,
                in1=gsb[:, l, :].unsqueeze(2).to_broadcast([128, 8, 2]), op0=ALU.add, op1=ALU.mult),
               rd=[MODS, CONST], wr=[AB])

    emit_mods(0)

    def modv(l, which, kc, i):
        return MODS[:, l, which * 8 + kc, i:i + 1]

    ck(2)
    XST = [P.tile(f"xst{i}", [128, 1024], F32, dma=True, at=WORK0 + i * 4096) for i in range(2)]
    for tt in range(16):
        st = XST[tt % 2]
        P.dma("sp", [(st.ap, xin[tt * 128:(tt + 1) * 128, :])], st, load=True)
        g = tt // 4
        for hh in range(2):
            ps = psf()
            for k4 in range(4):
                kc = hh * 4 + k4
                op("pe", lambda e, ps=ps, st=st, kc=kc, k4=k4: e.transpose(
                    out=ps[:, k4 * 128:(k4 + 1) * 128], in_=st[:, kc * 128:(kc + 1) * 128], identity=idf),
                   rd=[st, CONST], wr=[ps], inc=(k4 == 3))
            eng = "dve" if hh == 0 else "act"
            dst = x_all[:, hh * 4:(hh + 1) * 4, tt * 128:(tt + 1) * 128]
            if eng == "dve":
                op("dve", lambda e, ps=ps, dst=dst: e.tensor_copy(out=dst, in_=ps[:, :].rearrange("p (a b) -> p a b", b=128)),
                   rd=[ps], wr=[XG[g]])
            else:
                op("act", lambda e, ps=ps, dst=dst: e.activation(out=dst, in_=ps[:, :].rearrange("p (a b) -> p a b", b=128), func=AF.Copy),
                   rd=[ps], wr=[XG[g]])

    ck(3)
    def rmsnorm_to(g, l, which, dstT):
        i = g // 2
        xg = XG[g]
        ps = psf()
        for kc in range(8):
            sq = rr(SQ, "sq")
            op("act", lambda e, sq=sq, kc=kc: e.activation(out=sq.ap, in_=xg[:, kc, :], func=AF.Square), rd=[xg], wr=[sq])
            op("pe", lambda e, sq=sq, kc=kc, ps=ps: e.matmul(ps[:, :], lhsT=ones_bf, rhs=sq.ap, start=(kc == 0), stop=(kc == 7)),
               rd=[sq, SMALL], wr=[ps])
        op("act", lambda e, ps=ps: e.activation(out=RSTD.ap, in_=ps[:, :], func=AF.Ln, scale=1.0 / DM, bias=EPS), rd=[ps], wr=[RSTD])
        op("act", lambda e: e.activation(out=RSTD.ap, in_=RSTD.ap, func=AF.Exp, scale=-0.5), rd=[], wr=[RSTD])
        sh = 0 if which == 0 else 3
        for kc in range(8):
            tmp = rr(TMPF, "tmpf")
            op("dve" if kc % 2 == 0 else "pool", lambda e, tmp=tmp, kc=kc: e.tensor_tensor(out=tmp.ap, in0=xg[:, kc, :], in1=RSTD.ap, op=ALU.mult),
               rd=[xg, RSTD], wr=[tmp])
            op("act", lambda e, tmp=tmp, kc=kc: e.activation(
                out=dstT[:, kc, :], in_=tmp.ap, func=AF.Identity, scale=AB[:, l, which, kc, i:i + 1], bias=modv(l, sh, kc, i)),
               rd=[tmp, AB, MODS], wr=[dstT])

    W = WORK0
    a = [W]

    def wt(name, shape, dtype, dma=False):
        t = P.tile(name, shape, dtype, dma=dma, at=a[0])
        a[0] += t.nbytes
        assert a[0] <= WORK_END, f"work overflow {name} {a[0]}"
        return t

    QKF = [wt(f"qkf{i}", [128, 640], F32, dma=True) for i in range(2)]
    SQFS = [wt(f"sqf{i}", [128, 640], BF16) for i in range(2)]
    RT1 = wt("rt1", [128, 640], F32, dma=True)
    RT2 = wt("rt2", [128, 640], F32, dma=True)
    VF = [wt(f"vf{i}", [128, 128], F32, dma=True) for i in range(2)]
    QB = [wt(f"qb{i}", [128, 640], BF16) for i in range(2)]
    QT = wt("qT", [128, 8, 512], BF16)
    KT = wt("kT", [128, 1280], BF16)
    VA0 = wt("va0", [128, 10, 65], BF16)
    VA1 = wt("va1", [128, 10, 128], BF16)
    PT = [wt(f"pt{i}", [128, 512], BF16) for i in range(3)]
    RECS = [wt(f"rec{i}", [128, 512], F32) for i in range(2)]
    BC = wt("bc", [128, 512], F32)
    CK = wt("ck", [128, 2, 128], F32, dma=True)
    CV = wt("cv", [128, 2, 128], F32, dma=True)
    CKB = wt("ckb", [128, 2, 128], BF16)
    SSQS = [wt(f"ssq{i}", [128, 10], F32) for i in range(2)]
    att_end = a[0]
    a[0] = W
    Q32 = wt("q32", [128, 512], BF16)
    TS = [[wt(f"t{k}_{d}", [128, 512], F32) for k in range(4)] for d in range(2)]
    QM = wt("qm", [128, 512], BF16)
    QML = wt("qml", [128, 512], BF16)
    KME = wt("kme", [128, 512], BF16)
    KML = wt("kml", [128, 512], BF16)
    QTD = [wt(f"qtd{d}", [128, 1024], BF16) for d in range(2)]
    KDT = wt("kdt", [128, 512], BF16)
    OPB = [(QM, QML, KME, KML, KDT), (QM, QML, KME, KML, KDT)]
    KBS = [P.tile(f"kb{d}", [128, 512], BF16, at=TMPF[0].addr + d * 1024) for d in range(2)]
    KMS = [P.tile(f"km{d}", [128, 512], BF16, at=TMPF[1].addr + d * 1024) for d in range(2)]
    KDEC = [wt(f"kdec{d}", [128, 8, 128], BF16) for d in range(2)]
    GATE = wt("gate", [128, 1024], BF16)
    VT = wt("vt", [128, 8, 128], BF16)
    AMALL = [wt(f"amall{d}", [128, 8, 128], BF16) for d in range(2)]
    DEC = [wt(f"dec{d}", [128, 16], F32) for d in range(2)]
    S32 = [[wt(f"s32{d}{k}", [128, 128], F32, dma=True) for k in range(2)] for d in range(2)]
    SBC = [[wt(f"sbc{d}{k}", [128, 128], BF16) for k in range(2)] for d in range(2)]
    OSQ = wt("osq", [128, 512], BF16)
    ORS = TS[0][0]
    hg_end = a[0]
    a[0] = W
    UT = [wt(f"ut{i}", [128, 4, 512], BF16) for i in range(2)]
    RL = [wt(f"rl{i}", [128, 512], BF16) for i in range(2)]

    def init_va():
        op("dve", lambda e: e.memset(VA0.ap, 1.0), wr=[VA0])
        op("dve", lambda e: e.memset(VA1.ap, 0.0), wr=[VA1])
        op("dve", lambda e: e.memset(VA1[:, :, 0:1], 1.0), wr=[VA1])

    DMA_WORK = XST + QKF + VF + [CK, CV, RT1, RT2] + S32[0] + S32[1]
    for l in range(NL):
        P.dma("sp", [(GQKL.ap, gqk[l])], GQKL, load=True)
        for hf in range(2):
            P.barrier(DMA_WORK)
            i = hf
            sample = (hf == 1)
            hT = HB[0]
            om = HB[1]
            gs = [2 * hf, 2 * hf + 1]
            for gl in range(2):
                rmsnorm_to(gs[gl], l, 0, hT[gl])

            ck(3.1)
            slotA = acquire(("A", l, hf))
            wA = slotA.ap[:, 0:8 * 768].rearrange("p (a b) -> p a b", b=768)
            init_va()
            ctx_tiles = 2 if sample else 0
            if sample:
                P.dma("sp", [(CK.ap, cachek[l].rearrange("(a p) d -> p a d", p=128))], CK, load=True)
                P.dma("sp", [(CV.ap, cachev[l].rearrange("(a p) d -> p a d", p=128))], CV, load=True)
                op("act", lambda e: e.activation(out=CKB.ap, in_=CK.ap, func=AF.Copy), rd=[CK], wr=[CKB])
                pb = psb()
                for t2 in range(2):
                    op("pe", lambda e, pb=pb, t2=t2: e.matmul(pb[:, t2 * 128:(t2 + 1) * 128], lhsT=CKB[:, t2, :], rhs=idb, start=True, stop=True),
                       rd=[CKB, SMALL], wr=[pb], inc=(t2 == 1))
                op("dve", lambda e, pb=pb: e.tensor_copy(out=KT[:, 0:256], in_=pb[:, 0:256]), rd=[pb], wr=[KT])
                op("dve", lambda e: e.tensor_copy(out=VA0[:, 0:2, 0:64], in_=CV[:, :, 0:64]), rd=[CV], wr=[VA0])
                op("dve", lambda e: e.tensor_copy(out=VA1[:, 0:2, 64:128], in_=CV[:, :, 64:128]), rd=[CV], wr=[VA1])
            def stageA(tt):
                gl = tt // 4
                tsl = slice((tt % 4) * 128, (tt % 4 + 1) * 128)
                sqf = SQFS[tt % 2]
                ssq = SSQS[tt % 2]
                psq = psf()
                pskv = psf()
                for kc in range(8):
                    op("pe", lambda e, kc=kc, psq=psq, gl=gl, tsl=tsl: e.matmul(
                        psq[:, :], lhsT=hT[gl][:, kc, tsl], rhs=wA[:, kc, 0:512], start=(kc == 0), stop=(kc == 7)),
                       rd=[hT[gl], slotA], wr=[psq], inc=(kc == 7))
                for kc in range(8):
                    op("pe", lambda e, kc=kc, pskv=pskv, gl=gl, tsl=tsl: e.matmul(
                        pskv[:, 0:256], lhsT=hT[gl][:, kc, tsl], rhs=wA[:, kc, 512:768], start=(kc == 0), stop=(kc == 7)),
                       rd=[hT[gl], slotA], wr=[pskv], inc=(kc == 7))
                kti = ctx_tiles + tt
                op("act", lambda e, pskv=pskv, kti=kti: e.activation(out=VA0[:, kti, 0:64], in_=pskv[:, 128:192], func=AF.Copy), rd=[pskv], wr=[VA0])
                op("act", lambda e, pskv=pskv, kti=kti: e.activation(out=VA1[:, kti, 64:128], in_=pskv[:, 192:256], func=AF.Copy), rd=[pskv], wr=[VA1])
                if not sample:
                    vf = rr(VF, "vf")
                    op("act", lambda e, pskv=pskv, vf=vf: e.activation(out=vf.ap, in_=pskv[:, 128:256], func=AF.Copy), rd=[pskv], wr=[vf])
                    sq_i, t_i = tt // 2, (tt % 2) * 128
                    P.dma("sp", [(newv[sq_i, l, t_i:t_i + 128, :], vf.ap)], vf, load=False)
                op("act", lambda e, psq=psq: e.activation(out=sqf[:, 0:512], in_=psq[:, :], func=AF.Square), rd=[psq], wr=[sqf])
                op("act", lambda e, pskv=pskv: e.activation(out=sqf[:, 512:640], in_=pskv[:, 0:128], func=AF.Square), rd=[pskv], wr=[sqf])
                op("dve", lambda e: e.tensor_reduce(out=ssq.ap, in_=sqf.ap.rearrange("p (a b) -> p a b", b=64), axis=AX.X, op=ALU.add),
                   rd=[sqf], wr=[ssq])
                qkf = rr(QKF, "qkf")
                op("dve", lambda e, psq=psq, qkf=qkf: e.tensor_tensor(out=qkf[:, 0:512], in0=psq[:, :], in1=GQKL[:, 0:512], op=ALU.mult),
                   rd=[psq, GQKL], wr=[qkf])
                op("dve", lambda e, pskv=pskv, qkf=qkf: e.tensor_tensor(out=qkf[:, 512:640], in0=pskv[:, 0:128], in1=GQKL[:, 512:640], op=ALU.mult),
                   rd=[pskv, GQKL], wr=[qkf])
                return dict(qkf=qkf, kti=kti, ssq=ssq)

            def stageB(tt, st_):
                qkf = st_['qkf']
                kti = st_['kti']
                ssq = st_['ssq']
                op("act", lambda e: e.activation(out=ssq.ap, in_=ssq.ap, func=AF.Ln, scale=1.0 / 64, bias=EPS), rd=[ssq], wr=[ssq])
                op("act", lambda e: e.activation(out=ssq.ap, in_=ssq.ap, func=AF.Exp, scale=-0.5), rd=[ssq], wr=[ssq])
                op("dve", lambda e, qkf=qkf: e.tensor_tensor(
                    out=qkf.ap.rearrange("p (a b) -> p a b", b=64), in0=qkf.ap.rearrange("p (a b) -> p a b", b=64),
                    in1=ssq.ap.unsqueeze(2).to_broadcast([128, 10, 64]), op=ALU.mult), rd=[ssq], wr=[qkf])
                qb = rr(QB, "qb")
                if not sample:
                    sq_i, t_i = tt // 2, (tt % 2) * 128
                    P.dma("sp", [(newk[sq_i, l, t_i:t_i + 128, :], qkf[:, 512:640])], qkf, load=False)
                    op("act", lambda e, qkf=qkf, qb=qb: e.activation(
                        out=qb[:, 0:512].rearrange("p (g k d) -> p k g d", g=4, k=2),
                        in_=qkf[:, 0:512].rearrange("p (k g d) -> p k g d", k=2, g=4), func=AF.Copy), rd=[qkf], wr=[qb])
                    op("act", lambda e, qkf=qkf, qb=qb: e.activation(out=qb[:, 512:640], in_=qkf[:, 512:640], func=AF.Copy), rd=[qkf], wr=[qb])
                else:
                    q3 = qkf.ap.rearrange("p (a b) -> p a b", b=64)
                    r2 = RT2.ap.rearrange("p (a b) -> p a b", b=64)
                    P.dma("sp", [(RT1.ap, cosr[tt])], RT1, load=True)
                    P.dma("sp", [(RT2.ap, sinr[tt])], RT2, load=True)
                    op("dve", lambda e, qkf=qkf: e.tensor_tensor(out=RT1.ap, in0=RT1.ap, in1=qkf.ap, op=ALU.mult), rd=[qkf], wr=[RT1])
                    for ax in range(2):
                        lo = slice(ax * 32, ax * 32 + 16)
                        hi = slice(ax * 32 + 16, ax * 32 + 32)
                        op("dve", lambda e, q3=q3, r2=r2, lo=lo, hi=hi: e.tensor_tensor(out=r2[:, :, lo], in0=r2[:, :, lo], in1=q3[:, :, hi], op=ALU.mult),
                           rd=[qkf], wr=[RT2])
                        op("dve", lambda e, q3=q3, r2=r2, lo=lo, hi=hi: e.tensor_tensor(out=r2[:, :, hi], in0=r2[:, :, hi], in1=q3[:, :, lo], op=ALU.mult),
                           rd=[qkf], wr=[RT2])
                    op("dve", lambda e, qb=qb: e.tensor_tensor(
                        out=qb[:, 0:512].rearrange("p (g k d) -> p k g d", g=4, k=2),
                        in0=RT1[:, 0:512].rearrange("p (k g d) -> p k g d", k=2, g=4),
                        in1=RT2[:, 0:512].rearrange("p (k g d) -> p k g d", k=2, g=4), op=ALU.add), rd=[RT1, RT2], wr=[qb])
                    op("dve", lambda e, qb=qb: e.tensor_tensor(out=qb[:, 512:640], in0=RT1[:, 512:640], in1=RT2[:, 512:640], op=ALU.add),
                       rd=[RT1, RT2], wr=[qb])
                pb = psb()
                for g4 in range(4):
                    src = qb[:, g4 * 128:(g4 + 1) * 128]
                    op("pe", lambda e, pb=pb, src=src, g4=g4: e.matmul(pb[:, g4 * 128:(g4 + 1) * 128], lhsT=src, rhs=idb, start=True, stop=True),
                       rd=[qb, SMALL], wr=[pb], inc=(g4 == 3))
                pbk = psf()
                op("pe", lambda e, pbk=pbk, qb=qb: e.matmul(pbk[:, 0:128], lhsT=qb[:, 512:640], rhs=idb, start=True, stop=True),
                   rd=[qb, SMALL], wr=[pbk])
                op("dve", lambda e, pb=pb, tt=tt: e.tensor_copy(
                    out=QT[:, tt, :], in_=pb[:, 0:512]), rd=[pb], wr=[QT])
                op("act", lambda e, pbk=pbk, kti=kti: e.activation(out=KT[:, kti * 128:(kti + 1) * 128], in_=pbk[:, 0:128], func=AF.Copy),
                   rd=[pbk], wr=[KT])


            st_q = {}
            for tt in range(9):
                if tt < 8:
                    st_q[tt] = stageA(tt)
                if tt >= 1:
                    stageB(tt - 1, st_q.pop(tt - 1))

            ck(4)
            if sample:
                units = [(list(range(8)), list(range(10)))]
            else:
                units = [([2 * j, 2 * j + 1], [2 * j, 2 * j + 1]) for j in range(4)]
            blocks = []
            for (qtiles, ktiles) in units:
                for kv in range(2):
                    for qt in qtiles:
                        blocks.append((kv, qt, ktiles))
            tasks = [(bi, ki) for bi, b in enumerate(blocks) for ki in range(len(b[2]))]
            LA = 2
            ptq = {}
            psoq = {}
            fin2 = {}

            def emit_qk(j):
                bi, ki = tasks[j]
                kv, qt, ktiles = blocks[bi]
                kt = ktiles[ki]
                pr = slice(kv * 64, kv * 64 + 64)
                pss = psf()
                op("pe", lambda e, pss=pss, kt=kt, qt=qt, pr=pr: e.matmul(
                    pss[:, :], lhsT=KT[pr, kt * 128:(kt + 1) * 128], rhs=QT[pr, qt, :], start=True, stop=True),
                   rd=[KT, QT], wr=[pss])
                pt = rr(PT, "pt")
                op("act", lambda e, pss=pss, pt=pt: e.activation(out=pt.ap, in_=pss[:, :], func=AF.Exp, scale=0.125), rd=[pss], wr=[pt])
                ptq[j] = pt

            def emit_pv(j):
                bi, ki = tasks[j]
                kv, qt, ktiles = blocks[bi]
                kt = ktiles[ki]
                va = VA0 if kv == 0 else VA1
                M = 65 if kv == 0 else 128
                if ki == 0:
                    psoq[bi] = psacc()
                pso = psoq[bi]
                pt = ptq.pop(j)
                n = len(ktiles)
                op("pe", lambda e, pso=pso, va=va, kt=kt, pt=pt, M=M, ki=ki, n=n: e.matmul(
                    pso[0:M, :], lhsT=va[:, kt, 0:M], rhs=pt.ap, start=(ki == 0), stop=(ki == n - 1)),
                   rd=[va, pt], wr=[pso])
                if ki == n - 1:
                    srow = 64 if kv == 0 else 0
                    REC = RECS[bi % 2]
                    op("act", lambda e, pso=pso, srow=srow, REC=REC: e.activation(out=REC[srow:srow + 1, :], in_=pso[srow:srow + 1, :], func=AF.Ln), rd=[pso], wr=[REC])
                    op("act", lambda e, srow=srow, REC=REC: e.activation(out=REC[srow:srow + 1, :], in_=REC[srow:srow + 1, :], func=AF.Exp, scale=-1.0), rd=[], wr=[REC])
                    fin2.setdefault(j + 2, []).append(bi)

            def emit_fin2(bi):
                kv, qt, ktiles = blocks[bi]
                pr = slice(kv * 64, kv * 64 + 64)
                srow = 64 if kv == 0 else 0
                pso = psoq[bi]
                REC = RECS[bi % 2]
                psbc = psf()
                op("pe", lambda e, psbc=psbc, srow=srow, REC=REC: e.matmul(
                    psbc[:, :], lhsT=ones_f[srow:srow + 1, :], rhs=REC[srow:srow + 1, :], start=True, stop=True),
                   rd=[REC, SMALL], wr=[psbc])
                op("act", lambda e, psbc=psbc, pr=pr: e.activation(out=BC[pr, :], in_=psbc[pr, :], func=AF.Copy), rd=[psbc], wr=[BC])
                gl = qt // 4
                tsl = slice((qt % 4) * 128, (qt % 4 + 1) * 128)
                op("dve", lambda e, pso=pso, pr=pr, gl=gl, tsl=tsl: e.tensor_tensor(
                    out=om[gl][pr, 4:8, tsl], in0=pso[pr, :].rearrange("p (a b) -> p a b", b=128),
                    in1=BC[pr, :].rearrange("p (a b) -> p a b", b=128), op=ALU.mult), rd=[pso, BC], wr=[om[gl]])

            nt_ = len(tasks)
            for j in range(nt_ + LA + 3):
                if j < nt_:
                    emit_qk(j)
                jj = j - LA
                if 0 <= jj < nt_:
                    emit_pv(jj)
                for bi in fin2.pop(jj, []):
                    emit_fin2(bi)
            assert not fin2 and not ptq

            ck(5)
            P.barrier(DMA_WORK)
            seqs = [(0, 1024)] if sample else [(j * 256, 256) for j in range(4)]
            OACC = [PSF[6], PSF[7]]
            for hd in range(4):
                slotR = acquire(("R", l, hf, hd))
                wR = slotR.ap[:, 0:8 * 640].rearrange("p (a b) -> p a b", b=640)
                for gl in range(2):
                    gsl = slice(gl * 512, (gl + 1) * 512)
                    pps = {}

                    def proj(ci, gl=gl):
                        ps = psf()
                        pps[ci] = ps
                        for kc in range(8):
                            op("pe", lambda e, ps=ps, kc=kc, ci=ci, gl=gl: e.matmul(
                                ps[:, :], lhsT=wR[:, kc, ci * 128:(ci + 1) * 128], rhs=hT[gl][:, kc, :], start=(kc == 0), stop=(kc == 7)),
                               rd=[slotR, hT[gl]], wr=[ps], inc=(kc == 7))
                        return ps

                    def emit_q():
                        ps = proj(0)
                        op("act", lambda e, ps=ps: e.activation(out=Q32.ap, in_=ps[:, :], func=AF.Copy), rd=[ps], wr=[Q32])

                    def emit_g(gsl=gsl):
                        ps = proj(4)
                        GT = TS[1][3]
                        op("act", lambda e, ps=ps, GT=GT: e.activation(out=GT.ap, in_=ps[:, :], func=AF.Exp, scale=-1.0), rd=[ps], wr=[GT])
                        op("act", lambda e, GT=GT: e.activation(out=GT.ap, in_=GT.ap, func=AF.Ln, bias=1.0), rd=[], wr=[GT])
                        op("act", lambda e, GT=GT: e.activation(out=GT.ap, in_=GT.ap, func=AF.Exp, scale=-1.0), rd=[], wr=[GT])
                        op("dve", lambda e, ps=ps, gsl=gsl, GT=GT: e.tensor_tensor(out=GATE[:, gsl], in0=ps[:, :], in1=GT.ap, op=ALU.mult), rd=[ps, GT], wr=[GATE])

                    proj(1)
                    proj(2)

                    def dir_steps(d, ps, gl=gl, gsl=gsl, hd=hd, l=l):
                        T1, T2, T3, T4 = TS[d]
                        QM, QML, KME, KML, KDT = OPB[d]
                        KB = KBS[d]
                        KMb = KMS[d]
                        sc_oml = oml_sb[:, d * 4 + hd, l:l + 1]
                        sc_lb = lb_sb[:, d * 4 + hd, l:l + 1]
                        sc_noml = noml_sb[:, d * 4 + hd, l:l + 1]
                        last = 63 if d == 0 else 0
                        mid = 31 if d == 0 else 32
                        mE, mL = (MH0, MH1) if d == 0 else (MH1, MH0)
                        t2c = T2.ap.rearrange("p (c t) -> p c t", t=64)
                        t3c = T3.ap.rearrange("p (c t) -> p c t", t=64)
                        t4c = T4.ap.rearrange("p (c t) -> p c t", t=64)
                        msk = mf if d == 0 else mb
                        st = []
                        A = st.append
                        A(lambda: op("act", lambda e: e.activation(out=T1.ap, in_=ps[:, :], func=AF.Exp, scale=-1.0), rd=[ps], wr=[T1]))
                        A(lambda: op("act", lambda e: e.activation(out=T2.ap, in_=T1.ap, func=AF.Ln, bias=1.0), rd=[T1], wr=[T2]))
                        A(lambda: op("act", lambda e: e.activation(out=T4.ap, in_=T1.ap, func=AF.Ln, scale=sc_lb, bias=1.0), rd=[T1, SMALL], wr=[T4]))
                        A(lambda: op("pool", lambda e: e.tensor_tensor(out=T2.ap, in0=T4.ap, in1=T2.ap, op=ALU.subtract), rd=[T4], wr=[T2]))
                        A(lambda: op("act", lambda e: e.activation(out=T1.ap, in_=T2.ap, func=AF.Exp), rd=[T2], wr=[T1]))
                        A(lambda: op("dve", lambda e: e.tensor_tensor_scan(out=T3.ap, data0=rst, data1=T2.ap, initial=0.0, op0=ALU.mult, op1=ALU.add),
                                     rd=[T2, CONST], wr=[T3]))
                        if d == 1:
                            A(lambda: op("dve", lambda e: e.tensor_tensor(
                                out=t3c, in0=t3c[:, :, 63:64].to_broadcast([128, 8, 64]), in1=t3c, op=ALU.subtract), rd=[], wr=[T3]))
                            A(lambda: op("dve", lambda e: e.tensor_tensor(out=T3.ap, in0=T3.ap, in1=T2.ap, op=ALU.add), rd=[T2], wr=[T3]))
                        A(lambda: op("dve", lambda e: e.tensor_scalar(out=KB.ap, in0=T1.ap, scalar1=-1.0, scalar2=1.0, op0=ALU.mult, op1=ALU.add), rd=[T1], wr=[KB]))
                        A(lambda: op("act", lambda e: e.activation(out=T2.ap, in_=T3.ap, func=AF.Exp), rd=[T3], wr=[T2]))
                        A(lambda: op("pool", lambda e: e.tensor_tensor(out=QTD[d][:, gsl], in0=Q32.ap, in1=T2.ap, op=ALU.mult), rd=[Q32, T2], wr=[QTD[d]]))
                        A(lambda: op("pool", lambda e: e.tensor_copy(out=DEC[d][:, gl * 8:(gl + 1) * 8], in_=t2c[:, :, last]), rd=[T2], wr=[DEC[d]]))
                        A(lambda: op("dve", lambda e: e.tensor_tensor(
                            out=t4c, in0=t3c[:, :, last:last + 1].to_broadcast([128, 8, 64]), in1=t3c, op=ALU.subtract), rd=[T3], wr=[T4]))
                        A(lambda: op("act", lambda e: e.activation(out=T4.ap, in_=T4.ap, func=AF.Exp), rd=[], wr=[T4]))
                        A(lambda: op("dve", lambda e: e.tensor_tensor(out=KDT.ap, in0=T4.ap, in1=KB.ap, op=ALU.mult), rd=[KB, T4], wr=[KDT]))

                        def kdt_tr():
                            pb = psf()
                            for t4 in range(4):
                                op("pe", lambda e, pb=pb, t4=t4: e.matmul(pb[:, t4 * 128:(t4 + 1) * 128], lhsT=KDT[:, t4 * 128:(t4 + 1) * 128], rhs=idb, start=True, stop=True),
                                   rd=[KDT, SMALL], wr=[pb], inc=(t4 == 3))
                            op("act", lambda e, pb=pb: e.activation(
                                out=KDEC[d][:, gl * 4:(gl + 1) * 4, :], in_=pb[:, 0:512].rearrange("p (a b) -> p a b", b=128), func=AF.Copy), rd=[pb], wr=[KDEC[d]])
                        A(kdt_tr)
                        A(lambda: op("dve", lambda e: e.tensor_tensor(
                            out=t4c, in0=t3c, in1=t3c[:, :, mid:mid + 1].to_broadcast([128, 8, 64]), op=ALU.subtract), rd=[T3], wr=[T4]))
                        A(lambda: op("act", lambda e: e.activation(out=T2.ap, in_=T4.ap, func=AF.Exp), rd=[T4], wr=[T2]))
                        A(lambda: op("act", lambda e: e.activation(out=T4.ap, in_=T4.ap, func=AF.Exp, scale=-1.0), rd=[], wr=[T4]))
                        A(lambda: op("dve", lambda e: e.tensor_tensor(out=QM.ap, in0=Q32.ap, in1=T2.ap, op=ALU.mult), rd=[Q32, T2], wr=[QM]))
                        A(lambda: op("pool", lambda e: e.tensor_tensor(out=QML.ap, in0=QM.ap, in1=mL, op=ALU.mult), rd=[QM, CONSTB], wr=[QML]))
                        A(lambda: op("dve", lambda e: e.tensor_tensor(out=KMb.ap, in0=T4.ap, in1=KB.ap, op=ALU.mult), rd=[T4, KB], wr=[KMb]))
                        A(lambda: op("pool", lambda e: e.tensor_tensor(out=KME.ap, in0=KMb.ap, in1=mE, op=ALU.mult), rd=[KMb, CONSTB], wr=[KME]))
                        A(lambda: op("dve", lambda e: e.tensor_tensor(out=KML.ap, in0=KMb.ap, in1=mL, op=ALU.mult), rd=[KMb, CONSTB], wr=[KML]))

                        def amat():
                            psa = psf()
                            for t4 in range(4):
                                tsl = slice(t4 * 128, (t4 + 1) * 128)
                                op("pe", lambda e, psa=psa, tsl=tsl: e.matmul(psa[:, tsl], lhsT=KME[:, tsl], rhs=QM[:, tsl], start=True, stop=False, skip_group_check=True),
                                   rd=[KME, QM], wr=[psa], inc=False)
                                op("pe", lambda e, psa=psa, tsl=tsl: e.matmul(psa[:, tsl], lhsT=KML[:, tsl], rhs=QML[:, tsl], start=False, stop=True, skip_group_check=True),
                                   rd=[KML, QML], wr=[psa], inc=(t4 == 3))
                            for t4 in range(4):
                                op("dve", lambda e, psa=psa, t4=t4: e.tensor_tensor(
                                    out=AMALL[d][:, gl * 4 + t4, :], in0=psa[:, t4 * 128:(t4 + 1) * 128], in1=msk, op=ALU.mult),
                                   rd=[psa, CONST], wr=[AMALL[d]])
                        A(amat)
                        return st

                    sts = [dir_steps(0, pps[1]), dir_steps(1, pps[2])]
                    sts[0].insert(2, emit_q)
                    sts[1].insert(10, emit_g)
                    LAG = 3
                    for k_ in range(max(len(sts[0]), len(sts[1]) + LAG)):
                        if k_ < len(sts[0]):
                            sts[0][k_]()
                        if 0 <= k_ - LAG < len(sts[1]):
                            sts[1][k_ - LAG]()
                    psv = psf()
                    for t4 in range(4):
                        for kc in range(8):
                            op("pe", lambda e, psv=psv, t4=t4, kc=kc, gl=gl: e.matmul(
                                psv[:, t4 * 128:(t4 + 1) * 128], lhsT=hT[gl][:, kc, t4 * 128:(t4 + 1) * 128], rhs=wR[:, kc, 384:512],
                                start=(kc == 0), stop=(kc == 7)), rd=[slotR, hT[gl]], wr=[psv], inc=(kc == 7 and t4 == 3))
                    op("act", lambda e, psv=psv, gl=gl: e.activation(
                        out=VT[:, gl * 4:(gl + 1) * 4, :], in_=psv[:, :].rearrange("p (a b) -> p a b", b=128), func=AF.Copy), rd=[psv], wr=[VT])

                ck(5.1)
                for acc in OACC:
                    op("pe", lambda e, acc=acc: e.matmul(acc[:, :], lhsT=zeros_bf, rhs=MH0, start=True, stop=True, skip_group_check=True),
                       rd=[SMALL, CONSTB], wr=[acc])
                for ti in range(8):
                    acc = OACC[ti // 4]
                    osl = slice((ti % 4) * 128, (ti % 4 + 1) * 128)
                    for d in range(2):
                        op("pe", lambda e, acc=acc, osl=osl, ti=ti, d=d: e.matmul(
                            acc[:, osl], lhsT=VT[:, ti, :], rhs=AMALL[d][:, ti, :], start=False, stop=True, skip_group_check=True),
                           rd=[VT, AMALL[d]], wr=[acc], inc=(d == 1))
                ck(5.2)
                for p0 in range(0, len(seqs), 2):
                    grp = seqs[p0:p0 + 2]
                    cur = {}
                    for k, (s0, slen) in enumerate(grp):
                        for d in range(2):
                            if sample:
                                P.dma("sp", [(S32[d][k].ap, state0[l, d, hd])], S32[d][k], load=True)
                            else:
                                op("dve", lambda e, d=d, k=k: e.memset(S32[d][k].ap, 0.0), wr=[S32[d][k]])
                            op("act", lambda e, d=d, k=k: e.activation(out=SBC[d][k].ap, in_=S32[d][k].ap, func=AF.Copy), rd=[S32[d][k]], wr=[SBC[d][k]])
                    nch = grp[0][1] // 64
                    for step in range(nch):
                        for k, (s0, slen) in enumerate(grp):
                            c0 = s0 // 64
                            for d in range(2):
                                c = c0 + (step if d == 0 else nch - 1 - step)
                                tile_i, half_i = c // 2, c % 2
                                prt = slice(half_i * 64, half_i * 64 + 64)
                                acc = OACC[tile_i // 4]
                                ocol = (tile_i % 4) * 128 + half_i * 64
                                op("pe", lambda e, acc=acc, ocol=ocol, d=d, k=k, c=c: e.matmul(
                                    acc[:, ocol:ocol + 64], lhsT=SBC[d][k].ap, rhs=QTD[d][:, c * 64:(c + 1) * 64], start=False, stop=True, skip_group_check=True),
                                   rd=[SBC[d][k], QTD[d]], wr=[acc])
                                pkv = psf()
                                op("pe", lambda e, pkv=pkv, d=d, tile_i=tile_i, prt=prt: e.matmul(
                                    pkv[:, 0:128], lhsT=KDEC[d][prt, tile_i, :], rhs=VT[prt, tile_i, :], start=True, stop=True),
                                   rd=[KDEC[d], VT], wr=[pkv])
                                op("dve", lambda e, pkv=pkv, d=d, k=k, c=c: e.scalar_tensor_tensor(
                                    out=S32[d][k].ap, in0=S32[d][k].ap, scalar=DEC[d][:, c:c + 1], in1=pkv[:, 0:128], op0=ALU.mult, op1=ALU.add),
                                   rd=[pkv, DEC[d]], wr=[S32[d][k]])
                                if step < nch - 1:
                                    op("dve", lambda e, d=d, k=k: e.tensor_copy(out=SBC[d][k].ap, in_=S32[d][k].ap), rd=[S32[d][k]], wr=[SBC[d][k]])
                    if not sample:
                        for k in range(len(grp)):
                            for d in range(2):
                                P.dma("sp", [(news[p0 + k, l, d, hd], S32[d][k].ap)], S32[d][k], load=False)
                ck(5.3)
                for gl in range(2):
                    acc = OACC[gl]
                    hsl = slice(gl * 512, (gl + 1) * 512)
                    op("act", lambda e, acc=acc: e.activation(out=OSQ.ap, in_=acc[:, :], func=AF.Square), rd=[acc], wr=[OSQ])
                    pss = psf()
                    op("pe", lambda e, pss=pss: e.matmul(pss[:, :], lhsT=ones_bf, rhs=OSQ.ap, start=True, stop=True), rd=[OSQ, SMALL], wr=[pss])
                    op("act", lambda e, pss=pss: e.activation(out=ORS.ap, in_=pss[:, :], func=AF.Ln, scale=1.0 / 128, bias=EPS), rd=[pss], wr=[ORS])
                    op("act", lambda e: e.activation(out=ORS.ap, in_=ORS.ap, func=AF.Exp, scale=-0.5), rd=[], wr=[ORS])
                    op("dve", lambda e, hsl=hsl: e.tensor_tensor(out=ORS.ap, in0=ORS.ap, in1=GATE[:, hsl], op=ALU.mult), rd=[GATE], wr=[ORS])
                    op("dve", lambda e, acc=acc, gl=gl, hd=hd: e.scalar_tensor_tensor(
                        out=om[gl][:, hd, :], in0=acc[:, :], scalar=hgn_sb[:, l:l + 1], in1=ORS.ap, op0=ALU.mult, op1=ALU.mult),
                       rd=[acc, ORS, CONST], wr=[om[gl]])

            ck(6)
            slotO = acquire(("O", l, hf))
            wO = slotO.ap.rearrange("p (a b) -> p a b", b=1024)
            for gl in range(2):
                g = gs[gl]
                for dc in range(8):
                    ps = psf()
                    for kc in range(8):
                        op("pe", lambda e, ps=ps, kc=kc, dc=dc, gl=gl: e.matmul(
                            ps[:, :], lhsT=wO[:, kc, dc * 128:(dc + 1) * 128], rhs=om[gl][:, kc, :], start=(kc == 0), stop=(kc == 7)),
                           rd=[slotO, om[gl]], wr=[ps], inc=(kc == 7))
                    op("dve", lambda e, ps=ps, dc=dc, g=g: e.scalar_tensor_tensor(
                        out=XG[g][:, dc, :], in0=ps[:, :], scalar=modv(l, 2, dc, i), in1=XG[g][:, dc, :], op0=ALU.mult, op1=ALU.add),
                       rd=[ps, MODS], wr=[XG[g]])

        ck(8)
        P.barrier(DMA_WORK)
        if l + 1 < NL:
            emit_mods(l + 1)
        def ffn_u(c, g, slotF):
            w1v = slotF.ap[:, 0:4096].rearrange("p (a b) -> p a b", b=512)
            h2 = HB[g // 2][g % 2]
            ut = rr(UT, "ut")
            for f in range(4):
                ps = psf()
                for kc in range(8):
                    op("pe", lambda e, ps=ps, kc=kc, f=f, h2=h2: e.matmul(
                        ps[:, :], lhsT=w1v[:, kc, f * 128:(f + 1) * 128], rhs=h2[:, kc, :], start=(kc == 0), stop=(kc == 7)),
                       rd=[slotF, h2], wr=[ps], inc=(kc == 7))
                rl = rr(RL, "rl")
                op("act", lambda e, ps=ps, rl=rl: e.activation(out=rl.ap, in_=ps[:, :], func=AF.Relu), rd=[ps], wr=[rl])
                op("dve", lambda e, rl=rl, ut=ut, f=f: e.tensor_tensor(out=ut[:, f, :], in0=rl.ap, in1=rl.ap, op=ALU.mult), rd=[rl], wr=[ut])
            return ut

        def ffn_y(c, g, slotF, ut):
            w2v = slotF.ap[:, 4096:8192].rearrange("p (a b) -> p a b", b=1024)
            i = g // 2
            for dc in range(8):
                ps = psf()
                for f in range(4):
                    op("pe", lambda e, ps=ps, f=f, dc=dc, ut=ut: e.matmul(
                        ps[:, :], lhsT=w2v[:, f, dc * 128:(dc + 1) * 128], rhs=ut[:, f, :], start=(f == 0), stop=(f == 3)),
                       rd=[slotF, ut], wr=[ps], inc=(f == 3))
                op("dve", lambda e, ps=ps, dc=dc, g=g, i=i: e.scalar_tensor_tensor(
                    out=XG[g][:, dc, :], in0=ps[:, :], scalar=modv(l, 5, dc, i), in1=XG[g][:, dc, :], op0=ALU.mult, op1=ALU.add),
                   rd=[ps, MODS], wr=[XG[g]])

        pend = None
        slotF = acquire(("F", l, 0))
        for g in range(4):
            rmsnorm_to(g, l, 1, HB[g // 2][g % 2])
            ut = ffn_u(0, g, slotF)
            if pend is not None:
                ffn_y(*pend)
            pend = (0, g, slotF, ut)
        for c in range(1, 8):
            slotF = acquire(("F", l, c))
            for g in range(4):
                ut = ffn_u(c, g, slotF)
                ffn_y(*pend)
                pend = (c, g, slotF, ut)
        ffn_y(*pend)

    ck(10)
    P.barrier(DMA_WORK)
    for g in range(4):
        xg = XG[g]
        ps = psf()
        for kc in range(8):
            sq = rr(SQ, "sq")
            op("act", lambda e, sq=sq, kc=kc, xg=xg: e.activation(out=sq.ap, in_=xg[:, kc, :], func=AF.Square), rd=[xg], wr=[sq])
            op("pe", lambda e, sq=sq, kc=kc, ps=ps: e.matmul(ps[:, :], lhsT=ones_bf, rhs=sq.ap, start=(kc == 0), stop=(kc == 7)),
               rd=[sq, SMALL], wr=[ps])
        op("act", lambda e, ps=ps: e.activation(out=RSTD.ap, in_=ps[:, :], func=AF.Ln, scale=1.0 / DM, bias=EPS), rd=[ps], wr=[RSTD])
        op("act", lambda e: e.activation(out=RSTD.ap, in_=RSTD.ap, func=AF.Exp, scale=-0.5), rd=[], wr=[RSTD])
        for kc in range(8):
            op("dve", lambda e, kc=kc, xg=xg: e.scalar_tensor_tensor(
                out=xg[:, kc, :], in0=xg[:, kc, :], scalar=fng_sb[:, kc:kc + 1], in1=RSTD.ap, op0=ALU.mult, op1=ALU.mult),
               rd=[RSTD, CONST], wr=[xg])
        for t4 in range(4):
            tt = g * 4 + t4
            st = XST[tt % 2]
            for hh in range(2):
                ps2 = psf()
                for k4 in range(4):
                    kc = hh * 4 + k4
                    op("pe", lambda e, ps2=ps2, kc=kc, k4=k4, t4=t4, xg=xg: e.transpose(
                        out=ps2[:, k4 * 128:(k4 + 1) * 128], in_=xg[:, kc, t4 * 128:(t4 + 1) * 128], identity=idf),
                       rd=[xg, CONST], wr=[ps2], inc=(k4 == 3))
                if hh == 0:
                    op("dve", lambda e, ps2=ps2, st=st: e.tensor_copy(out=st[:, 0:512], in_=ps2[:, :]), rd=[ps2], wr=[st])
                else:
                    op("act", lambda e, ps2=ps2, st=st: e.activation(out=st[:, 512:1024], in_=ps2[:, :], func=AF.Copy), rd=[ps2], wr=[st])
            P.dma("sp", [(yout[tt * 128:(tt + 1) * 128, :], st.ap)], st, load=False)

def _tail(nc, es, P):
    P.finish()
    print("stream sizes", P.simulate(), "cnt", P.cnt)
    with nc.Block() as block:
        P.replay(block)
    es.close()
    return nc


def build_program():
    nc = bass.Bass("TRN2", target_bir_lowering=False)
    es = contextlib.ExitStack()
    P = Prog(nc, es)
    try:
        _body(nc, es, P)
    except _Stop:
        pass
    return _tail(nc, es, P)


_CACHE = {}


def _get_prog():
    if "nc" not in _CACHE:
        _CACHE["nc"] = build_program()
    return _CACHE["nc"]


def _rope_tables():
    T = DEC_SEQ
    GRID_W = 64
    rows = T // GRID_W
    row = np.repeat(np.arange(rows, dtype=np.float32), GRID_W)
    col = np.tile(np.arange(GRID_W, dtype=np.float32), rows)
    inv = (10000.0 ** (-np.arange(0, 32, 2, dtype=np.float32) / 32)).astype(np.float32)
    ar = row[:, None] * inv[None, :]
    ac = col[:, None] * inv[None, :]
    ang = np.concatenate([ar, ar, ac, ac], axis=-1).astype(np.float32)
    cos = np.cos(ang).astype(np.float32)
    sin = np.sin(ang).astype(np.float32)
    sgn = np.ones(64, np.float32)
    sgn[0:16] = -1.0
    sgn[32:48] = -1.0
    sins = sin * sgn[None, :]
    cosT = np.tile(cos.reshape(8, 128, 64), (1, 1, 10))
    sinT = np.tile(sins.reshape(8, 128, 64), (1, 1, 10))
    return np.ascontiguousarray(cosT), np.ascontiguousarray(sinT)


def kernel(x_prompt, x_sample, cache_k, cache_v, state_hgrn, c, c_ctx, w_mod, b_mod, norm1_g, w_in, lb_raw,
           hgrn_norm_g, q_norm_g, k_norm_g, w_out, norm2_g, w1, w2, final_norm_g):
    f = lambda a: np.ascontiguousarray(np.asarray(a, dtype=np.float32))
    x_prompt, x_sample, cache_k, cache_v, state_hgrn = map(f, (x_prompt, x_sample, cache_k, cache_v, state_hgrn))
    c, c_ctx, w_mod, b_mod, norm1_g, w_in, lb_raw = map(f, (c, c_ctx, w_mod, b_mod, norm1_g, w_in, lb_raw))
    hgrn_norm_g, q_norm_g, k_norm_g, w_out, norm2_g, w1, w2, final_norm_g = map(
        f, (hgrn_norm_g, q_norm_g, k_norm_g, w_out, norm2_g, w1, w2, final_norm_g))

    colperm = []
    for hd in range(4):
        for blk in (0, 512, 1024, 1536, 2048):
            colperm += list(range(blk + hd * 128, blk + (hd + 1) * 128))
    colperm += list(range(2560, 3328))
    w_in_p = np.ascontiguousarray(w_in[:, :, colperm])
    rowperm = list(range(512))
    for g in range(4):
        rowperm += list(range(512 + g * 64, 512 + (g + 1) * 64))
        rowperm += list(range(512 + (4 + g) * 64, 512 + (5 + g) * 64))
    w_out_p = np.ascontiguousarray(w_out[:, rowperm, :])

    def fm(v):
        return v.reshape(v.shape[:-1] + (8, 128))

    bmod_l = np.ascontiguousarray(b_mod.reshape(NL, 48, 128).transpose(2, 0, 1).reshape(128, NL * 48))
    n1g_l = np.ascontiguousarray(norm1_g.reshape(NL, 8, 128).transpose(2, 0, 1).reshape(128, NL * 8))
    n2g_l = np.ascontiguousarray(norm2_g.reshape(NL, 8, 128).transpose(2, 0, 1).reshape(128, NL * 8))
    fng_l = np.ascontiguousarray(final_norm_g.reshape(8, 128).T)
    lbr_l = np.ascontiguousarray(lb_raw.reshape(NL, 2, 4, 128).transpose(3, 1, 2, 0).reshape(128, 8 * NL))
    hgn_l = np.ascontiguousarray(hgrn_norm_g.T)
    gqk_l = np.concatenate([np.tile(q_norm_g, (1, 8)), np.tile(k_norm_g, (1, 2))], axis=1)
    gqk_l = np.ascontiguousarray(np.broadcast_to(gqk_l[:, None, :], (NL, 128, 640)))
    cosT, sinT = _rope_tables()
    ident = np.eye(128, dtype=np.float32)
    s_idx = np.arange(128)[:, None]
    t_idx = np.arange(128)[None, :]
    same = (s_idx // 64) == (t_idx // 64)
    maskf = (same & (s_idx <= t_idx)).astype(np.float32)
    maskb = (same & (s_idx >= t_idx)).astype(np.float32)
    mh0 = np.ascontiguousarray(np.broadcast_to(((np.arange(512) % 64) < 32).astype(np.float32)[None, :], (128, 512)))
    mh1 = np.ascontiguousarray(1.0 - mh0)
    rstm = np.ones((128, 1024), np.float32)
    rstm[:, ::64] = 0.0

    in_maps = []
    for core in range(NCORES):
        b = core % 4
        xin = np.concatenate([x_prompt[core * 4:(core + 1) * 4].reshape(1024, DM), x_sample[b]], axis=0)
        cv = np.stack([c_ctx.reshape(8, 128), c[b].reshape(8, 128)], axis=-1)
        cv = np.ascontiguousarray(cv.transpose(1, 0, 2).reshape(128, 16))
        in_maps.append({
            "xin": np.ascontiguousarray(xin), "cvec": cv, "w_mod": w_mod, "bmod": bmod_l, "n1g": n1g_l, "n2g": n2g_l,
            "fng": fng_l, "w_in": w_in_p, "w_out": w_out_p, "w1": w1, "w2": w2, "lbr": lbr_l, "hgn": hgn_l,
            "gqk": gqk_l,
            "cachek": np.ascontiguousarray(cache_k[b].reshape(NL, PAST, 128)),
            "cachev": np.ascontiguousarray(cache_v[b].reshape(NL, PAST, 128)),
            "state0": np.ascontiguousarray(state_hgrn[b]),
            "cosr": cosT, "sinr": sinT, "identd": ident, "maskf": maskf, "maskb": maskb, "rstd": rstm, "mh0d": mh0, "mh1d": mh1,
        })
    nc = _get_prog()
    res = run_bass_kernel_spmd(nc, in_maps, core_ids=list(range(NCORES)))
    R = res.results
    y_prompt = np.concatenate([R[cix]["yout"][:1024].reshape(4, SEQ, DM) for cix in range(NCORES)], axis=0)
    y_sample = np.stack([R[cix]["yout"][1024:] for cix in range(4)], axis=0)
    new_k = np.concatenate([R[cix]["newk"].reshape(4, NL, SEQ, 2, 64) for cix in range(NCORES)], axis=0)
    new_v = np.concatenate([R[cix]["newv"].reshape(4, NL, SEQ, 2, 64) for cix in range(NCORES)], axis=0)
    new_s = np.concatenate([R[cix]["news"] for cix in range(NCORES)], axis=0)
    return (y_prompt.astype(np.float32), y_sample.astype(np.float32), new_k.astype(np.float32),
            new_v.astype(np.float32), new_s.astype(np.float32))
```
